# Optimizing a Trainium2 kernel written in Bass

```python
import functools
import jax, jax.numpy as jnp
from jax import lax
import numpy as np

D_MODEL = 1024
BATCH = 2
SEQ = 8192
DEPTH = 2
DEC_BATCH = 32
DEC_SEQ = 4
PAST_LEN = 8192
PAGE_SIZE = 128

N_EVEN = (DEPTH + 1) // 2
N_ODD = DEPTH // 2
A_HEADS = 4
A_HEAD_DIM = D_MODEL // 8
A_WIDTH = A_HEADS * A_HEAD_DIM
B_HEADS = 4
B_HEAD_DIM = D_MODEL // 8
B_WIDTH = B_HEADS * B_HEAD_DIM
C_HEADS = 4
C_KEY_WIDTH = D_MODEL // 2
C_VAL_WIDTH = D_MODEL
C_KEY_DIM = C_KEY_WIDTH // C_HEADS
C_VAL_DIM = C_VAL_WIDTH // C_HEADS
C_GATE_RANK = 16
GLA_GATE_NORMALIZER = 16.0
CHUNK = 64
Q_BLOCK = 128
EVEN_IN = 4 * A_WIDTH + 4 * B_WIDTH + B_HEADS
ODD_IN = 2 * C_KEY_WIDTH + 2 * C_VAL_WIDTH + C_GATE_RANK
EPS = 1e-6

kernel_name = 'hgrn2_fox_gla_hybrid_step'


def _split(a, widths):
    idx = [int(i) for i in np.cumsum(widths)[:-1]]
    return jnp.split(a, idx, axis=-1)


def rmsnorm(x, g):
    xf = x.astype(jnp.float32)
    y = xf * lax.rsqrt(jnp.mean(xf * xf, axis=-1, keepdims=True) + EPS)
    return (y * g.astype(jnp.float32)).astype(x.dtype)


def head_rmsnorm(o, g):
    of = o.astype(jnp.float32)
    y = of * lax.rsqrt(jnp.mean(of * of, axis=-1, keepdims=True) + EPS)
    return y * g.reshape(o.shape[-2], o.shape[-1]).astype(jnp.float32)


def gated_linear_chunked(q, k, v, log_f, s0):
    bsz, L, H, dk = q.shape
    dv = v.shape[-1]
    c = min(CHUNK, L)
    pad = (-L) % c
    q, k, v, log_f = (a.astype(jnp.float32) for a in (q, k, v, log_f))
    if pad:
        pw = ((0, 0), (0, pad), (0, 0), (0, 0))
        q, k, v, log_f = (jnp.pad(a, pw) for a in (q, k, v, log_f))
    n = (L + pad) // c

    def to_chunks(a):
        return a.reshape(bsz, n, c, H, a.shape[-1]).swapaxes(0, 1)

    causal = jnp.tril(jnp.ones((c, c), dtype=bool))

    def step(S, inp):
        qc, kc, vc, gc = inp
        b = jnp.cumsum(gc, axis=1)
        o_inter = jnp.einsum('bthk,bhkv->bthv', qc * jnp.exp(b), S)
        diff = b[:, :, None] - b[:, None, :]
        decay = jnp.exp(jnp.where(causal[None, :, :, None, None], diff, -jnp.inf))
        scores = jnp.einsum('bthk,bshk,btshk->btsh', qc, kc, decay)
        o_intra = jnp.einsum('btsh,bshv->bthv', scores, vc)
        b_last = b[:, -1]
        S_new = S * jnp.exp(b_last)[..., None] + jnp.einsum(
            'bshk,bshv->bhkv', kc * jnp.exp(b_last[:, None] - b), vc)
        return S_new, o_inter + o_intra

    s_final, o = lax.scan(step, s0.astype(jnp.float32),
                          (to_chunks(q), to_chunks(k), to_chunks(v), to_chunks(log_f)))
    o = o.swapaxes(0, 1).reshape(bsz, n * c, H, dv)[:, :L]
    return o, s_final


def fox_prompt(q, k, v, log_f):
    bsz, L, H, dh = q.shape
    scale = dh ** -0.5
    c = jnp.cumsum(log_f, axis=1).swapaxes(1, 2)
    nb = L // Q_BLOCK
    qb = q.reshape(bsz, nb, Q_BLOCK, H, dh).swapaxes(0, 1)
    cb = c.reshape(bsz, H, nb, Q_BLOCK).transpose(2, 0, 1, 3)
    starts = jnp.arange(nb, dtype=jnp.int32) * Q_BLOCK
    key_pos = jnp.arange(L, dtype=jnp.int32)

    def block(args):
        qi, ci, s0 = args
        t = s0 + jnp.arange(Q_BLOCK, dtype=jnp.int32)
        logits = jnp.einsum('bthd,bshd->bhts', qi, k).astype(jnp.float32) * scale \
            + ci[..., None] - c[:, :, None, :]
        logits = jnp.where(t[:, None] >= key_pos[None, :], logits, -jnp.inf)
        p = jax.nn.softmax(logits, axis=-1).astype(v.dtype)
        return jnp.einsum('bhts,bshd->bthd', p, v)

    out = lax.map(block, (qb, cb, starts))
    return out.swapaxes(0, 1).reshape(bsz, L, H, dh)


def fox_sample(q, k, v, log_f, k_pages, v_pages, logf_pages, page_table):
    db, T, H, dh = q.shape
    scale = dh ** -0.5
    k_past = k_pages[page_table].reshape(db, -1, H, dh)
    v_past = v_pages[page_table].reshape(db, -1, H, dh)
    lf_past = logf_pages[page_table].reshape(db, -1, H).astype(jnp.float32)
    P = k_past.shape[1]
    cum_new = jnp.cumsum(log_f, axis=1).swapaxes(1, 2)
    suffix_past = (lax.cumsum(lf_past, axis=1, reverse=True) - lf_past).swapaxes(1, 2)
    logit_past = jnp.einsum('bthd,bshd->bhts', q, k_past).astype(jnp.float32) * scale \
        + cum_new[..., None] + suffix_past[:, :, None, :]
    logit_new = jnp.einsum('bthd,bshd->bhts', q, k).astype(jnp.float32) * scale \
        + cum_new[..., None] - cum_new[:, :, None, :]
    causal = jnp.tril(jnp.ones((T, T), dtype=bool))
    logit_new = jnp.where(causal, logit_new, -jnp.inf)
    p = jax.nn.softmax(jnp.concatenate([logit_past, logit_new], axis=-1), axis=-1).astype(v.dtype)
    return jnp.einsum('bhts,bshd->bthd', p[..., :P], v_past) + jnp.einsum('bhts,bshd->bthd', p[..., P:], v)


def even_mixer(h, w_in, b_fox, lb, hgrn_gain, w_out, hgrn_s0, fox_attend):
    bsz, L, _ = h.shape
    proj = h @ w_in
    qa, fa, ia, ga, qb, kb, vb, gb, fb = _split(proj, [A_WIDTH] * 4 + [B_WIDTH] * 4 + [B_HEADS])
    f = lb + (1.0 - lb) * jax.nn.sigmoid(fa.astype(jnp.float32))
    ha = lambda a: a.reshape(bsz, L, A_HEADS, A_HEAD_DIM)
    o_a, s_a = gated_linear_chunked(ha(qa), ha(1.0 - f), ha(ia), ha(jnp.log(f)), hgrn_s0)
    o_a = head_rmsnorm(o_a, hgrn_gain).reshape(bsz, L, A_WIDTH).astype(h.dtype) * jax.nn.silu(ga)
    hb = lambda a: a.reshape(bsz, L, B_HEADS, B_HEAD_DIM)
    log_fb = jax.nn.log_sigmoid((fb + b_fox).astype(jnp.float32))
    kb_h, vb_h = hb(kb), hb(vb)
    o_b = fox_attend(hb(qb), kb_h, vb_h, log_fb).reshape(bsz, L, B_WIDTH).astype(h.dtype) * jax.nn.silu(gb)
    out = jnp.concatenate([o_a, o_b], axis=-1) @ w_out
    return out, s_a, kb_h, vb_h, log_fb


def odd_mixer(h, w_in, w_gate, b_gate, gla_gain, w_out, gla_s0):
    bsz, L, _ = h.shape
    proj = h @ w_in
    q, k, v, g, r = _split(proj, [C_KEY_WIDTH, C_KEY_WIDTH, C_VAL_WIDTH, C_VAL_WIDTH, C_GATE_RANK])
    log_f = jax.nn.log_sigmoid((r @ w_gate + b_gate).astype(jnp.float32)) / GLA_GATE_NORMALIZER
    hk = lambda a: a.reshape(bsz, L, C_HEADS, C_KEY_DIM)
    hv = lambda a: a.reshape(bsz, L, C_HEADS, C_VAL_DIM)
    o, s = gated_linear_chunked(hk(q) * C_KEY_DIM ** -0.5, hk(k), hv(v), hk(log_f), gla_s0)
    o = head_rmsnorm(o, gla_gain).reshape(bsz, L, C_VAL_WIDTH).astype(h.dtype) * jax.nn.silu(g)
    return o @ w_out, s


def run_trunk(x, fox_attend, hgrn_s0, gla_s0, norm_even, w_in_even, b_fox_f, lb_logits, hgrn_gain,
              w_out_even, norm_odd, w_in_odd, w_gla_gate, b_gla_gate, gla_gain, w_out_odd, final_norm):
    lb_all = jnp.cumsum(jax.nn.softmax(lb_logits.astype(jnp.float32), axis=0), axis=0)
    ks, vs, lfs, hs, gs = [], [], [], [], []
    for l in range(DEPTH):
        i = l // 2
        if l % 2 == 0:
            out, s_a, kb, vb, lfb = even_mixer(rmsnorm(x, norm_even[i]), w_in_even[i], b_fox_f[i], lb_all[i],
                                               hgrn_gain[i], w_out_even[i], hgrn_s0[i],
                                               functools.partial(fox_attend, i))
            ks.append(kb); vs.append(vb); lfs.append(lfb); hs.append(s_a)
        else:
            out, s_c = odd_mixer(rmsnorm(x, norm_odd[i]), w_in_odd[i], w_gla_gate[i], b_gla_gate[i],
                                 gla_gain[i], w_out_odd[i], gla_s0[i])
            gs.append(s_c)
        x = x + out
    return rmsnorm(x, final_norm), jnp.stack(ks), jnp.stack(vs), jnp.stack(lfs), jnp.stack(hs), jnp.stack(gs)


def setup_inputs(seed: int = 0) -> dict:
    key = jax.random.key(seed)
    ks = jax.random.split(key, 24)
    n_pages = PAST_LEN // PAGE_SIZE
    n_used = DEC_BATCH * n_pages
    n_pool = n_used + (n_used + 3) // 4
    nrm = jax.random.normal
    page_table = jax.random.permutation(ks[0], n_pool)[:n_used].reshape(DEC_BATCH, n_pages).astype(jnp.int32)
    return {
        'x_prompt': nrm(ks[1], (BATCH, SEQ, D_MODEL), jnp.float32),
        'x_sample': nrm(ks[2], (DEC_BATCH, DEC_SEQ, D_MODEL), jnp.float32),
        'cache_fox_k': nrm(ks[3], (N_EVEN, n_pool, PAGE_SIZE, B_HEADS, B_HEAD_DIM), jnp.float32),
        'cache_fox_v': nrm(ks[4], (N_EVEN, n_pool, PAGE_SIZE, B_HEADS, B_HEAD_DIM), jnp.float32),
        'cache_fox_logf': jax.nn.log_sigmoid(8.0 + 0.5 * nrm(ks[5], (N_EVEN, n_pool, PAGE_SIZE, B_HEADS), jnp.float32)),
        'state_hgrn': 0.5 * nrm(ks[6], (N_EVEN, DEC_BATCH, A_HEADS, A_HEAD_DIM, A_HEAD_DIM), jnp.float32),
        'state_gla': nrm(ks[7], (N_ODD, DEC_BATCH, C_HEADS, C_KEY_DIM, C_VAL_DIM), jnp.float32),
        'page_table': page_table,
        'norm_even': 1.0 + 0.1 * nrm(ks[8], (N_EVEN, D_MODEL), jnp.float32),
        'w_in_even': nrm(ks[9], (N_EVEN, D_MODEL, EVEN_IN), jnp.float32) * D_MODEL ** -0.5,
        'b_fox_f': 4.0 + 0.5 * nrm(ks[10], (N_EVEN, B_HEADS), jnp.float32),
        'lb_logits': 0.5 * nrm(ks[11], (N_EVEN + 1, A_WIDTH), jnp.float32),
        'hgrn_gain': 1.0 + 0.1 * nrm(ks[12], (N_EVEN, A_WIDTH), jnp.float32),
        'w_out_even': nrm(ks[13], (N_EVEN, A_WIDTH + B_WIDTH, D_MODEL), jnp.float32) * (A_WIDTH + B_WIDTH) ** -0.5,
        'norm_odd': 1.0 + 0.1 * nrm(ks[14], (N_ODD, D_MODEL), jnp.float32),
        'w_in_odd': nrm(ks[15], (N_ODD, D_MODEL, ODD_IN), jnp.float32) * D_MODEL ** -0.5,
        'w_gla_gate': nrm(ks[16], (N_ODD, C_GATE_RANK, C_KEY_WIDTH), jnp.float32) * C_GATE_RANK ** -0.5,
        'b_gla_gate': 0.1 * nrm(ks[17], (N_ODD, C_KEY_WIDTH), jnp.float32),
        'gla_gain': 1.0 + 0.1 * nrm(ks[18], (N_ODD, C_VAL_WIDTH), jnp.float32),
        'w_out_odd': nrm(ks[19], (N_ODD, C_VAL_WIDTH, D_MODEL), jnp.float32) * C_VAL_WIDTH ** -0.5,
        'final_norm': 1.0 + 0.1 * nrm(ks[20], (D_MODEL,), jnp.float32),
    }


def reference(x_prompt, x_sample, cache_fox_k, cache_fox_v, cache_fox_logf, state_hgrn, state_gla, page_table,
              norm_even, w_in_even, b_fox_f, lb_logits, hgrn_gain, w_out_even, norm_odd, w_in_odd,
              w_gla_gate, b_gla_gate, gla_gain, w_out_odd, final_norm):
    weights = (norm_even, w_in_even, b_fox_f, lb_logits, hgrn_gain, w_out_even, norm_odd, w_in_odd,
               w_gla_gate, b_gla_gate, gla_gain, w_out_odd, final_norm)
    bsz, L, _ = x_prompt.shape
    hgrn0 = jnp.zeros((N_EVEN, bsz, A_HEADS, A_HEAD_DIM, A_HEAD_DIM), jnp.float32)
    gla0 = jnp.zeros((N_ODD, bsz, C_HEADS, C_KEY_DIM, C_VAL_DIM), jnp.float32)
    fox_p = lambda i, q, k, v, lf: fox_prompt(q, k, v, lf)
    y_prompt, kp, vp, lfp, hgrn_p, gla_p = run_trunk(x_prompt, fox_p, hgrn0, gla0, *weights)
    n_pp = L // PAGE_SIZE
    fox_k_prompt = kp.reshape(N_EVEN, bsz, n_pp, PAGE_SIZE, B_HEADS, B_HEAD_DIM)
    fox_v_prompt = vp.reshape(N_EVEN, bsz, n_pp, PAGE_SIZE, B_HEADS, B_HEAD_DIM)
    fox_logf_prompt = lfp.reshape(N_EVEN, bsz, n_pp, PAGE_SIZE, B_HEADS)
    fox_s = lambda i, q, k, v, lf: fox_sample(q, k, v, lf, cache_fox_k[i], cache_fox_v[i], cache_fox_logf[i], page_table)
    y_sample, fox_k_sample, fox_v_sample, fox_logf_sample, hgrn_s, gla_s = run_trunk(
        x_sample, fox_s, state_hgrn, state_gla, *weights)
    return (y_prompt, y_sample, fox_k_prompt, fox_v_prompt, fox_logf_prompt, hgrn_p, gla_p,
            fox_k_sample, fox_v_sample, fox_logf_sample, hgrn_s, gla_s)
```

```python
import numpy as np
from concourse.bass_utils import run_bass_kernel_spmd
from contextlib import ExitStack
import concourse.bass as bass
import concourse.mybir as mybir

F32 = mybir.dt.float32
BF16 = mybir.dt.bfloat16
I32 = mybir.dt.int32
U32 = mybir.dt.uint32
AF = mybir.ActivationFunctionType
ALU = mybir.AluOpType
AX = mybir.AxisListType


class Buf:
    __slots__ = ("w", "r", "name", "excl")

    def __init__(self, name=""):
        self.excl = False
        self.w = None
        self.r = {}
        self.name = name


class Eng:
    def __init__(self, name):
        self.name = name
        self.prog = []
        self.count = 0
        self.waited = {}


NDS = 12


class Rec:
    COMPUTE = ("pe", "act", "dve", "pool")

    def __init__(self, nc, stack):
        self.nc = nc
        self.e = {n: Eng(n) for n in ("pe", "act", "dve", "pool", "sp")}
        self.sems = {}
        for n in self.COMPUTE:
            self.sems[n] = stack.enter_context(nc.semaphore("s_" + n))
        self.dq = {}
        for q in ("sp", "pool", "act"):
            sl = []
            for i in range(NDS):
                key = "d_%s_%d" % (q, i)
                self.sems[key] = stack.enter_context(nc.semaphore(key))
                sl.append(key)
            self.dq[q] = dict(slots=sl, uses=[0] * NDS, n=0)
        self.final = []

    def _need(self, eng, deps):
        for k, v in deps.items():
            if eng.waited.get(k, 0) < v:
                eng.waited[k] = v
                eng.prog.append(("wait", k, v))

    def _collect(self, ename, R, W):
        deps = {}

        def add(d, kind):
            if d is None:
                return
            k, v, en = d
            if en == ename and ename in self.COMPUTE:
                if ename == "pe":
                    return
                if kind == "war":
                    return
            if deps.get(k, 0) < v:
                deps[k] = v

        for b in R:
            add(b.w, "raw")
        for b in W:
            add(b.w, "waw")
            for k, (v, en) in b.r.items():
                add((k, v, en), "war")
        return deps

    def _mark(self, tok, R, W):
        k, v, en = tok
        for b in R:
            b.r[k] = (v, en)
        for b in W:
            b.w = tok
            b.r = {}

    def op(self, ename, fn, R=(), W=()):
        W = list(W) + [b for b in R if b.excl]
        R = [b for b in R if not b.excl]
        eng = self.e[ename]
        deps = self._collect(ename, R, W)
        self._need(eng, deps)
        eng.count += 1
        eng.prog.append(("op", fn, ename, 1))
        self._mark((ename, eng.count, ename), R, W)

    def dma(self, q, fn, R=(), W=(), final=False):
        eng = self.e[q]
        dq = self.dq[q]
        slot = dq["n"] % NDS
        dq["n"] += 1
        key = dq["slots"][slot]
        if dq["uses"][slot] > 0:
            self._need(eng, {key: 16 * dq["uses"][slot]})
        dq["uses"][slot] += 1
        val = 16 * dq["uses"][slot]
        deps = self._collect("dma_" + q, R, W)
        self._need(eng, deps)
        eng.prog.append(("op", fn, key, 16))
        self._mark((key, val, "dma_" + q), R, W)
        if final:
            self.final.append((key, val))

    def finish(self):
        eng = self.e["sp"]
        last = {}
        for k, v in self.final:
            last[k] = max(last.get(k, 0), v)
        for k, v in last.items():
            eng.prog.append(("wait", k, v))
        for q in ("pool", "act"):
            dq = self.dq[q]
            for i, u in enumerate(dq["uses"]):
                if u:
                    self.e[q].prog.append(("wait", dq["slots"][i], 16 * u))

    def replay(self, block):
        nc = self.nc
        sems = self.sems

        def run(engobj, prog):
            for it in prog:
                if it[0] == "wait":
                    engobj.wait_ge(sems[it[1]], it[2])
                else:
                    _, fn, key, inc = it
                    ins = fn(engobj)
                    ins.then_inc(sems[key], inc)

        @block.sync
        def _(e):
            run(e, self.e["sp"].prog)

        @block.tensor
        def _(e):
            run(e, self.e["pe"].prog)

        @block.scalar
        def _(e):
            run(e, self.e["act"].prog)

        @block.vector
        def _(e):
            run(e, self.e["dve"].prog)

        @block.gpsimd
        def _(e):
            run(e, self.e["pool"].prog)


import os
KSTOP = int(os.environ.get('KSTOP', '99'))
KSUB = int(os.environ.get('KSUB', '99'))

D = 1024
KC = 8
EPS = 1e-6
SCALE = 128 ** -0.5


class K:
    def __init__(self, L, NPG, POOL, dbg=False):
        self.L, self.NPG, self.POOL, self.dbg = L, NPG, POOL, dbg
        self.nc = bass.Bass("TRN2", target_bir_lowering=False)
        self.B = {}

    def buf(self, name):
        if name not in self.B:
            self.B[name] = Buf(name)
        return self.B[name]

    def din(self, name, shape, dt=F32):
        return self.nc.dram_tensor(name, list(shape), dt, kind="ExternalInput").ap()

    def dout(self, name, shape, dt=F32):
        return self.nc.dram_tensor(name, list(shape), dt, kind="ExternalOutput").ap()

    def dscr(self, name, shape, dt):
        return self.nc.dram_tensor(name, list(shape), dt, kind="Internal").ap()


def barrier(rec):
    tgt = {}
    for n in Rec.COMPUTE:
        if rec.e[n].count:
            tgt[n] = rec.e[n].count
    for q, dq in rec.dq.items():
        for i, u in enumerate(dq["uses"]):
            if u:
                tgt[dq["slots"][i]] = 16 * u
    for n in ("pe", "act", "dve", "pool", "sp"):
        d = {k: v for k, v in tgt.items() if k != n}
        rec._need(rec.e[n], d)


def flush(k, rec):
    barrier(rec)
    with k.nc.Block() as block:
        rec.replay(block)
    for e in rec.e.values():
        e.prog = []


def build(L, NPG, POOL, dbg=False, phases=(1, 2, 3)):
    k = K(L, NPG, POOL, dbg)
    nc = k.nc
    NG = L // 512
    NT = L // 128
    buf = k.buf
    xp = k.din("xp", [L, D])
    xs = k.din("xs", [256, D])
    w_in_e = k.din("w_in_even", [D, 4100])
    w_out_e = k.din("w_out_even", [D, D])
    w_in_o = k.din("w_in_odd", [D, 3088])
    w_out_o = k.din("w_out_odd", [D, D])
    w_gate = k.din("w_gla_gate", [16, 512])
    fnorm = k.din("fnorm", [D])
    vecs = k.din("vecs", [128, 64])
    bfox = k.din("bfox", [128, 4])
    consts = k.din("consts", [128, 1280])
    st_h = k.din("st_hgrn", [4, 4, 128, 128])
    st_g = k.din("st_gla", [4, 4, 128, 256])
    ptab = k.din("ptab", [4, NPG], I32)
    ck = k.din("cache_k", [POOL * 128, 512])
    cv = k.din("cache_v", [POOL * 128, 512])
    clf = k.din("cache_lf", [POOL, 512])

    y_p = k.dout("y_p", [L, D])
    y_s = k.dout("y_s", [256, D])
    fk_p = k.dout("fk_p", [L, 512])
    fv_p = k.dout("fv_p", [L, 512])
    flf_p = k.dout("flf_p", [L, 4])
    hg_p = k.dout("hg_p", [4, 128, 128])
    gl_p = k.dout("gl_p", [4, 128, 256])
    fk_s = k.dout("fk_s", [256, 512])
    fv_s = k.dout("fv_s", [256, 512])
    flf_s = k.dout("flf_s", [256, 4])
    hg_s = k.dout("hg_s", [4, 4, 128, 128])
    gl_s = k.dout("gl_s", [4, 4, 128, 256])

    LT = L + 256
    qbT = k.dscr("qbT", [4, 128, LT], BF16)
    kbT = k.dscr("kbT", [4, 128, LT], BF16)
    sgbT = k.dscr("sgbT", [4, 128, LT], BF16)
    vbs = k.dscr("vbs", [LT, 512], BF16)
    negc = k.dscr("negc", [LT, 4], F32)
    oaT = k.dscr("oaT", [4, 128, LT], BF16)
    obT = k.dscr("obT", [4, 128, LT], BF16)

    with ExitStack() as st:
        rec = Rec(nc, st)
        sb = lambda n, s, d: st.enter_context(nc.sbuf_tensor(n, s, d))
        ps = lambda n, s, d: st.enter_context(nc.psum_tensor(n, s, d))
        pA = [ps("pA%d" % i, [128, 512], F32) for i in range(2)]
        pT = ps("pT", [128, 1024], BF16)
        pS = ps("pS", [128, 512], F32)
        pP = ps("pP", [128, 512], F32)
        pO = ps("pO", [128, 512], F32)
        pN = ps("pN", [128, 512], F32)
        pX = ps("pX", [128, 512], F32)
        for nm in ("pA0", "pA1", "pT", "pS", "pP", "pO", "pN", "pX"):
            buf(nm).excl = True
        cst = sb("cst", [128, 1280], F32)
        vec = sb("vec", [128, 64], F32)
        bfx = sb("bfx", [128, 4], F32)
        idb = sb("idb", [128, 128], BF16)
        onesb = sb("onesb", [128, 128], BF16)
        m64 = sb("m64", [128, 64], F32)
        lbt = sb("lbt", [128, 4], F32)
        omlt = sb("omlt", [128, 4], F32)
        nomlt = sb("nomlt", [128, 4], F32)
        rec.dma("sp", lambda e: e.dma_start(out=cst[:], in_=consts), W=[buf("cst")])
        rec.dma("sp", lambda e: e.dma_start(out=vec[:], in_=vecs), W=[buf("vec")])
        rec.dma("sp", lambda e: e.dma_start(out=bfx[:], in_=bfox), W=[buf("bfx")])
        ident = cst[:, 0:128]
        tri = cst[:, 128:256]
        ones = cst[:, 320:448]
        bt32 = cst[:, 448:576]
        rst64 = cst[:, 576:1088]
        rst32 = cst[:, 1088:1216]
        rec.op("dve", lambda e: e.tensor_copy(out=idb[:], in_=ident), R=[buf("cst")], W=[buf("idb")])
        rec.op("dve", lambda e: e.tensor_copy(out=onesb[:], in_=ones), R=[buf("cst")], W=[buf("onesb")])
        rec.op("dve", lambda e: e.tensor_copy(out=m64[:], in_=cst[:, 256:320]), R=[buf("cst")], W=[buf("m64")])
        rec.op("dve", lambda e: e.tensor_sub(out=lbt[:], in0=vec[:, 24:28], in1=vec[:, 28:32]), R=[buf("vec")], W=[buf("lbt")])
        rec.op("act", lambda e: e.activation(out=lbt[:], in_=lbt[:], func=AF.Sigmoid), R=[buf("lbt")], W=[buf("lbt")])
        rec.op("dve", lambda e: e.tensor_scalar(out=omlt[:], in0=lbt[:], scalar1=-1.0, scalar2=1.0, op0=ALU.mult, op1=ALU.add),
               R=[buf("lbt")], W=[buf("omlt")])
        rec.op("dve", lambda e: e.tensor_scalar(out=nomlt[:], in0=omlt[:], scalar1=-1.0, scalar2=None, op0=ALU.mult),
               R=[buf("omlt")], W=[buf("nomlt")])

        G = dict(k=k, rec=rec, nc=nc, st=st, pA=pA, pT=pT, pS=pS, pP=pP, pO=pO, pN=pN, pX=pX, cst=cst, vec=vec, bfx=bfx,
                 idb=idb, onesb=onesb, m64=m64, lbt=lbt, omlt=omlt, nomlt=nomlt, ident=ident, tri=tri, ones=ones, bt32=bt32,
                 rst64=rst64)
        G.update(locals())
        if 1 in phases:
            phase1(G)
        if 2 in phases:
            phase2(G)
        if 3 in phases:
            phase3(G)
        if dbg:
            dbg_ob = k.dout("dbg_obT", [4, 128, LT], BF16)
            if 2 in phases:
                rec.dma("sp", lambda e: e.dma_start(out=dbg_ob, in_=obT), R=[buf("obT_d")], final=True)
            dbg_oa = k.dout("dbg_oaT", [4, 128, LT], BF16)
            rec.dma("sp", lambda e: e.dma_start(out=dbg_oa, in_=oaT), R=[buf("oaT_d")], final=True)
            dbg_nc = k.dout("dbg_negc", [LT, 4], F32)
            rec.dma("sp", lambda e: e.dma_start(out=dbg_nc, in_=negc), R=[buf("negc_d")], final=True)
        rec.finish()
        flush(k, rec)
    return nc


def load_weight_bf16(G, ph, wdram, ncols, gaincol, name):
    k, rec, nc, vec = G["k"], G["rec"], G["nc"], G["vec"]
    buf = k.buf
    W = ph.enter_context(nc.sbuf_tensor(name, [128, KC, ncols], BF16))
    with ExitStack() as tmp:
        stg = [tmp.enter_context(nc.sbuf_tensor(name + "_stg%d" % i, [128, ncols], F32)) for i in range(2)]
        wv = wdram.rearrange("(kc p) n -> p kc n", p=128)
        for kc in range(KC):
            s = stg[kc % 2]
            sbuf = buf(name + "_stg%d" % (kc % 2))
            rec.dma("sp", lambda e, s=s, kc=kc: e.dma_start(out=s[:], in_=wv[:, kc, :]), W=[sbuf])
            if gaincol is None:
                rec.op("pool", lambda e, s=s, kc=kc: e.tensor_copy(out=W[:, kc, :], in_=s[:]), R=[sbuf], W=[buf(name)])
            else:
                rec.op("pool", lambda e, s=s, kc=kc: e.tensor_scalar(out=W[:, kc, :], in0=s[:], scalar1=vec[:, gaincol + kc:gaincol + kc + 1],
                                                                   scalar2=None, op0=ALU.mult), R=[sbuf, buf("vec")], W=[buf(name)])
        flush(k, rec)
    return W


def norm_and_transpose(G, xt, xtb, rstd_col, hb, hbb, hT, hTb, tcol, gain_in_w=True):
    rec, pT, idb = G["rec"], G["pT"], G["idb"]
    buf = G["k"].buf
    junk, ss = G["junk"], G["ss"]
    rec.op("act", lambda e: e.activation(out=junk[:], in_=xt[:], func=AF.Square, accum_out=ss[:, rstd_col:rstd_col + 1]),
           R=[xtb], W=[buf("junk"), buf("ss")])
    rec.op("act", lambda e: e.activation(out=ss[:, rstd_col:rstd_col + 1], in_=ss[:, rstd_col:rstd_col + 1], func=AF.Ln, scale=1.0 / D, bias=G["epsb"][:, 0:1]),
           R=[buf("ss"), buf("epsb")], W=[buf("ss")])
    rec.op("act", lambda e: e.activation(out=ss[:, rstd_col:rstd_col + 1], in_=ss[:, rstd_col:rstd_col + 1], func=AF.Exp, scale=-0.5),
           R=[buf("ss")], W=[buf("ss")])
    rec.op("dve", lambda e: e.tensor_scalar(out=hb[:], in0=xt[:], scalar1=ss[:, rstd_col:rstd_col + 1], scalar2=None, op0=ALU.mult),
           R=[xtb, buf("ss")], W=[hbb])
    for kc in range(KC):
        rec.op("pe", lambda e, kc=kc: e.transpose(out=pT[:, kc * 128:(kc + 1) * 128], in_=hb[:, kc * 128:(kc + 1) * 128], identity=idb[:]),
               R=[hbb, buf("idb")], W=[buf("pT")])
    rec.op("act", lambda e: e.activation(out=hT[:, :, tcol:tcol + 128], in_=pT[:].rearrange("p (kc t) -> p kc t", kc=KC), func=AF.Copy),
           R=[buf("pT")], W=[hTb])


def phase1(G):
    k, rec, nc = G["k"], G["rec"], G["nc"]
    buf = k.buf
    L = k.L
    NG = L // 512
    pA, pT, pS, pP, pO, pN, pX = G["pA"], G["pT"], G["pS"], G["pP"], G["pO"], G["pN"], G["pX"]
    vec, lbt, omlt, nomlt, m64, onesb, idb = G["vec"], G["lbt"], G["omlt"], G["nomlt"], G["m64"], G["onesb"], G["idb"]
    with ExitStack() as ph:
        sb = lambda n, s, d: ph.enter_context(nc.sbuf_tensor(n, s, d))
        W0 = load_weight_bf16(G, ph, G["w_in_e"], 4100, 0, "W0")
        G["junk"] = sb("junk", [128, 1024], BF16)
        G["ss"] = sb("ss", [128, 8], F32)
        G["epsb"] = sb("epsb", [128, 1], F32)
        rec.op("pool", lambda e: e.memset(G["epsb"][:], EPS), W=[buf("epsb")])
        xt = [sb("xt%d" % i, [128, D], F32) for i in range(3)]
        hb = [sb("hb%d" % i, [128, D], BF16) for i in range(2)]
        hT = sb("hT", [128, KC, 512], BF16)
        tmp = {n: sb("t_" + n, [128, 512], F32) for n in ("sig", "f", "omf", "lf", "b", "d", "eq", "ek")}
        qt = sb("qt", [128, 4, 512], BF16)
        kt = sb("kt", [128, 4, 512], BF16)
        sm = sb("sm", [128, 4, 3, 8], F32)
        vtok = sb("vtok", [128, 4, 512], BF16)
        ktok = sb("ktok", [128, 4, 4, 128], BF16)
        AT = [sb("AT%d" % i, [128, 4, 64], BF16) for i in range(2)]
        S = sb("S", [128, 4, 128], F32)
        Sbf = [sb("Sbf%d" % i, [128, 4, 128], BF16) for i in range(2)]
        oT = sb("oT", [128, 4, 512], F32)
        sq = sb("sq", [128, 512], BF16)
        lnt = sb("lnt", [128, 512], F32)
        sg = sb("sg", [128, 4, 512], BF16)
        tg = sb("tg", [128, 512], F32)
        oa = sb("oa", [128, 4, 512], BF16)
        qb = sb("qb", [128, 4, 512], BF16)
        kb = sb("kb", [128, 4, 512], BF16)
        sgb = sb("sgb", [128, 4, 512], BF16)
        ktm = [sb("ktm%d" % i, [128, 512], F32) for i in range(2)]
        vtm = [sb("vtm%d" % i, [128, 512], F32) for i in range(2)]
        vbf = sb("vbf", [128, 4, 512], BF16)
        lfb = sb("lfb", [128, 4, 4], F32)
        ncb = sb("ncb", [128, 4, 4], F32)
        tot = sb("tot", [128, 4], F32)
        rec.op("pool", lambda e: e.memset(tot[:], 0.0), W=[buf("tot")])
        rec.op("pool", lambda e: e.memset(S[:], 0.0), W=[buf("S%d" % h) for h in range(4)])

        pa_i = [0]

        def fm(col, N, hTN):
            p = pA[pa_i[0] % 2]
            pb = buf("pA%d" % (pa_i[0] % 2))
            pa_i[0] += 1
            for kc in range(KC):
                rec.op("pe", lambda e, kc=kc, p=p: e.matmul(out=p[:, 0:N], lhsT=W0[:, kc, col:col + 128], rhs=hTN[:, kc, 0:N],
                                                           start=(kc == 0), stop=(kc == KC - 1)),
                       R=[buf("W0"), buf("hT")], W=[pb])
            return p, pb

        def tm(col, ncols, t):
            p = pA[pa_i[0] % 2]
            pb = buf("pA%d" % (pa_i[0] % 2))
            pa_i[0] += 1
            for kc in range(KC):
                rec.op("pe", lambda e, kc=kc, p=p: e.matmul(out=p[:, 0:ncols], lhsT=hT[:, kc, t * 128:(t + 1) * 128], rhs=W0[:, kc, col:col + ncols],
                                                           start=(kc == 0), stop=(kc == KC - 1)),
                       R=[buf("W0"), buf("hT")], W=[pb])
            return p, pb

        groups = [(g * 512, 512, False) for g in range(NG)] + [(L, 256, True)]
        xi = [0]
        def do_group(t0, N, samp):
            ntile = N // 128
            C = 64
            nch = N // C
            mid = 1 if samp else 31
            last = 3 if samp else C - 1
            for t in range(ntile):
                x_ = xt[xi[0] % 3]
                xb_ = buf("xt%d" % (xi[0] % 3))
                h_ = hb[xi[0] % 2]
                hb_ = buf("hb%d" % (xi[0] % 2))
                xi[0] += 1
                src = G["xs"][t * 128:(t + 1) * 128, :] if samp else G["xp"][t0 + t * 128:t0 + (t + 1) * 128, :]
                rec.dma("sp", lambda e, x_=x_, src=src: e.dma_start(out=x_[:], in_=src), W=[xb_])
                norm_and_transpose(G, x_, xb_, t, h_, hb_, hT, buf("hT"), t * 128)
            if KSTOP < 2:
                return
            for t in range(ntile):
                r0 = t0 + t * 128
                p, pb = tm(1024, 512, t)
                rec.op("act", lambda e, p=p, t=t: e.activation(out=vtok[:, t, :], in_=p[:, 0:512], func=AF.Copy), R=[pb], W=[buf("vtok")])
                p, pb = tm(2560, 512, t)
                kk = ktm[t % 2]
                kkb = buf("ktm%d" % (t % 2))
                rec.op("act", lambda e, p=p, kk=kk: e.activation(out=kk[:], in_=p[:, 0:512], func=AF.Copy), R=[pb], W=[kkb])
                dst = G["fk_s"][t * 128:(t + 1) * 128, :] if samp else G["fk_p"][r0:r0 + 128, :]
                rec.dma("pool", lambda e, kk=kk, dst=dst: e.dma_start(out=dst, in_=kk[:]), R=[kkb], final=True)
                p, pb = tm(3072, 512, t)
                vv = vtm[t % 2]
                vvb = buf("vtm%d" % (t % 2))
                rec.op("act", lambda e, p=p, vv=vv: e.activation(out=vv[:], in_=p[:, 0:512], func=AF.Copy), R=[pb], W=[vvb])
                rec.op("dve", lambda e, p=p, t=t: e.tensor_copy(out=vbf[:, t, :], in_=p[:, 0:512]), R=[pb], W=[buf("vbf")])
                dst = G["fv_s"][t * 128:(t + 1) * 128, :] if samp else G["fv_p"][r0:r0 + 128, :]
                rec.dma("pool", lambda e, vv=vv, dst=dst: e.dma_start(out=dst, in_=vv[:]), R=[vvb], final=True)
                if KSUB < 1:
                    continue
                p, pb = tm(4096, 4, t)
                rec.op("dve", lambda e, p=p, t=t: e.tensor_tensor(out=lfb[:, t, :], in0=p[:, 0:4], in1=G["bfx"][:], op=ALU.add),
                       R=[pb, buf("bfx")], W=[buf("lfb")])
                rec.op("act", lambda e, t=t: e.activation(out=lfb[:, t, :], in_=lfb[:, t, :], func=AF.Sigmoid), R=[buf("lfb")], W=[buf("lfb")])
                rec.op("act", lambda e, t=t: e.activation(out=lfb[:, t, :], in_=lfb[:, t, :], func=AF.Ln), R=[buf("lfb")], W=[buf("lfb")])
                if KSUB < 2:
                    continue
                trim = G["bt32"] if samp else G["tri"]
                rec.op("pe", lambda e, t=t, trim=trim: e.matmul(out=pX[:, 0:4], lhsT=trim, rhs=lfb[:, t, :], start=True, stop=True),
                       R=[buf("cst"), buf("lfb")], W=[buf("pX")])
                if samp:
                    rec.op("dve", lambda e, t=t: e.tensor_scalar(out=ncb[:, t, :], in0=pX[:, 0:4], scalar1=-1.0, scalar2=None, op0=ALU.mult),
                           R=[buf("pX")], W=[buf("ncb")])
                else:
                    rec.op("dve", lambda e, t=t: e.scalar_tensor_tensor(out=ncb[:, t, :], in0=pX[:, 0:4], scalar=-1.0, in1=tot[:], op0=ALU.mult, op1=ALU.subtract),
                           R=[buf("pX"), buf("tot")], W=[buf("ncb")])
                    rec.op("pe", lambda e, t=t: e.matmul(out=pX[:, 8:12], lhsT=G["ones"], rhs=lfb[:, t, :], start=True, stop=True),
                           R=[buf("cst"), buf("lfb")], W=[buf("pX")])
                    rec.op("dve", lambda e: e.tensor_tensor(out=tot[:], in0=tot[:], in1=pX[:, 8:12], op=ALU.add),
                           R=[buf("pX"), buf("tot")], W=[buf("tot")])
            if KSUB < 3:
                return
            dst = (G["flf_s"] if samp else G["flf_p"][t0:t0 + N, :]).rearrange("(t p) h -> p t h", p=128)
            rec.dma("pool", lambda e, dst=dst: e.dma_start(out=dst, in_=lfb[:, 0:ntile, :]), R=[buf("lfb")], final=True)
            rec.dma("pool", lambda e: e.dma_start(out=G["negc"][t0:t0 + N, :].rearrange("(t p) h -> p t h", p=128), in_=ncb[:, 0:ntile, :]),
                    R=[buf("ncb")], W=[buf("negc_d")])
            rec.dma("pool", lambda e: e.dma_start(out=G["vbs"][t0:t0 + N, :].rearrange("(t p) c -> p t c", p=128), in_=vbf[:, 0:ntile, :]),
                    R=[buf("vbf")], W=[buf("vbs_d")])
            if KSTOP < 3:
                return
            for h in range(4):
                p, pb = fm(2048 + 128 * h, N, hT)
                rec.op("act", lambda e, p=p, h=h: e.activation(out=qb[:, h, 0:N], in_=p[:, 0:N], func=AF.Copy), R=[pb], W=[buf("qb")])
                p, pb = fm(2560 + 128 * h, N, hT)
                rec.op("dve", lambda e, p=p, h=h: e.tensor_copy(out=kb[:, h, 0:N], in_=p[:, 0:N]), R=[pb], W=[buf("kb")])
                p, pb = fm(3584 + 128 * h, N, hT)
                rec.op("act", lambda e, p=p, h=h: e.activation(out=sgb[:, h, 0:N], in_=p[:, 0:N], func=AF.Silu), R=[pb], W=[buf("sgb")])
            for (src_t, srcn, dstT) in ((qb, "qb", G["qbT"]), (kb, "kb", G["kbT"]), (sgb, "sgb", G["sgbT"])):
                rec.dma("pool", lambda e, src_t=src_t, dstT=dstT: e.dma_start(out=dstT[:, :, t0:t0 + N].rearrange("h p n -> p h n"), in_=src_t[:, :, 0:N]),
                        R=[buf(srcn)], W=[buf(srcn + "T_d")])
            if KSTOP < 4:
                return
            for h in range(4):
                T_ = tmp
                p, pb = fm(512 + 128 * h, N, hT)
                rec.op("act", lambda e, p=p: e.activation(out=T_["sig"][:, 0:N], in_=p[:, 0:N], func=AF.Sigmoid), R=[pb], W=[buf("t_sig")])
                rec.op("dve", lambda e, h=h: e.tensor_scalar(out=T_["f"][:, 0:N], in0=T_["sig"][:, 0:N], scalar1=omlt[:, h:h + 1], scalar2=lbt[:, h:h + 1],
                                                           op0=ALU.mult, op1=ALU.add), R=[buf("t_sig"), buf("omlt"), buf("lbt")], W=[buf("t_f")])
                rec.op("dve", lambda e, h=h: e.tensor_scalar(out=T_["omf"][:, 0:N], in0=T_["sig"][:, 0:N], scalar1=nomlt[:, h:h + 1], scalar2=omlt[:, h:h + 1],
                                                           op0=ALU.mult, op1=ALU.add), R=[buf("t_sig"), buf("omlt"), buf("nomlt")], W=[buf("t_omf")])
                rec.op("act", lambda e: e.activation(out=T_["lf"][:, 0:N], in_=T_["f"][:, 0:N], func=AF.Ln), R=[buf("t_f")], W=[buf("t_lf")])
                rmask = G["rst64"][:, 0:N]
                rec.op("dve", lambda e, rmask=rmask: e.tensor_tensor_scan(out=T_["b"][:, 0:N], data0=rmask, data1=T_["lf"][:, 0:N], initial=0.0,
                                                                         op0=ALU.mult, op1=ALU.add), R=[buf("t_lf"), buf("cst")], W=[buf("t_b")])
                b3 = T_["b"][:, 0:N].rearrange("p (c k) -> p c k", k=C)
                d3 = T_["d"][:, 0:N].rearrange("p (c k) -> p c k", k=C)
                rec.op("dve", lambda e, b3=b3, d3=d3: e.tensor_tensor(out=d3, in0=b3, in1=b3[:, :, mid:mid + 1].to_broadcast([128, nch, C]), op=ALU.subtract),
                       R=[buf("t_b")], W=[buf("t_d")])
                rec.op("act", lambda e: e.activation(out=T_["eq"][:, 0:N], in_=T_["d"][:, 0:N], func=AF.Exp), R=[buf("t_d")], W=[buf("t_eq")])
                rec.op("act", lambda e: e.activation(out=T_["ek"][:, 0:N], in_=T_["d"][:, 0:N], func=AF.Exp, scale=-1.0), R=[buf("t_d")], W=[buf("t_ek")])
                rec.op("act", lambda e, h=h, b3=b3: e.activation(out=sm[:, h, 0, 0:nch], in_=b3[:, :, mid], func=AF.Exp), R=[buf("t_b")], W=[buf("sm")])
                rec.op("act", lambda e, h=h, b3=b3: e.activation(out=sm[:, h, 2, 0:nch], in_=b3[:, :, last], func=AF.Exp), R=[buf("t_b")], W=[buf("sm")])
                rec.op("act", lambda e, h=h, d3=d3: e.activation(out=sm[:, h, 1, 0:nch], in_=d3[:, :, last], func=AF.Exp), R=[buf("t_d")], W=[buf("sm")])
                p, pb = fm(0 + 128 * h, N, hT)
                rec.op("dve", lambda e, p=p, h=h: e.tensor_tensor(out=qt[:, h, 0:N], in0=p[:, 0:N], in1=T_["eq"][:, 0:N], op=ALU.mult),
                       R=[pb, buf("t_eq")], W=[buf("qt")])
                rec.op("dve", lambda e, h=h: e.tensor_tensor(out=kt[:, h, 0:N], in0=T_["omf"][:, 0:N], in1=T_["ek"][:, 0:N], op=ALU.mult),
                       R=[buf("t_omf"), buf("t_ek")], W=[buf("kt")])
                p, pb = fm(1536 + 128 * h, N, hT)
                rec.op("act", lambda e, p=p, h=h: e.activation(out=sg[:, h, 0:N], in_=p[:, 0:N], func=AF.Silu), R=[pb], W=[buf("sg")])
                for t in range(ntile):
                    rec.op("pe", lambda e, h=h, t=t: e.transpose(out=pT[:, h * 128:(h + 1) * 128], in_=kt[:, h, t * 128:(t + 1) * 128], identity=idb[:]),
                           R=[buf("kt"), buf("idb")], W=[buf("pT")])
                    rec.op("act", lambda e, h=h, t=t: e.activation(out=ktok[:, t, h, :], in_=pT[:, h * 128:(h + 1) * 128], func=AF.Copy),
                           R=[buf("pT")], W=[buf("ktok")])
            if KSTOP < 5:
                return
            hg_state_in = G["st_h"]
            def do_chunk(c):
                cols = slice(c * C, (c + 1) * C)
                t = (c * C) // 128
                r0 = (c * C) % 128
                rows = slice(r0, r0 + C)
                at = AT[c % 2]
                atb = buf("AT%d" % (c % 2))
                sbf = Sbf[c % 2]
                sbfb = buf("Sbf%d" % (c % 2))
                for h in range(4):
                    if samp:
                        rec.dma("sp", lambda e, h=h, c=c: e.dma_start(out=S[:, h, :], in_=hg_state_in[c, h]), W=[buf("S%d" % h)])
                    rec.op("dve", lambda e, h=h, c=c, sbf=sbf: e.tensor_scalar(out=sbf[:, h, :], in0=S[:, h, :], scalar1=sm[:, h, 0, c:c + 1], scalar2=None, op0=ALU.mult),
                           R=[buf("S%d" % h), buf("sm")], W=[sbfb])
                    rec.op("pe", lambda e, h=h, cols=cols, rows=rows: e.matmul(out=pS[rows, h * 64:h * 64 + C], lhsT=kt[:, h, cols], rhs=qt[:, h, cols], start=True, stop=True),
                           R=[buf("kt"), buf("qt")], W=[buf("pS")])
                    msk = m64[rows, :]
                    rec.op("dve", lambda e, h=h, rows=rows, at=at, msk=msk: e.tensor_tensor(out=at[rows, h, 0:C], in0=pS[rows, h * 64:h * 64 + C], in1=msk, op=ALU.mult),
                           R=[buf("pS"), buf("m64"), buf("cst")], W=[atb])
                    rec.op("pe", lambda e, h=h, cols=cols, sbf=sbf: e.matmul(out=pO[:, h * 64:h * 64 + C], lhsT=sbf[:, h, :], rhs=qt[:, h, cols], start=True, stop=False),
                           R=[sbfb, buf("qt")], W=[buf("pO")])
                    rec.op("pe", lambda e, h=h, rows=rows, t=t, at=at: e.matmul(out=pO[:, h * 64:h * 64 + C], lhsT=vtok[rows, t, h * 128:(h + 1) * 128], rhs=at[rows, h, 0:C], start=False, stop=True),
                           R=[buf("vtok"), atb], W=[buf("pO")])
                    rec.op("act", lambda e, h=h, cols=cols: e.activation(out=oT[:, h, cols], in_=pO[:, h * 64:h * 64 + C], func=AF.Copy), R=[buf("pO")], W=[buf("oT")])
                    rec.op("pe", lambda e, h=h, rows=rows, t=t: e.matmul(out=pP[:, h * 128:(h + 1) * 128], lhsT=ktok[rows, t, h, :], rhs=vtok[rows, t, h * 128:(h + 1) * 128], start=True, stop=True),
                           R=[buf("ktok"), buf("vtok")], W=[buf("pP")])
                    rec.op("dve", lambda e, h=h, c=c: e.tensor_scalar(out=S[:, h, :], in0=S[:, h, :], scalar1=sm[:, h, 2, c:c + 1], scalar2=None, op0=ALU.mult),
                           R=[buf("S%d" % h), buf("sm")], W=[buf("S%d" % h)])
                    rec.op("dve", lambda e, h=h, c=c: e.scalar_tensor_tensor(out=S[:, h, :], in0=pP[:, h * 128:(h + 1) * 128], scalar=sm[:, h, 1, c:c + 1], in1=S[:, h, :], op0=ALU.mult, op1=ALU.add),
                           R=[buf("pP"), buf("S%d" % h), buf("sm")], W=[buf("S%d" % h)])
                    if samp:
                        rec.dma("pool", lambda e, h=h, c=c: e.dma_start(out=G["hg_s"][c, h], in_=S[:, h, :]), R=[buf("S%d" % h)], final=True)
            for c in range(nch):
                do_chunk(c)
            if (not samp) and t0 + N == L:
                for h in range(4):
                    rec.dma("pool", lambda e, h=h: e.dma_start(out=G["hg_p"][h], in_=S[:, h, :]), R=[buf("S%d" % h)], final=True)
            if KSTOP < 6:
                return
            for h in range(4):
                rec.op("act", lambda e, h=h: e.activation(out=sq[:, 0:N], in_=oT[:, h, 0:N], func=AF.Square), R=[buf("oT")], W=[buf("sq")])
                rec.op("pe", lambda e: e.matmul(out=pN[:, 0:N], lhsT=onesb[:], rhs=sq[:, 0:N], start=True, stop=True), R=[buf("onesb"), buf("sq")], W=[buf("pN")])
                rec.op("act", lambda e: e.activation(out=lnt[:, 0:N], in_=pN[:, 0:N], func=AF.Ln, scale=1.0 / 128, bias=G["epsb"][:, 0:1]), R=[buf("pN"), buf("epsb")], W=[buf("lnt")])
                rec.op("act", lambda e: e.activation(out=lnt[:, 0:N], in_=lnt[:, 0:N], func=AF.Exp, scale=-0.5), R=[buf("lnt")], W=[buf("lnt")])
                rec.op("dve", lambda e, h=h: e.scalar_tensor_tensor(out=tg[:, 0:N], in0=lnt[:, 0:N], scalar=vec[:, 32 + h:33 + h], in1=sg[:, h, 0:N], op0=ALU.mult, op1=ALU.mult),
                       R=[buf("lnt"), buf("vec"), buf("sg")], W=[buf("tg")])
                rec.op("dve", lambda e, h=h: e.tensor_tensor(out=oa[:, h, 0:N], in0=oT[:, h, 0:N], in1=tg[:, 0:N], op=ALU.mult), R=[buf("oT"), buf("tg")], W=[buf("oa")])
            rec.dma("pool", lambda e: e.dma_start(out=G["oaT"][:, :, t0:t0 + N].rearrange("h p n -> p h n"), in_=oa[:, :, 0:N]), R=[buf("oa")], W=[buf("oaT_d")])
        for (t0_, N_, samp_) in groups:
            if KSTOP >= 1:
                do_group(t0_, N_, samp_)
        flush(k, rec)


def phase2(G):
    k, rec, nc = G["k"], G["rec"], G["nc"]
    buf = k.buf
    L, NPG = k.L, k.NPG
    NT = L // 128
    NG = L // 512
    pS, pP, pO, pN, pX = G["pS"], G["pP"], G["pO"], G["pN"], G["pX"]
    onesb, cst = G["onesb"], G["cst"]
    KW = max(L, NPG * 128 + 128)
    NB = max(NT, NPG + 1)
    with ExitStack() as ph:
        sb = lambda n, s, d: ph.enter_context(nc.sbuf_tensor(n, s, d))
        kT = sb("kT", [128, 4, KW], BF16)
        V = sb("V", [128, NB, 512], BF16)
        ngs = sb("ngs", [128, NB, 4], F32)
        biasT = [sb("biasT%d" % i, [128, 4, NB], F32) for i in range(2)]
        qb = [sb("qb2_%d" % i, [128, 4, 512], BF16) for i in range(2)]
        sgb = [sb("sgb2_%d" % i, [128, 4, 512], BF16) for i in range(2)]
        ob = [sb("ob%d" % i, [128, 4, 512], BF16) for i in range(2)]
        PT = [sb("PT%d" % i, [128, 512], BF16) for i in range(2)]
        trib = sb("trib", [128, 128], BF16)
        m64b = sb("m64b", [128, 64], BF16)
        nq = sb("nq", [128, 4], F32)
        rl = sb("rl", [128, 512], F32)
        tg2 = sb("tg2", [128, 512], F32)
        rec.op("dve", lambda e: e.tensor_copy(out=trib[:], in_=G["tri"]), R=[buf("cst")], W=[buf("trib")])
        rec.op("dve", lambda e: e.tensor_copy(out=m64b[:], in_=cst[:, 256:320]), R=[buf("cst")], W=[buf("m64b")])
        for h in range(4):
            rec.dma("sp", lambda e, h=h: e.dma_start(out=kT[:, h, 0:L], in_=G["kbT"][h, :, 0:L]), R=[buf("kbT_d")], W=[buf("kT")])
        rec.dma("sp", lambda e: e.dma_start(out=V[:, 0:NT, :], in_=G["vbs"][0:L, :].rearrange("(t p) c -> p t c", p=128)), R=[buf("vbs_d")], W=[buf("V")])
        rec.dma("sp", lambda e: e.dma_start(out=ngs[:, 0:NT, :], in_=G["negc"][0:L, :].rearrange("(t p) h -> p t h", p=128)), R=[buf("negc_d")], W=[buf("ngs")])

        def attend(h, qt_, qtb, NQ, blist, bT, bTb):
            nbk = len(blist)
            for bi, (kc0, ks, vt, bj, c0, diag) in enumerate(blist):
                n = NQ - c0
                pb = (pS, pP)[bi % 2]
                pbb = buf(("pS", "pP")[bi % 2])
                P_ = PT[bi % 2]
                Pb = buf("PT%d" % (bi % 2))
                rec.op("pe", lambda e, pb=pb, kc0=kc0, ks=ks, c0=c0, n=n: e.matmul(out=pb[0:ks, 0:n], lhsT=kT[:, h, kc0:kc0 + ks], rhs=qt_[:, h, c0:NQ], start=True, stop=True),
                       R=[buf("kT"), qtb], W=[pbb])
                rec.op("act", lambda e, pb=pb, P_=P_, ks=ks, n=n, bj=bj: e.activation(out=P_[0:ks, 0:n], in_=pb[0:ks, 0:n], func=AF.Exp, scale=SCALE, bias=bT[0:ks, h, bj:bj + 1]),
                       R=[pbb, bTb], W=[Pb])
                if diag:
                    mk = trib if ks == 128 else m64b
                    rec.op("dve", lambda e, P_=P_, ks=ks, mk=mk: e.tensor_tensor(out=P_[0:ks, 0:ks], in0=P_[0:ks, 0:ks], in1=mk[0:ks, 0:ks], op=ALU.mult),
                           R=[Pb, buf("trib"), buf("m64b")], W=[Pb])
                rec.op("pe", lambda e, P_=P_, ks=ks, vt=vt, c0=c0, n=n, bi=bi: e.matmul(out=pO[:, c0:NQ], lhsT=V[0:ks, vt, h * 128:(h + 1) * 128], rhs=P_[0:ks, 0:n], start=(bi == 0), stop=(bi == nbk - 1)),
                       R=[buf("V"), Pb], W=[buf("pO")])
                rec.op("pe", lambda e, P_=P_, ks=ks, c0=c0, n=n, bi=bi: e.matmul(out=pN[:, c0:NQ], lhsT=onesb[0:ks, :], rhs=P_[0:ks, 0:n], start=(bi == 0), stop=(bi == nbk - 1)),
                       R=[buf("onesb"), Pb], W=[buf("pN")])

        def epilogue(h, NQ, sg_, sgbb, ob_, obb):
            rec.op("dve", lambda e: e.reciprocal(out=rl[:, 0:NQ], in_=pN[:, 0:NQ]), R=[buf("pN")], W=[buf("rl")])
            rec.op("dve", lambda e: e.tensor_tensor(out=tg2[:, 0:NQ], in0=rl[:, 0:NQ], in1=sg_[:, h, 0:NQ], op=ALU.mult), R=[buf("rl"), sgbb], W=[buf("tg2")])
            rec.op("dve", lambda e: e.tensor_tensor(out=ob_[:, h, 0:NQ], in0=pO[:, 0:NQ], in1=tg2[:, 0:NQ], op=ALU.mult), R=[buf("pO"), buf("tg2")], W=[obb])

        def load_q(i, c0, n):
            q_, qbb = qb[i % 2], buf("qb2_%d" % (i % 2))
            s_, sbb = sgb[i % 2], buf("sgb2_%d" % (i % 2))
            rec.dma("sp", lambda e: e.dma_start(out=q_[:, :, 0:n], in_=G["qbT"][:, :, c0:c0 + n].rearrange("h p n -> p h n")), R=[buf("qbT_d")], W=[qbb])
            rec.dma("sp", lambda e: e.dma_start(out=s_[:, :, 0:n], in_=G["sgbT"][:, :, c0:c0 + n].rearrange("h p n -> p h n")), R=[buf("sgbT_d")], W=[sbb])
            return q_, qbb, s_, sbb

        def prompt_group(I):
            t0 = 512 * I
            q_, qbb, s_, sbb = load_q(I, t0, 512)
            o_, obb = ob[I % 2], buf("ob%d" % (I % 2))
            bT, bTb = biasT[I % 2], buf("biasT%d" % (I % 2))
            nb = 4 * I + 4
            rec.op("pe", lambda e: e.matmul(out=pX[:, 0:4], lhsT=G["ones"][0:1, :], rhs=ngs[0:1, 4 * I, :], start=True, stop=True), R=[buf("cst"), buf("ngs")], W=[buf("pX")])
            rec.op("act", lambda e: e.activation(out=nq[:], in_=pX[:, 0:4], func=AF.Copy), R=[buf("pX")], W=[buf("nq")])
            for h in range(4):
                rec.op("dve", lambda e, h=h: e.tensor_scalar(out=bT[:, h, 0:nb], in0=ngs[:, 0:nb, h], scalar1=nq[:, h:h + 1], scalar2=None, op0=ALU.subtract),
                       R=[buf("ngs"), buf("nq")], W=[bTb])
            for h in range(4):
                blist = []
                for j in range(nb):
                    r = j - 4 * I
                    blist.append((128 * j, 128, j, j, 128 * max(r, 0), r >= 0))
                attend(h, q_, qbb, 512, blist, bT, bTb)
                epilogue(h, 512, s_, sbb, o_, obb)
            rec.dma("pool", lambda e: e.dma_start(out=G["obT"][:, :, t0:t0 + 512].rearrange("h p n -> p h n"), in_=o_[:, :, :]), R=[obb], W=[buf("obT_d")])

        for I in range(NG):
            prompt_group(I)

        pti = sb("pti", [128, NPG], I32)
        ptc = sb("ptc", [128, 1], I32)
        idxf = sb("idxf", [128, NPG], F32)
        idx = sb("idx", [128, NPG], I32)
        lfp = sb("lfp", [128, 512], F32)
        cum = sb("cum", [128, 4, 128], F32)
        T4 = sb("T4", [128, 4], F32)
        TR = sb("TR", [128, 4], F32)
        sufp = sb("sufp", [128, 4, 128], F32)
        kst = [sb("kst%d" % i, [128, 512], F32) for i in range(2)]
        vst = [sb("vst%d" % i, [128, 512], F32) for i in range(2)]
        iot = cst[:, 1216:1217]
        ustr = cst[:, 1088:1216]

        def sample_seq(j):
            c0 = L + 64 * j
            rec.dma("sp", lambda e: e.dma_start(out=pti[:], in_=G["ptab"][j].partition_broadcast(128)), W=[buf("pti")])
            rec.dma("sp", lambda e: e.dma_start(out=ptc[0:NPG, :], in_=G["ptab"][j].rearrange("(n o) -> n o", o=1)), W=[buf("ptc")])
            rec.op("dve", lambda e: e.tensor_scalar(out=idxf[:], in0=pti[:], scalar1=128.0, scalar2=iot, op0=ALU.mult, op1=ALU.add), R=[buf("pti"), buf("cst")], W=[buf("idxf")])
            rec.op("dve", lambda e: e.tensor_copy(out=idx[:], in_=idxf[:]), R=[buf("idxf")], W=[buf("idx")])
            rec.dma("pool", lambda e: e.indirect_dma_start(out=lfp[0:NPG, :], out_offset=None, in_=G["clf"], in_offset=bass.IndirectOffsetOnAxis(ap=ptc[0:NPG, 0:1], axis=0)),
                    R=[buf("ptc")], W=[buf("lfp")])
            lf3 = lfp[0:NPG, :].rearrange("p (s h) -> p h s", h=4)
            bT, bTb = biasT[j % 2], buf("biasT%d" % (j % 2))
            for h in range(4):
                rec.op("dve", lambda e, h=h: e.tensor_tensor_scan(out=cum[0:NPG, h, :], data0=G["ones"][0:NPG, :], data1=lf3[:, h, :], initial=0.0, op0=ALU.mult, op1=ALU.add),
                       R=[buf("lfp"), buf("cst")], W=[buf("cum")])
            rec.op("dve", lambda e: e.tensor_copy(out=T4[0:NPG, :], in_=cum[0:NPG, :, 127]), R=[buf("cum")], W=[buf("T4")])
            rec.op("pe", lambda e: e.matmul(out=pX[0:NPG, 0:4], lhsT=ustr[0:NPG, 0:NPG], rhs=T4[0:NPG, :], start=True, stop=True), R=[buf("cst"), buf("T4")], W=[buf("pX")])
            rec.op("dve", lambda e: e.tensor_tensor(out=TR[0:NPG, :], in0=pX[0:NPG, 0:4], in1=T4[0:NPG, :], op=ALU.add), R=[buf("pX"), buf("T4")], W=[buf("TR")])
            for h in range(4):
                rec.op("dve", lambda e, h=h: e.tensor_scalar(out=sufp[0:NPG, h, :], in0=cum[0:NPG, h, :], scalar1=-1.0, scalar2=TR[0:NPG, h:h + 1], op0=ALU.mult, op1=ALU.add),
                       R=[buf("cum"), buf("TR")], W=[buf("sufp")])
                rec.op("pe", lambda e, h=h: e.transpose(out=pX[:, 0:NPG], in_=sufp[0:NPG, h, :], identity=G["ident"][0:NPG, 0:NPG]), R=[buf("sufp"), buf("cst")], W=[buf("pX")])
                rec.op("act", lambda e, h=h: e.activation(out=bT[:, h, 0:NPG], in_=pX[:, 0:NPG], func=AF.Copy), R=[buf("pX")], W=[bTb])
            rec.dma("sp", lambda e: e.dma_start(out=ngs[0:64, 0, :], in_=G["negc"][c0:c0 + 64, :]), R=[buf("negc_d")], W=[buf("ngs")])
            rec.op("dve", lambda e: e.tensor_copy(out=bT[0:64, :, NPG], in_=ngs[0:64, 0, :]), R=[buf("ngs")], W=[bTb])
            for i in range(NPG):
                ks_, ksb = kst[i % 2], buf("kst%d" % (i % 2))
                vs_, vsb = vst[i % 2], buf("vst%d" % (i % 2))
                rec.dma("pool", lambda e, ks_=ks_, i=i: e.indirect_dma_start(out=ks_[:], out_offset=None, in_=G["ck"], in_offset=bass.IndirectOffsetOnAxis(ap=idx[:, i:i + 1], axis=0)),
                        R=[buf("idx")], W=[ksb])
                rec.dma("pool", lambda e, vs_=vs_, i=i: e.indirect_dma_start(out=vs_[:], out_offset=None, in_=G["cv"], in_offset=bass.IndirectOffsetOnAxis(ap=idx[:, i:i + 1], axis=0)),
                        R=[buf("idx")], W=[vsb])
                for h in range(4):
                    rec.op("pe", lambda e, ks_=ks_, h=h: e.transpose(out=pX[:, h * 128:(h + 1) * 128], in_=ks_[:, h * 128:(h + 1) * 128], identity=G["ident"]), R=[ksb, buf("cst")], W=[buf("pX")])
                rec.op("act", lambda e, i=i: e.activation(out=kT[:, :, 128 * i:128 * i + 128], in_=pX[:].rearrange("p (h s) -> p h s", h=4), func=AF.Copy), R=[buf("pX")], W=[buf("kT")])
                rec.op("dve", lambda e, vs_=vs_, i=i: e.tensor_copy(out=V[:, i, :], in_=vs_[:]), R=[vsb], W=[buf("V")])
            rec.dma("sp", lambda e: e.dma_start(out=kT[:, :, 128 * NPG:128 * NPG + 64], in_=G["kbT"][:, :, c0:c0 + 64].rearrange("h p n -> p h n")), R=[buf("kbT_d")], W=[buf("kT")])
            rec.dma("sp", lambda e: e.dma_start(out=V[0:64, NPG, :], in_=G["vbs"][c0:c0 + 64, :]), R=[buf("vbs_d")], W=[buf("V")])
            q_, qbb, s_, sbb = load_q(j, c0, 64)
            o_, obb = ob[j % 2], buf("ob%d" % (j % 2))
            for h in range(4):
                blist = [(128 * i, 128, i, i, 0, False) for i in range(NPG)] + [(128 * NPG, 64, NPG, NPG, 0, True)]
                attend(h, q_, qbb, 64, blist, bT, bTb)
                epilogue(h, 64, s_, sbb, o_, obb)
            rec.dma("pool", lambda e: e.dma_start(out=G["obT"][:, :, c0:c0 + 64].rearrange("h p n -> p h n"), in_=o_[:, :, 0:64]), R=[obb], W=[buf("obT_d")])

        for j in range(4):
            sample_seq(j)
        flush(k, rec)


def phase3(G):
    k, rec, nc = G["k"], G["rec"], G["nc"]
    buf = k.buf
    L = k.L
    NG = L // 512
    pA, pT, pS, pP, pO, pN, pX = G["pA"], G["pT"], G["pS"], G["pP"], G["pO"], G["pN"], G["pX"]
    vec, m64, onesb, idb = G["vec"], G["m64"], G["onesb"], G["idb"]
    with ExitStack() as ph:
        sb = lambda n, s, d: ph.enter_context(nc.sbuf_tensor(n, s, d))
        W1 = load_weight_bf16(G, ph, G["w_out_e"], 1024, None, "W1")
        W2 = load_weight_bf16(G, ph, G["w_in_o"], 3088, 8, "W2")
        W3 = load_weight_bf16(G, ph, G["w_out_o"], 1024, None, "W3")
        wgf = sb("wgf", [16, 512], F32)
        wg = sb("wg", [16, 512], BF16)
        rec.dma("sp", lambda e: e.dma_start(out=wgf[:], in_=G["w_gate"]), W=[buf("wgf")])
        rec.op("dve", lambda e: e.tensor_copy(out=wg[:], in_=wgf[:]), R=[buf("wgf")], W=[buf("wg")])
        fnb = sb("fnb", [128, D], F32)
        rec.dma("sp", lambda e: e.dma_start(out=fnb[:], in_=G["fnorm"].partition_broadcast(128)), W=[buf("fnb")])
        G["junk"] = sb("junk3", [128, 1024], BF16)
        G["ss"] = sb("ss3", [128, 8], F32)
        G["epsb"] = sb("epsb3", [128, 1], F32)
        rec.op("pool", lambda e: e.memset(G["epsb"][:], EPS), W=[buf("epsb")])
        ss2 = sb("ss2", [128, 4], F32)
        oc = sb("oc", [128, 8, 512], BF16)
        xt = [sb("x3_%d" % i, [128, D], F32) for i in range(2)]
        x1 = sb("x1", [128, 4, D], F32)
        hb = [sb("hb3_%d" % i, [128, D], BF16) for i in range(2)]
        hT = sb("hT3", [128, KC, 512], BF16)
        tmp = {n: sb("u_" + n, [128, 512], F32) for n in ("lf", "b", "d", "eq", "ek")}
        qt = sb("qt3", [128, 4, 512], BF16)
        kt = sb("kt3", [128, 4, 512], BF16)
        sm = sb("sm3", [128, 4, 3, 8], F32)
        vtok = sb("vtok3", [128, 4, 1024], BF16)
        ktok = sb("ktok3", [128, 4, 4, 128], BF16)
        AT = [sb("AT3_%d" % i, [128, 4, 64], BF16) for i in range(2)]
        S = sb("S3", [128, 4, 256], F32)
        Sbf = sb("Sbf3", [128, 4, 256], BF16)
        oT = sb("oT3", [128, 8, 512], F32)
        sq = sb("sq3", [128, 2, 512], BF16)
        lnt = sb("lnt3", [128, 512], F32)
        sg = sb("sg3", [128, 8, 512], BF16)
        tg = sb("tg3", [128, 512], F32)
        oc2 = oc
        rT = sb("rT", [16, 512], BF16)
        rec.op("pool", lambda e: e.memset(S[:], 0.0), W=[buf("G%d" % h) for h in range(4)])
        pa_i = [0]

        def nextp():
            i = pa_i[0] % 2
            pa_i[0] += 1
            return pA[i], buf("pA%d" % i)

        def fm(Wt, wname, col, M, N, src, srcb):
            p, pb = nextp()
            for kc in range(KC):
                rec.op("pe", lambda e, kc=kc: e.matmul(out=p[0:M, 0:N], lhsT=Wt[:, kc, col:col + M], rhs=src[:, kc, 0:N], start=(kc == 0), stop=(kc == KC - 1)),
                       R=[buf(wname), srcb], W=[pb])
            return p, pb

        def tm(Wt, wname, col, ncols, t, src, srcb):
            p, pb = nextp()
            for kc in range(KC):
                rec.op("pe", lambda e, kc=kc: e.matmul(out=p[:, 0:ncols], lhsT=src[:, kc, t * 128:(t + 1) * 128], rhs=Wt[:, kc, col:col + ncols], start=(kc == 0), stop=(kc == KC - 1)),
                       R=[buf(wname), srcb], W=[pb])
            return p, pb

        xi = [0]

        def do_group(t0, N, samp):
            ntile = N // 128
            C = 64
            nch = N // C
            mid = 1 if samp else 31
            last = 3 if samp else C - 1
            rec.dma("sp", lambda e: e.dma_start(out=oc[:, 0:4, 0:N], in_=G["oaT"][:, :, t0:t0 + N].rearrange("h p n -> p h n")), R=[buf("oaT_d")], W=[buf("oc")])
            rec.dma("sp", lambda e: e.dma_start(out=oc[:, 4:8, 0:N], in_=G["obT"][:, :, t0:t0 + N].rearrange("h p n -> p h n")), R=[buf("obT_d")], W=[buf("oc")])
            for t in range(ntile):
                x_ = xt[xi[0] % 2]
                xb_ = buf("x3_%d" % (xi[0] % 2))
                h_ = hb[xi[0] % 2]
                hb_ = buf("hb3_%d" % (xi[0] % 2))
                xi[0] += 1
                src = G["xs"][t * 128:(t + 1) * 128, :] if samp else G["xp"][t0 + t * 128:t0 + (t + 1) * 128, :]
                rec.dma("sp", lambda e, x_=x_, src=src: e.dma_start(out=x_[:], in_=src), W=[xb_])
                for hf in range(2):
                    p, pb = tm(W1, "W1", hf * 512, 512, t, oc, buf("oc"))
                    rec.op("dve", lambda e, p=p, t=t, hf=hf, x_=x_: e.tensor_tensor(out=x1[:, t, hf * 512:(hf + 1) * 512], in0=p[:, 0:512], in1=x_[:, hf * 512:(hf + 1) * 512], op=ALU.add),
                           R=[pb, xb_], W=[buf("x1_%d" % t)])
                norm_and_transpose(G, x1[:, t, :], buf("x1_%d" % t), t, h_, hb_, hT, buf("hT3"), t * 128)
            for t in range(ntile):
                for hf in range(2):
                    p, pb = tm(W2, "W2", 1024 + hf * 512, 512, t, hT, buf("hT3"))
                    rec.op("act", lambda e, p=p, t=t, hf=hf: e.activation(out=vtok[:, t, hf * 512:(hf + 1) * 512], in_=p[:, 0:512], func=AF.Copy), R=[pb], W=[buf("vtok3")])
            p, pb = fm(W2, "W2", 3072, 16, N, hT, buf("hT3"))
            rec.op("act", lambda e, p=p: e.activation(out=rT[:, 0:N], in_=p[0:16, 0:N], func=AF.Copy), R=[pb], W=[buf("rT")])
            for c8 in range(8):
                p, pb = fm(W2, "W2", 2048 + 128 * c8, 128, N, hT, buf("hT3"))
                rec.op("act", lambda e, p=p, c8=c8: e.activation(out=sg[:, c8, 0:N], in_=p[:, 0:N], func=AF.Silu), R=[pb], W=[buf("sg3")])
            for h in range(4):
                T_ = tmp
                p, pb = nextp()
                rec.op("pe", lambda e, p=p, h=h: e.matmul(out=p[:, 0:N], lhsT=wg[0:16, h * 128:(h + 1) * 128], rhs=rT[0:16, 0:N], start=True, stop=True), R=[buf("wg"), buf("rT")], W=[pb])
                rec.op("act", lambda e, p=p, h=h: e.activation(out=T_["lf"][:, 0:N], in_=p[:, 0:N], func=AF.Sigmoid, bias=vec[:, 36 + h:37 + h]), R=[pb, buf("vec")], W=[buf("u_lf")])
                rec.op("act", lambda e: e.activation(out=T_["lf"][:, 0:N], in_=T_["lf"][:, 0:N], func=AF.Ln), R=[buf("u_lf")], W=[buf("u_lf")])
                rec.op("dve", lambda e: e.tensor_scalar(out=T_["lf"][:, 0:N], in0=T_["lf"][:, 0:N], scalar1=1.0 / 16.0, scalar2=None, op0=ALU.mult), R=[buf("u_lf")], W=[buf("u_lf")])
                rec.op("dve", lambda e: e.tensor_tensor_scan(out=T_["b"][:, 0:N], data0=G["rst64"][:, 0:N], data1=T_["lf"][:, 0:N], initial=0.0, op0=ALU.mult, op1=ALU.add),
                       R=[buf("u_lf"), buf("cst")], W=[buf("u_b")])
                b3 = T_["b"][:, 0:N].rearrange("p (c k) -> p c k", k=C)
                d3 = T_["d"][:, 0:N].rearrange("p (c k) -> p c k", k=C)
                rec.op("dve", lambda e, b3=b3, d3=d3: e.tensor_tensor(out=d3, in0=b3, in1=b3[:, :, mid:mid + 1].to_broadcast([128, nch, C]), op=ALU.subtract), R=[buf("u_b")], W=[buf("u_d")])
                rec.op("act", lambda e: e.activation(out=T_["eq"][:, 0:N], in_=T_["d"][:, 0:N], func=AF.Exp), R=[buf("u_d")], W=[buf("u_eq")])
                rec.op("act", lambda e: e.activation(out=T_["ek"][:, 0:N], in_=T_["d"][:, 0:N], func=AF.Exp, scale=-1.0), R=[buf("u_d")], W=[buf("u_ek")])
                rec.op("act", lambda e, h=h, b3=b3: e.activation(out=sm[:, h, 0, 0:nch], in_=b3[:, :, mid], func=AF.Exp), R=[buf("u_b")], W=[buf("sm3")])
                rec.op("act", lambda e, h=h, b3=b3: e.activation(out=sm[:, h, 2, 0:nch], in_=b3[:, :, last], func=AF.Exp), R=[buf("u_b")], W=[buf("sm3")])
                rec.op("act", lambda e, h=h, d3=d3: e.activation(out=sm[:, h, 1, 0:nch], in_=d3[:, :, last], func=AF.Exp), R=[buf("u_d")], W=[buf("sm3")])
                p, pb = fm(W2, "W2", 128 * h, 128, N, hT, buf("hT3"))
                rec.op("dve", lambda e, p=p, h=h: e.scalar_tensor_tensor(out=qt[:, h, 0:N], in0=p[:, 0:N], scalar=SCALE, in1=T_["eq"][:, 0:N], op0=ALU.mult, op1=ALU.mult),
                       R=[pb, buf("u_eq")], W=[buf("qt3")])
                p, pb = fm(W2, "W2", 512 + 128 * h, 128, N, hT, buf("hT3"))
                rec.op("dve", lambda e, p=p, h=h: e.tensor_tensor(out=kt[:, h, 0:N], in0=p[:, 0:N], in1=T_["ek"][:, 0:N], op=ALU.mult), R=[pb, buf("u_ek")], W=[buf("kt3")])
                for t in range(ntile):
                    rec.op("pe", lambda e, h=h, t=t: e.transpose(out=pT[:, h * 128:(h + 1) * 128], in_=kt[:, h, t * 128:(t + 1) * 128], identity=idb[:]), R=[buf("kt3"), buf("idb")], W=[buf("pT")])
                    rec.op("act", lambda e, h=h, t=t: e.activation(out=ktok[:, t, h, :], in_=pT[:, h * 128:(h + 1) * 128], func=AF.Copy), R=[buf("pT")], W=[buf("ktok3")])

            def do_chunk(c):
                cols = slice(c * C, (c + 1) * C)
                t = (c * C) // 128
                r0 = (c * C) % 128
                rows = slice(r0, r0 + C)
                at = AT[c % 2]
                atb = buf("AT3_%d" % (c % 2))
                for h in range(4):
                    Sb = buf("G%d" % h)
                    pst, pstb = (pP, buf("pP")) if h < 2 else (pX, buf("pX"))
                    hc = (h % 2) * 256
                    if samp:
                        rec.dma("sp", lambda e, h=h: e.dma_start(out=S[:, h, :], in_=G["st_g"][c, h]), W=[Sb])
                    rec.op("dve", lambda e, h=h: e.tensor_scalar(out=Sbf[:, h, :], in0=S[:, h, :], scalar1=sm[:, h, 0, c:c + 1], scalar2=None, op0=ALU.mult), R=[Sb, buf("sm3")], W=[buf("Sbf3_%d" % h)])
                    rec.op("pe", lambda e, h=h: e.matmul(out=pS[rows, h * 64:h * 64 + C], lhsT=kt[:, h, cols], rhs=qt[:, h, cols], start=True, stop=True), R=[buf("kt3"), buf("qt3")], W=[buf("pS")])
                    rec.op("dve", lambda e, h=h: e.tensor_tensor(out=at[rows, h, 0:C], in0=pS[rows, h * 64:h * 64 + C], in1=m64[rows, :], op=ALU.mult), R=[buf("pS"), buf("m64")], W=[atb])
                    for e2 in range(2):
                        oc_ = (2 * h + e2) * 64
                        rec.op("pe", lambda e, h=h, e2=e2, oc_=oc_: e.matmul(out=pO[:, oc_:oc_ + C], lhsT=Sbf[:, h, e2 * 128:(e2 + 1) * 128], rhs=qt[:, h, cols], start=True, stop=False),
                               R=[buf("Sbf3_%d" % h), buf("qt3")], W=[buf("pO")])
                        rec.op("pe", lambda e, h=h, e2=e2, oc_=oc_: e.matmul(out=pO[:, oc_:oc_ + C], lhsT=vtok[rows, t, h * 256 + e2 * 128:h * 256 + (e2 + 1) * 128], rhs=at[rows, h, 0:C], start=False, stop=True),
                               R=[buf("vtok3"), atb], W=[buf("pO")])
                        rec.op("act", lambda e, h=h, e2=e2, oc_=oc_: e.activation(out=oT[:, 2 * h + e2, cols], in_=pO[:, oc_:oc_ + C], func=AF.Copy), R=[buf("pO")], W=[buf("oT3")])
                    rec.op("pe", lambda e, h=h: e.matmul(out=pst[:, hc:hc + 256], lhsT=ktok[rows, t, h, :], rhs=vtok[rows, t, h * 256:(h + 1) * 256], start=True, stop=True), R=[buf("ktok3"), buf("vtok3")], W=[pstb])
                    rec.op("dve", lambda e, h=h: e.tensor_scalar(out=S[:, h, :], in0=S[:, h, :], scalar1=sm[:, h, 2, c:c + 1], scalar2=None, op0=ALU.mult), R=[Sb, buf("sm3")], W=[Sb])
                    rec.op("dve", lambda e, h=h: e.scalar_tensor_tensor(out=S[:, h, :], in0=pst[:, hc:hc + 256], scalar=sm[:, h, 1, c:c + 1], in1=S[:, h, :], op0=ALU.mult, op1=ALU.add), R=[pstb, Sb, buf("sm3")], W=[Sb])
                    if samp:
                        rec.dma("pool", lambda e, h=h: e.dma_start(out=G["gl_s"][c, h], in_=S[:, h, :]), R=[Sb], final=True)

            for c in range(nch):
                do_chunk(c)
            if (not samp) and t0 + N == L:
                for h in range(4):
                    rec.dma("pool", lambda e, h=h: e.dma_start(out=G["gl_p"][h], in_=S[:, h, :]), R=[buf("G%d" % h)], final=True)
            for h in range(4):
                for e2 in range(2):
                    rec.op("act", lambda e, h=h, e2=e2: e.activation(out=sq[:, e2, 0:N], in_=oT[:, 2 * h + e2, 0:N], func=AF.Square), R=[buf("oT3")], W=[buf("sq3")])
                for e2 in range(2):
                    rec.op("pe", lambda e, e2=e2: e.matmul(out=pN[:, 0:N], lhsT=onesb[:], rhs=sq[:, e2, 0:N], start=(e2 == 0), stop=(e2 == 1)), R=[buf("onesb"), buf("sq3")], W=[buf("pN")])
                rec.op("act", lambda e: e.activation(out=lnt[:, 0:N], in_=pN[:, 0:N], func=AF.Ln, scale=1.0 / 256, bias=G["epsb"][:, 0:1]), R=[buf("pN"), buf("epsb")], W=[buf("lnt3")])
                rec.op("act", lambda e: e.activation(out=lnt[:, 0:N], in_=lnt[:, 0:N], func=AF.Exp, scale=-0.5), R=[buf("lnt3")], W=[buf("lnt3")])
                for e2 in range(2):
                    c8 = 2 * h + e2
                    rec.op("dve", lambda e, c8=c8: e.scalar_tensor_tensor(out=tg[:, 0:N], in0=lnt[:, 0:N], scalar=vec[:, 40 + c8:41 + c8], in1=sg[:, c8, 0:N], op0=ALU.mult, op1=ALU.mult),
                           R=[buf("lnt3"), buf("vec"), buf("sg3")], W=[buf("tg3")])
                    rec.op("dve", lambda e, c8=c8: e.tensor_tensor(out=oc2[:, c8, 0:N], in0=oT[:, c8, 0:N], in1=tg[:, 0:N], op=ALU.mult), R=[buf("oT3"), buf("tg3")], W=[buf("oc")])
            for t in range(ntile):
                xb1 = buf("x1_%d" % t)
                for hf in range(2):
                    p, pb = tm(W3, "W3", hf * 512, 512, t, oc2, buf("oc"))
                    rec.op("dve", lambda e, p=p, t=t, hf=hf: e.tensor_tensor(out=x1[:, t, hf * 512:(hf + 1) * 512], in0=p[:, 0:512], in1=x1[:, t, hf * 512:(hf + 1) * 512], op=ALU.add), R=[pb, xb1], W=[xb1])
                rec.op("act", lambda e, t=t: e.activation(out=G["junk"][:], in_=x1[:, t, :], func=AF.Square, accum_out=ss2[:, t:t + 1]), R=[xb1], W=[buf("junk"), buf("ss2")])
                rec.op("act", lambda e, t=t: e.activation(out=ss2[:, t:t + 1], in_=ss2[:, t:t + 1], func=AF.Ln, scale=1.0 / D, bias=G["epsb"][:, 0:1]), R=[buf("ss2"), buf("epsb")], W=[buf("ss2")])
                rec.op("act", lambda e, t=t: e.activation(out=ss2[:, t:t + 1], in_=ss2[:, t:t + 1], func=AF.Exp, scale=-0.5), R=[buf("ss2")], W=[buf("ss2")])
                rec.op("dve", lambda e, t=t: e.scalar_tensor_tensor(out=x1[:, t, :], in0=x1[:, t, :], scalar=ss2[:, t:t + 1], in1=fnb[:], op0=ALU.mult, op1=ALU.mult), R=[xb1, buf("ss2"), buf("fnb")], W=[xb1])
                dst = G["y_s"][t * 128:(t + 1) * 128, :] if samp else G["y_p"][t0 + t * 128:t0 + (t + 1) * 128, :]
                rec.dma("pool", lambda e, t=t, dst=dst: e.dma_start(out=dst, in_=x1[:, t, :]), R=[xb1], final=True)

        groups = [(g * 512, 512, False) for g in range(NG)] + [(L, 256, True)]
        for (t0_, N_, samp_) in groups:
            do_group(t0_, N_, samp_)
        flush(k, rec)


def make_consts():
    c = np.zeros((128, 1280), np.float32)
    p = np.arange(128)[:, None]
    j = np.arange(128)[None, :]
    c[:, 0:128] = np.eye(128)
    c[:, 128:256] = (p <= j)
    j64 = np.arange(64)[None, :]
    c[:, 256:320] = ((p % 64) <= j64)
    c[:, 320:448] = 1.0
    c[:, 448:576] = ((p // 64) == (j // 64)) & (p <= j)
    j512 = np.arange(512)[None, :]
    c[:, 576:1088] = (j512 % 64 != 0) * np.ones((128, 1))
    c[:, 1088:1216] = (p > j)
    c[:, 1216] = np.arange(128)
    return c

def pack_vecs(inp):
    v = np.zeros((128, 64), np.float32)
    f = lambda a, n: np.asarray(a, np.float32).reshape(n, 128).T
    v[:, 0:8] = f(inp['norm_even'][0], 8)
    v[:, 8:16] = f(inp['norm_odd'][0], 8)
    v[:, 16:24] = f(inp['final_norm'], 8)
    v[:, 24:28] = f(inp['lb_logits'][0], 4)
    v[:, 28:32] = f(inp['lb_logits'][1], 4)
    v[:, 32:36] = f(inp['hgrn_gain'][0], 4)
    v[:, 36:40] = f(inp['b_gla_gate'][0], 4)
    v[:, 40:48] = f(inp['gla_gain'][0], 8)
    return v

def core_inputs(inp, c, L, NPG, nseq_per_core=4):
    b = c // 4
    m = {}
    m['xp'] = np.ascontiguousarray(inp['x_prompt'][b, :L])
    xs = np.zeros((256, 1024), np.float32)
    for j in range(4):
        xs[64 * j:64 * j + 4] = inp['x_sample'][4 * c + j]
    m['xs'] = xs
    m['w_in_even'] = np.ascontiguousarray(inp['w_in_even'][0])
    m['w_out_even'] = np.ascontiguousarray(inp['w_out_even'][0])
    m['w_in_odd'] = np.ascontiguousarray(inp['w_in_odd'][0])
    m['w_out_odd'] = np.ascontiguousarray(inp['w_out_odd'][0])
    m['w_gla_gate'] = np.ascontiguousarray(inp['w_gla_gate'][0])
    m['vecs'] = pack_vecs(inp)
    m['fnorm'] = np.ascontiguousarray(np.asarray(inp['final_norm'], np.float32))
    m['bfox'] = np.ascontiguousarray(np.broadcast_to(np.asarray(inp['b_fox_f'][0], np.float32)[None, :], (128, 4)))
    m['consts'] = make_consts()
    m['st_hgrn'] = np.ascontiguousarray(inp['state_hgrn'][0, 4 * c:4 * c + 4])
    m['st_gla'] = np.ascontiguousarray(inp['state_gla'][0, 4 * c:4 * c + 4])
    m['ptab'] = np.ascontiguousarray(inp['page_table'][4 * c:4 * c + 4, :NPG]).astype(np.int32)
    pool = inp['cache_fox_k'].shape[1]
    m['cache_k'] = np.asarray(inp['cache_fox_k'][0]).reshape(pool * 128, 512)
    m['cache_v'] = np.asarray(inp['cache_fox_v'][0]).reshape(pool * 128, 512)
    m['cache_lf'] = np.asarray(inp['cache_fox_logf'][0]).reshape(pool, 512)
    return m


def assemble(results, L):
    f = lambda a: np.asarray(a, np.float32)
    idx = np.concatenate([np.arange(64 * j, 64 * j + 4) for j in range(4)])
    pc = [0, 4]
    y_p = np.stack([f(results[c]['y_p']) for c in pc])
    y_s = np.concatenate([f(results[c]['y_s'])[idx] for c in range(8)]).reshape(32, 4, 1024)
    npg = L // 128
    fk_p = np.stack([f(results[c]['fk_p']) for c in pc]).reshape(1, 2, npg, 128, 4, 128)
    fv_p = np.stack([f(results[c]['fv_p']) for c in pc]).reshape(1, 2, npg, 128, 4, 128)
    flf_p = np.stack([f(results[c]['flf_p']) for c in pc]).reshape(1, 2, npg, 128, 4)
    hg_p = np.stack([f(results[c]['hg_p']) for c in pc])[None]
    gl_p = np.stack([f(results[c]['gl_p']) for c in pc])[None]
    fk_s = np.concatenate([f(results[c]['fk_s'])[idx] for c in range(8)]).reshape(1, 32, 4, 4, 128)
    fv_s = np.concatenate([f(results[c]['fv_s'])[idx] for c in range(8)]).reshape(1, 32, 4, 4, 128)
    flf_s = np.concatenate([f(results[c]['flf_s'])[idx] for c in range(8)]).reshape(1, 32, 4, 4)
    hg_s = np.concatenate([f(results[c]['hg_s']) for c in range(8)])[None]
    gl_s = np.concatenate([f(results[c]['gl_s']) for c in range(8)])[None]
    return (y_p, y_s, fk_p, fv_p, flf_p, hg_p, gl_p, fk_s, fv_s, flf_s, hg_s, gl_s)


def kernel(**inputs):
    inp = {k_: np.asarray(v) for k_, v in inputs.items()}
    L = inp['x_prompt'].shape[1]
    NPG = inp['page_table'].shape[1]
    POOL = inp['cache_fox_k'].shape[1]
    nc = build(L, NPG, POOL, dbg=False, phases=(1, 2, 3))
    maps = [core_inputs(inp, c, L, NPG) for c in range(8)]
    res = run_bass_kernel_spmd(nc, maps, core_ids=list(range(8)))
    return assemble(res.results, L)
```

```python
import numpy as np
from concourse.bass_utils import run_bass_kernel_spmd
from contextlib import ExitStack
import concourse.bass as bass
import concourse.mybir as mybir

F32 = mybir.dt.float32
BF16 = mybir.dt.bfloat16
I32 = mybir.dt.int32
U32 = mybir.dt.uint32
AF = mybir.ActivationFunctionType
ALU = mybir.AluOpType
AX = mybir.AxisListType


class Buf:
    __slots__ = ("w", "r", "name", "excl")

    def __init__(self, name=""):
        self.excl = False
        self.w = None
        self.r = {}
        self.name = name


class Eng:
    def __init__(self, name):
        self.name = name
        self.prog = []
        self.count = 0
        self.waited = {}


NDS = 12


class Rec:
    COMPUTE = ("pe", "act", "dve", "pool")

    def __init__(self, nc, stack):
        self.nc = nc
        self.e = {n: Eng(n) for n in ("pe", "act", "dve", "pool", "sp")}
        self.sems = {}
        for n in self.COMPUTE:
            self.sems[n] = stack.enter_context(nc.semaphore("s_" + n))
        self.dq = {}
        for q in ("sp", "pool", "act"):
            sl = []
            for i in range(NDS):
                key = "d_%s_%d" % (q, i)
                self.sems[key] = stack.enter_context(nc.semaphore(key))
                sl.append(key)
            self.dq[q] = dict(slots=sl, uses=[0] * NDS, n=0)
        self.final = []

    def _need(self, eng, deps):
        for k, v in deps.items():
            if eng.waited.get(k, 0) < v:
                eng.waited[k] = v
                eng.prog.append(("wait", k, v))

    def _collect(self, ename, R, W):
        deps = {}

        def add(d, kind):
            if d is None:
                return
            k, v, en = d
            if en == ename and ename in self.COMPUTE:
                if ename == "pe":
                    return
                if kind == "war":
                    return
            if deps.get(k, 0) < v:
                deps[k] = v

        for b in R:
            add(b.w, "raw")
        for b in W:
            add(b.w, "waw")
            for k, (v, en) in b.r.items():
                add((k, v, en), "war")
        return deps

    def _mark(self, tok, R, W):
        k, v, en = tok
        for b in R:
            b.r[k] = (v, en)
        for b in W:
            b.w = tok
            b.r = {}

    def op(self, ename, fn, R=(), W=()):
        eng = self.e[ename]
        deps = self._collect(ename, R, W)
        for b in R:
            if b.excl:
                for k2, (v2, en2) in b.r.items():
                    if en2 != ename and deps.get(k2, 0) < v2:
                        deps[k2] = v2
        self._need(eng, deps)
        eng.count += 1
        eng.prog.append(("op", fn, ename, 1))
        self._mark((ename, eng.count, ename), R, W)

    def dma(self, q, fn, R=(), W=(), final=False):
        eng = self.e[q]
        dq = self.dq[q]
        slot = dq["n"] % NDS
        dq["n"] += 1
        key = dq["slots"][slot]
        if dq["uses"][slot] > 0:
            self._need(eng, {key: 16 * dq["uses"][slot]})
        dq["uses"][slot] += 1
        val = 16 * dq["uses"][slot]
        deps = self._collect("dma_" + q, R, W)
        self._need(eng, deps)
        eng.prog.append(("op", fn, key, 16))
        self._mark((key, val, "dma_" + q), R, W)
        if final:
            self.final.append((key, val))

    def finish(self):
        eng = self.e["sp"]
        last = {}
        for k, v in self.final:
            last[k] = max(last.get(k, 0), v)
        for k, v in last.items():
            eng.prog.append(("wait", k, v))
        for q in ("pool", "act"):
            dq = self.dq[q]
            for i, u in enumerate(dq["uses"]):
                if u:
                    self.e[q].prog.append(("wait", dq["slots"][i], 16 * u))

    def replay(self, block):
        nc = self.nc
        sems = self.sems

        def run(engobj, prog):
            for it in prog:
                if it[0] == "wait":
                    engobj.wait_ge(sems[it[1]], it[2])
                else:
                    _, fn, key, inc = it
                    ins = fn(engobj)
                    ins.then_inc(sems[key], inc)

        @block.sync
        def _(e):
            run(e, self.e["sp"].prog)

        @block.tensor
        def _(e):
            run(e, self.e["pe"].prog)

        @block.scalar
        def _(e):
            run(e, self.e["act"].prog)

        @block.vector
        def _(e):
            run(e, self.e["dve"].prog)

        @block.gpsimd
        def _(e):
            run(e, self.e["pool"].prog)


import os
KSTOP = int(os.environ.get('KSTOP', '99'))
KSUB = int(os.environ.get('KSUB', '99'))

D = 1024
KC = 8
EPS = 1e-6
SCALE = 128 ** -0.5


class K:
    def __init__(self, L, NPG, POOL, dbg=False):
        self.L, self.NPG, self.POOL, self.dbg = L, NPG, POOL, dbg
        self.nc = bass.Bass("TRN2", target_bir_lowering=False)
        self.B = {}

    def buf(self, name):
        if name not in self.B:
            self.B[name] = Buf(name)
        return self.B[name]

    def din(self, name, shape, dt=F32):
        return self.nc.dram_tensor(name, list(shape), dt, kind="ExternalInput").ap()

    def dout(self, name, shape, dt=F32):
        return self.nc.dram_tensor(name, list(shape), dt, kind="ExternalOutput").ap()

    def dscr(self, name, shape, dt):
        return self.nc.dram_tensor(name, list(shape), dt, kind="Internal").ap()


def barrier(rec):
    tgt = {}
    for n in Rec.COMPUTE:
        if rec.e[n].count:
            tgt[n] = rec.e[n].count
    for q, dq in rec.dq.items():
        for i, u in enumerate(dq["uses"]):
            if u:
                tgt[dq["slots"][i]] = 16 * u
    for n in ("pe", "act", "dve", "pool", "sp"):
        d = {k: v for k, v in tgt.items() if k != n}
        rec._need(rec.e[n], d)


def flush(k, rec):
    barrier(rec)
    with k.nc.Block() as block:
        rec.replay(block)
    for e in rec.e.values():
        e.prog = []


def build(L, NPG, POOL, dbg=False, phases=(1, 2, 3)):
    k = K(L, NPG, POOL, dbg)
    nc = k.nc
    NG = L // 512
    NT = L // 128
    buf = k.buf
    xp = k.din("xp", [L, D])
    xs = k.din("xs", [256, D])
    w_in_e = k.din("w_in_even", [D, 4100])
    w_out_e = k.din("w_out_even", [D, D])
    w_in_o = k.din("w_in_odd", [D, 3088])
    w_out_o = k.din("w_out_odd", [D, D])
    w_gate = k.din("w_gla_gate", [16, 512])
    fnorm = k.din("fnorm", [D])
    vecs = k.din("vecs", [128, 64])
    bfox = k.din("bfox", [128, 4])
    consts = k.din("consts", [128, 1280])
    st_h = k.din("st_hgrn", [4, 4, 128, 128])
    st_g = k.din("st_gla", [4, 4, 128, 256])
    ptab = k.din("ptab", [4, NPG], I32)
    ckv = k.din("cache_kv", [POOL * 128, 1024])
    clf = k.din("cache_lf", [POOL, 512])

    y_p = k.dout("y_p", [L, D])
    y_s = k.dout("y_s", [256, D])
    fk_p = k.dout("fk_p", [L, 512])
    fv_p = k.dout("fv_p", [L, 512])
    flf_p = k.dout("flf_p", [L, 4])
    hg_p = k.dout("hg_p", [4, 128, 128])
    gl_p = k.dout("gl_p", [4, 128, 256])
    fk_s = k.dout("fk_s", [256, 512])
    fv_s = k.dout("fv_s", [256, 512])
    flf_s = k.dout("flf_s", [256, 4])
    hg_s = k.dout("hg_s", [4, 4, 128, 128])
    gl_s = k.dout("gl_s", [4, 4, 128, 256])

    LT = L + 256
    qbT = k.dscr("qbT", [4, 128, LT], BF16)
    kbT = k.dscr("kbT", [4, 128, LT], BF16)
    sgbT = k.dscr("sgbT", [4, 128, LT], BF16)
    vbs = k.dscr("vbs", [LT, 512], BF16)
    negc = k.dscr("negc", [LT, 4], F32)
    oaT = k.dscr("oaT", [4, 128, LT], BF16)
    obT = k.dscr("obT", [4, 128, LT], BF16)

    with ExitStack() as st:
        rec = Rec(nc, st)
        sb = lambda n, s, d: st.enter_context(nc.sbuf_tensor(n, s, d))
        ps = lambda n, s, d: st.enter_context(nc.psum_tensor(n, s, d))
        pA = [ps("pA%d" % i, [128, 512], F32) for i in range(2)]
        pT = ps("pT", [128, 1024], BF16)
        pS = ps("pS", [128, 512], F32)
        pP = ps("pP", [128, 512], F32)
        pO = ps("pO", [128, 512], F32)
        pN = ps("pN", [128, 512], F32)
        pX = ps("pX", [128, 512], F32)
        for nm in ("pA0", "pA1", "pT", "pS", "pP", "pO", "pN", "pX"):
            buf(nm).excl = True
        cst = sb("cst", [128, 1280], F32)
        vec = sb("vec", [128, 64], F32)
        bfx = sb("bfx", [128, 4], F32)
        idb = sb("idb", [128, 128], BF16)
        onesb = sb("onesb", [128, 128], BF16)
        m64 = sb("m64", [128, 64], F32)
        lbt = sb("lbt", [128, 4], F32)
        omlt = sb("omlt", [128, 4], F32)
        nomlt = sb("nomlt", [128, 4], F32)
        rec.dma("sp", lambda e: e.dma_start(out=cst[:], in_=consts), W=[buf("cst")])
        rec.dma("sp", lambda e: e.dma_start(out=vec[:], in_=vecs), W=[buf("vec")])
        rec.dma("sp", lambda e: e.dma_start(out=bfx[:], in_=bfox), W=[buf("bfx")])
        ident = cst[:, 0:128]
        tri = cst[:, 128:256]
        ones = cst[:, 320:448]
        bt32 = cst[:, 448:576]
        rst64 = cst[:, 576:1088]
        rst32 = cst[:, 1088:1216]
        rec.op("dve", lambda e: e.tensor_copy(out=idb[:], in_=ident), R=[buf("cst")], W=[buf("idb")])
        rec.op("dve", lambda e: e.tensor_copy(out=onesb[:], in_=ones), R=[buf("cst")], W=[buf("onesb")])
        rec.op("dve", lambda e: e.tensor_copy(out=m64[:], in_=cst[:, 256:320]), R=[buf("cst")], W=[buf("m64")])
        rec.op("dve", lambda e: e.tensor_sub(out=lbt[:], in0=vec[:, 24:28], in1=vec[:, 28:32]), R=[buf("vec")], W=[buf("lbt")])
        rec.op("act", lambda e: e.activation(out=lbt[:], in_=lbt[:], func=AF.Sigmoid), R=[buf("lbt")], W=[buf("lbt")])
        rec.op("dve", lambda e: e.tensor_scalar(out=omlt[:], in0=lbt[:], scalar1=-1.0, scalar2=1.0, op0=ALU.mult, op1=ALU.add),
               R=[buf("lbt")], W=[buf("omlt")])
        rec.op("dve", lambda e: e.tensor_scalar(out=nomlt[:], in0=omlt[:], scalar1=-1.0, scalar2=None, op0=ALU.mult),
               R=[buf("omlt")], W=[buf("nomlt")])

        G = dict(k=k, rec=rec, nc=nc, st=st, pA=pA, pT=pT, pS=pS, pP=pP, pO=pO, pN=pN, pX=pX, cst=cst, vec=vec, bfx=bfx,
                 idb=idb, onesb=onesb, m64=m64, lbt=lbt, omlt=omlt, nomlt=nomlt, ident=ident, tri=tri, ones=ones, bt32=bt32,
                 rst64=rst64)
        G.update(locals())
        if 1 in phases:
            phase1(G)
        if 2 in phases:
            phase2(G)
        if 3 in phases:
            phase3(G)
        if dbg:
            dbg_ob = k.dout("dbg_obT", [4, 128, LT], BF16)
            if 2 in phases:
                rec.dma("sp", lambda e: e.dma_start(out=dbg_ob, in_=obT), R=[buf("obT_d")], final=True)
            dbg_oa = k.dout("dbg_oaT", [4, 128, LT], BF16)
            rec.dma("sp", lambda e: e.dma_start(out=dbg_oa, in_=oaT), R=[buf("oaT_d")], final=True)
            dbg_nc = k.dout("dbg_negc", [LT, 4], F32)
            rec.dma("sp", lambda e: e.dma_start(out=dbg_nc, in_=negc), R=[buf("negc_d")], final=True)
        rec.finish()
        flush(k, rec)
    return nc


def load_weight_bf16(G, ph, wdram, ncols, gaincol, name):
    k, rec, nc, vec = G["k"], G["rec"], G["nc"], G["vec"]
    buf = k.buf
    W = ph.enter_context(nc.sbuf_tensor(name, [128, KC, ncols], BF16))
    with ExitStack() as tmp:
        stg = [tmp.enter_context(nc.sbuf_tensor(name + "_stg%d" % i, [128, ncols], F32)) for i in range(2)]
        wv = wdram.rearrange("(kc p) n -> p kc n", p=128)
        for kc in range(KC):
            s = stg[kc % 2]
            sbuf = buf(name + "_stg%d" % (kc % 2))
            rec.dma("sp", lambda e, s=s, kc=kc: e.dma_start(out=s[:], in_=wv[:, kc, :]), W=[sbuf])
            hc = (ncols // 2 + 3) // 4 * 4
            if gaincol is None:
                rec.op("dve", lambda e, s=s, kc=kc: e.tensor_copy(out=W[:, kc, 0:hc], in_=s[:, 0:hc]), R=[sbuf], W=[buf(name)])
                rec.op("act", lambda e, s=s, kc=kc: e.activation(out=W[:, kc, hc:ncols], in_=s[:, hc:ncols], func=AF.Copy), R=[sbuf], W=[buf(name + "_b")])
            else:
                gc = vec[:, gaincol + kc:gaincol + kc + 1]
                rec.op("dve", lambda e, s=s, kc=kc, gc=gc: e.tensor_scalar(out=W[:, kc, 0:hc], in0=s[:, 0:hc], scalar1=gc, scalar2=None, op0=ALU.mult), R=[sbuf, buf("vec")], W=[buf(name)])
                rec.op("act", lambda e, s=s, kc=kc, gc=gc: e.activation(out=W[:, kc, hc:ncols], in_=s[:, hc:ncols], func=AF.Copy, scale=gc), R=[sbuf, buf("vec")], W=[buf(name + "_b")])
        flush(k, rec)
    return W


def norm_and_transpose(G, xt, xtb, rstd_col, hb, hbb, hT, hTb, tcol, gain_in_w=True):
    rec, pT, idb = G["rec"], G["pT"], G["idb"]
    buf = G["k"].buf
    junk, ss = G["junk"], G["ss"]
    rec.op("act", lambda e: e.activation(out=junk[:], in_=xt[:], func=AF.Square, accum_out=ss[:, rstd_col:rstd_col + 1]),
           R=[xtb], W=[buf("junk"), buf("ss")])
    rec.op("act", lambda e: e.activation(out=ss[:, rstd_col:rstd_col + 1], in_=ss[:, rstd_col:rstd_col + 1], func=AF.Ln, scale=1.0 / D, bias=G["epsb"][:, 0:1]),
           R=[buf("ss"), buf("epsb")], W=[buf("ss")])
    rec.op("act", lambda e: e.activation(out=ss[:, rstd_col:rstd_col + 1], in_=ss[:, rstd_col:rstd_col + 1], func=AF.Exp, scale=-0.5),
           R=[buf("ss")], W=[buf("ss")])
    rec.op("dve", lambda e: e.tensor_scalar(out=hb[:], in0=xt[:], scalar1=ss[:, rstd_col:rstd_col + 1], scalar2=None, op0=ALU.mult),
           R=[xtb, buf("ss")], W=[hbb])
    for kc in range(KC):
        rec.op("pe", lambda e, kc=kc: e.transpose(out=pT[:, kc * 128:(kc + 1) * 128], in_=hb[:, kc * 128:(kc + 1) * 128], identity=idb[:]),
               R=[hbb, buf("idb")], W=[buf("pT")])
    rec.op("act", lambda e: e.activation(out=hT[:, :, tcol:tcol + 128], in_=pT[:].rearrange("p (kc t) -> p kc t", kc=KC), func=AF.Copy),
           R=[buf("pT")], W=[hTb])


def phase1(G):
    k, rec, nc = G["k"], G["rec"], G["nc"]
    buf = k.buf
    L = k.L
    NG = L // 512
    pA, pT, pS, pP, pO, pN, pX = G["pA"], G["pT"], G["pS"], G["pP"], G["pO"], G["pN"], G["pX"]
    vec, lbt, omlt, nomlt, m64, onesb, idb = G["vec"], G["lbt"], G["omlt"], G["nomlt"], G["m64"], G["onesb"], G["idb"]
    with ExitStack() as ph:
        sb = lambda n, s, d: ph.enter_context(nc.sbuf_tensor(n, s, d))
        W0 = load_weight_bf16(G, ph, G["w_in_e"], 4100, 0, "W0")
        G["junk"] = sb("junk", [128, 1024], BF16)
        G["ss"] = sb("ss", [128, 8], F32)
        G["epsb"] = sb("epsb", [128, 1], F32)
        rec.op("pool", lambda e: e.memset(G["epsb"][:], EPS), W=[buf("epsb")])
        xt = [sb("xt%d" % i, [128, D], F32) for i in range(3)]
        hb = [sb("hb%d" % i, [128, D], BF16) for i in range(2)]
        hT = sb("hT", [128, KC, 512], BF16)
        tmp = {n: sb("t_" + n, [128, 512], F32) for n in ("sig", "f", "omf", "lf", "b", "d", "eq", "ek")}
        qt = sb("qt", [128, 4, 512], BF16)
        kt = sb("kt", [128, 4, 512], BF16)
        sm = sb("sm", [128, 4, 3, 8], F32)
        vtok = sb("vtok", [128, 4, 512], BF16)
        ktok = sb("ktok", [128, 4, 4, 128], BF16)
        AT = [sb("AT%d" % i, [128, 4, 64], BF16) for i in range(2)]
        S = sb("S", [128, 4, 128], F32)
        Sall = [sb("Sall%d" % i, [128, 8, 128], BF16) for i in range(2)]
        kt2 = sb("kt2", [128, 512], BF16)
        oT = sb("oT", [128, 4, 512], F32)
        sq = sb("sq", [128, 512], BF16)
        lnt = sb("lnt", [128, 512], F32)
        sg = sb("sg", [128, 4, 512], BF16)
        tg = sb("tg", [128, 512], F32)
        oa = sb("oa", [128, 4, 512], BF16)
        qb = sb("qb", [128, 4, 512], BF16)
        kb = sb("kb", [128, 4, 512], BF16)
        sgb = sb("sgb", [128, 4, 512], BF16)
        ktm = [sb("ktm%d" % i, [128, 512], F32) for i in range(2)]
        vtm = [sb("vtm%d" % i, [128, 512], F32) for i in range(2)]
        vbf = sb("vbf", [128, 4, 512], BF16)
        lfb = sb("lfb", [128, 4, 4], F32)
        ncb = sb("ncb", [128, 4, 4], F32)
        tot = sb("tot", [128, 4], F32)
        rec.op("pool", lambda e: e.memset(tot[:], 0.0), W=[buf("tot")])
        rec.op("pool", lambda e: e.memset(S[:], 0.0), W=[buf("S%d" % h) for h in range(4)])

        pa_i = [0]

        def fm(col, N, hTN):
            p = pA[pa_i[0] % 2]
            pb = buf("pA%d" % (pa_i[0] % 2))
            pa_i[0] += 1
            for kc in range(KC):
                rec.op("pe", lambda e, kc=kc, p=p: e.matmul(out=p[:, 0:N], lhsT=W0[:, kc, col:col + 128], rhs=hTN[:, kc, 0:N],
                                                           start=(kc == 0), stop=(kc == KC - 1)),
                       R=[buf("W0"), buf("hT")], W=[pb])
            return p, pb

        def tm(col, ncols, t):
            p = pA[pa_i[0] % 2]
            pb = buf("pA%d" % (pa_i[0] % 2))
            pa_i[0] += 1
            for kc in range(KC):
                rec.op("pe", lambda e, kc=kc, p=p: e.matmul(out=p[:, 0:ncols], lhsT=hT[:, kc, t * 128:(t + 1) * 128], rhs=W0[:, kc, col:col + ncols],
                                                           start=(kc == 0), stop=(kc == KC - 1)),
                       R=[buf("W0"), buf("hT")], W=[pb])
            return p, pb

        groups = [(g * 512, 512, False) for g in range(NG)] + [(L, 256, True)]
        xi = [0]
        def do_group(t0, N, samp):
            ntile = N // 128
            C = 64
            nch = N // C
            mid = 1 if samp else 31
            last = 3 if samp else C - 1
            for t in range(ntile):
                x_ = xt[xi[0] % 3]
                xb_ = buf("xt%d" % (xi[0] % 3))
                h_ = hb[xi[0] % 2]
                hb_ = buf("hb%d" % (xi[0] % 2))
                xi[0] += 1
                src = G["xs"][t * 128:(t + 1) * 128, :] if samp else G["xp"][t0 + t * 128:t0 + (t + 1) * 128, :]
                rec.dma("sp", lambda e, x_=x_, src=src: e.dma_start(out=x_[:], in_=src), W=[xb_])
                norm_and_transpose(G, x_, xb_, t, h_, hb_, hT, buf("hT"), t * 128)
            if KSTOP < 2:
                return
            for t in range(ntile):
                r0 = t0 + t * 128
                p, pb = tm(1024, 512, t)
                rec.op("act", lambda e, p=p, t=t: e.activation(out=vtok[:, t, :], in_=p[:, 0:512], func=AF.Copy), R=[pb], W=[buf("vtok")])
                p, pb = tm(2560, 512, t)
                kk = ktm[t % 2]
                kkb = buf("ktm%d" % (t % 2))
                rec.op("act", lambda e, p=p, kk=kk: e.activation(out=kk[:], in_=p[:, 0:512], func=AF.Copy), R=[pb], W=[kkb])
                dst = G["fk_s"][t * 128:(t + 1) * 128, :] if samp else G["fk_p"][r0:r0 + 128, :]
                rec.dma("pool", lambda e, kk=kk, dst=dst: e.dma_start(out=dst, in_=kk[:]), R=[kkb], final=True)
                p, pb = tm(3072, 512, t)
                vv = vtm[t % 2]
                vvb = buf("vtm%d" % (t % 2))
                rec.op("act", lambda e, p=p, vv=vv: e.activation(out=vv[:], in_=p[:, 0:512], func=AF.Copy), R=[pb], W=[vvb])
                rec.op("dve", lambda e, p=p, t=t: e.tensor_copy(out=vbf[:, t, :], in_=p[:, 0:512]), R=[pb], W=[buf("vbf")])
                dst = G["fv_s"][t * 128:(t + 1) * 128, :] if samp else G["fv_p"][r0:r0 + 128, :]
                rec.dma("pool", lambda e, vv=vv, dst=dst: e.dma_start(out=dst, in_=vv[:]), R=[vvb], final=True)
                if KSUB < 1:
                    continue
                p, pb = tm(4096, 4, t)
                rec.op("dve", lambda e, p=p, t=t: e.tensor_tensor(out=lfb[:, t, :], in0=p[:, 0:4], in1=G["bfx"][:], op=ALU.add),
                       R=[pb, buf("bfx")], W=[buf("lfb")])
                rec.op("act", lambda e, t=t: e.activation(out=lfb[:, t, :], in_=lfb[:, t, :], func=AF.Sigmoid), R=[buf("lfb")], W=[buf("lfb")])
                rec.op("act", lambda e, t=t: e.activation(out=lfb[:, t, :], in_=lfb[:, t, :], func=AF.Ln), R=[buf("lfb")], W=[buf("lfb")])
                if KSUB < 2:
                    continue
                trim = G["bt32"] if samp else G["tri"]
                rec.op("pe", lambda e, t=t, trim=trim: e.matmul(out=pX[:, 0:4], lhsT=trim, rhs=lfb[:, t, :], start=True, stop=True),
                       R=[buf("cst"), buf("lfb")], W=[buf("pX")])
                if samp:
                    rec.op("dve", lambda e, t=t: e.tensor_scalar(out=ncb[:, t, :], in0=pX[:, 0:4], scalar1=-1.0, scalar2=None, op0=ALU.mult),
                           R=[buf("pX")], W=[buf("ncb")])
                else:
                    rec.op("dve", lambda e, t=t: e.scalar_tensor_tensor(out=ncb[:, t, :], in0=pX[:, 0:4], scalar=-1.0, in1=tot[:], op0=ALU.mult, op1=ALU.subtract),
                           R=[buf("pX"), buf("tot")], W=[buf("ncb")])
                    rec.op("pe", lambda e, t=t: e.matmul(out=pX[:, 8:12], lhsT=G["ones"], rhs=lfb[:, t, :], start=True, stop=True),
                           R=[buf("cst"), buf("lfb")], W=[buf("pX")])
                    rec.op("dve", lambda e: e.tensor_tensor(out=tot[:], in0=tot[:], in1=pX[:, 8:12], op=ALU.add),
                           R=[buf("pX"), buf("tot")], W=[buf("tot")])
            if KSUB < 3:
                return
            dst = (G["flf_s"] if samp else G["flf_p"][t0:t0 + N, :]).rearrange("(t p) h -> p t h", p=128)
            rec.dma("pool", lambda e, dst=dst: e.dma_start(out=dst, in_=lfb[:, 0:ntile, :]), R=[buf("lfb")], final=True)
            rec.dma("pool", lambda e: e.dma_start(out=G["negc"][t0:t0 + N, :].rearrange("(t p) h -> p t h", p=128), in_=ncb[:, 0:ntile, :]),
                    R=[buf("ncb")], W=[buf("negc_d")])
            rec.dma("pool", lambda e: e.dma_start(out=G["vbs"][t0:t0 + N, :].rearrange("(t p) c -> p t c", p=128), in_=vbf[:, 0:ntile, :]),
                    R=[buf("vbf")], W=[buf("vbs_d")])
            if KSTOP < 3:
                return
            for h in range(4):
                p, pb = fm(2048 + 128 * h, N, hT)
                rec.op("act", lambda e, p=p, h=h: e.activation(out=qb[:, h, 0:N], in_=p[:, 0:N], func=AF.Copy), R=[pb], W=[buf("qb")])
                p, pb = fm(2560 + 128 * h, N, hT)
                rec.op("dve", lambda e, p=p, h=h: e.tensor_copy(out=kb[:, h, 0:N], in_=p[:, 0:N]), R=[pb], W=[buf("kb")])
                p, pb = fm(3584 + 128 * h, N, hT)
                rec.op("act", lambda e, p=p, h=h: e.activation(out=sgb[:, h, 0:N], in_=p[:, 0:N], func=AF.Silu), R=[pb], W=[buf("sgb")])
            for (src_t, srcn, dstT) in ((qb, "qb", G["qbT"]), (kb, "kb", G["kbT"]), (sgb, "sgb", G["sgbT"])):
                rec.dma("pool", lambda e, src_t=src_t, dstT=dstT: e.dma_start(out=dstT[:, :, t0:t0 + N].rearrange("h p n -> p h n"), in_=src_t[:, :, 0:N]),
                        R=[buf(srcn)], W=[buf(srcn + "T_d")])
            if KSTOP < 4:
                return
            for h in range(4):
                T_ = tmp
                p, pb = fm(512 + 128 * h, N, hT)
                rec.op("act", lambda e, p=p: e.activation(out=T_["sig"][:, 0:N], in_=p[:, 0:N], func=AF.Sigmoid), R=[pb], W=[buf("t_sig")])
                rec.op("dve", lambda e, h=h: e.tensor_scalar(out=T_["f"][:, 0:N], in0=T_["sig"][:, 0:N], scalar1=omlt[:, h:h + 1], scalar2=lbt[:, h:h + 1],
                                                           op0=ALU.mult, op1=ALU.add), R=[buf("t_sig"), buf("omlt"), buf("lbt")], W=[buf("t_f")])
                rec.op("dve", lambda e, h=h: e.tensor_scalar(out=T_["omf"][:, 0:N], in0=T_["sig"][:, 0:N], scalar1=nomlt[:, h:h + 1], scalar2=omlt[:, h:h + 1],
                                                           op0=ALU.mult, op1=ALU.add), R=[buf("t_sig"), buf("omlt"), buf("nomlt")], W=[buf("t_omf")])
                rec.op("act", lambda e: e.activation(out=T_["lf"][:, 0:N], in_=T_["f"][:, 0:N], func=AF.Ln), R=[buf("t_f")], W=[buf("t_lf")])
                rmask = G["rst64"][:, 0:N]
                rec.op("dve", lambda e, rmask=rmask: e.tensor_tensor_scan(out=T_["b"][:, 0:N], data0=rmask, data1=T_["lf"][:, 0:N], initial=0.0,
                                                                         op0=ALU.mult, op1=ALU.add), R=[buf("t_lf"), buf("cst")], W=[buf("t_b")])
                b3 = T_["b"][:, 0:N].rearrange("p (c k) -> p c k", k=C)
                d3 = T_["d"][:, 0:N].rearrange("p (c k) -> p c k", k=C)
                rec.op("dve", lambda e, b3=b3, d3=d3: e.tensor_tensor(out=d3, in0=b3, in1=b3[:, :, mid:mid + 1].to_broadcast([128, nch, C]), op=ALU.subtract),
                       R=[buf("t_b")], W=[buf("t_d")])
                rec.op("act", lambda e: e.activation(out=T_["eq"][:, 0:N], in_=T_["d"][:, 0:N], func=AF.Exp), R=[buf("t_d")], W=[buf("t_eq")])
                rec.op("act", lambda e: e.activation(out=T_["ek"][:, 0:N], in_=T_["d"][:, 0:N], func=AF.Exp, scale=-1.0), R=[buf("t_d")], W=[buf("t_ek")])
                rec.op("act", lambda e, h=h, b3=b3: e.activation(out=sm[:, h, 0, 0:nch], in_=b3[:, :, mid], func=AF.Exp), R=[buf("t_b")], W=[buf("sm")])
                rec.op("act", lambda e, h=h, b3=b3: e.activation(out=sm[:, h, 2, 0:nch], in_=b3[:, :, last], func=AF.Exp), R=[buf("t_b")], W=[buf("sm")])
                rec.op("act", lambda e, h=h, d3=d3: e.activation(out=sm[:, h, 1, 0:nch], in_=d3[:, :, last], func=AF.Exp), R=[buf("t_d")], W=[buf("sm")])
                p, pb = fm(0 + 128 * h, N, hT)
                rec.op("dve", lambda e, p=p, h=h: e.tensor_tensor(out=qt[:, h, 0:N], in0=p[:, 0:N], in1=T_["eq"][:, 0:N], op=ALU.mult),
                       R=[pb, buf("t_eq")], W=[buf("qt")])
                rec.op("dve", lambda e, h=h: e.tensor_tensor(out=kt[:, h, 0:N], in0=T_["omf"][:, 0:N], in1=T_["ek"][:, 0:N], op=ALU.mult),
                       R=[buf("t_omf"), buf("t_ek")], W=[buf("kt")])
                p, pb = fm(1536 + 128 * h, N, hT)
                rec.op("act", lambda e, p=p, h=h: e.activation(out=sg[:, h, 0:N], in_=p[:, 0:N], func=AF.Silu), R=[pb], W=[buf("sg")])
                rec.op("dve", lambda e, h=h: e.tensor_tensor(out=kt2[:, 0:N].rearrange("p (c k) -> p c k", k=C), in0=kt[:, h, 0:N].rearrange("p (c k) -> p c k", k=C),
                                                           in1=sm[:, h, 1, 0:nch].unsqueeze(2).to_broadcast([128, nch, C]), op=ALU.mult), R=[buf("kt"), buf("sm")], W=[buf("kt2")])
                for t in range(ntile):
                    rec.op("pe", lambda e, h=h, t=t: e.transpose(out=pT[:, h * 128:(h + 1) * 128], in_=kt2[:, t * 128:(t + 1) * 128], identity=idb[:]),
                           R=[buf("kt2"), buf("idb")], W=[buf("pT")])
                    rec.op("act", lambda e, h=h, t=t: e.activation(out=ktok[:, t, h, :], in_=pT[:, h * 128:(h + 1) * 128], func=AF.Copy),
                           R=[buf("pT")], W=[buf("ktok")])
            if KSTOP < 5:
                return
            hg_state_in = G["st_h"]

            def cinfo(c):
                r0 = (c * C) % 128
                return slice(c * C, (c + 1) * C), (c * C) // 128, slice(r0, r0 + C)

            for h in range(4):
                Sb = buf("S%d" % h)
                at = AT[h % 2]
                atb = buf("AT%d" % (h % 2))
                sal = Sall[h % 2]
                salb = buf("Sall%d" % (h % 2))
                hs = slice(h * 128, (h + 1) * 128)
                for c in range(nch):
                    cols, t, rows = cinfo(c)
                    rec.op("pe", lambda e, h=h, cols=cols, rows=rows, t=t: e.matmul(out=pS[rows, t * 64:t * 64 + C], lhsT=kt[:, h, cols], rhs=qt[:, h, cols], start=True, stop=True),
                           R=[buf("kt"), buf("qt")], W=[buf("pS")])
                if KSUB < 2:
                    continue
                rec.op("dve", lambda e, at=at: e.tensor_tensor(out=at[:, 0:ntile, :], in0=pS[:, 0:ntile * 64].rearrange("p (a b) -> p a b", b=64),
                                                              in1=m64[:, :].unsqueeze(1).to_broadcast([128, ntile, 64]), op=ALU.mult),
                       R=[buf("pS"), buf("m64")], W=[atb])
                if KSUB < 3:
                    continue
                for c in range(nch):
                    cols, t, rows = cinfo(c)
                    po_, pob_ = (pO, buf("pO")) if c % 2 == 0 else (pA[0], buf("pA0"))
                    rec.op("pe", lambda e, hs=hs, rows=rows, t=t, at=at, c=c, po_=po_: e.matmul(out=po_[:, (c // 2) * 64:(c // 2) * 64 + C], lhsT=vtok[rows, t, hs], rhs=at[rows, t, :], start=True, stop=True),
                           R=[buf("vtok"), atb], W=[pob_])
                if KSUB < 4:
                    continue
                for c in range(nch):
                    cols, t, rows = cinfo(c)
                    pb_, pbb_ = (pP, buf("pP")) if c % 2 == 0 else (pX, buf("pX"))
                    rec.op("pe", lambda e, h=h, hs=hs, rows=rows, t=t, c=c, pb_=pb_: e.matmul(out=pb_[:, (c // 2) * 128:(c // 2 + 1) * 128], lhsT=ktok[rows, t, h, :], rhs=vtok[rows, t, hs], start=True, stop=True),
                           R=[buf("ktok"), buf("vtok")], W=[pbb_])
                if KSUB < 5:
                    continue
                for c in range(nch):
                    pb_, pbb_ = (pP, buf("pP")) if c % 2 == 0 else (pX, buf("pX"))
                    if samp:
                        rec.dma("sp", lambda e, h=h, c=c: e.dma_start(out=S[:, h, :], in_=hg_state_in[c, h]), W=[Sb])
                    rec.op("dve", lambda e, h=h, c=c, sal=sal: e.tensor_scalar(out=sal[:, c, :], in0=S[:, h, :], scalar1=sm[:, h, 0, c:c + 1], scalar2=None, op0=ALU.mult),
                           R=[Sb, buf("sm")], W=[salb])
                    rec.op("dve", lambda e, h=h, c=c, pb_=pb_: e.scalar_tensor_tensor(out=S[:, h, :], in0=S[:, h, :], scalar=sm[:, h, 2, c:c + 1], in1=pb_[:, (c // 2) * 128:(c // 2 + 1) * 128], op0=ALU.mult, op1=ALU.add),
                           R=[pbb_, Sb, buf("sm")], W=[Sb])
                    if samp:
                        rec.dma("pool", lambda e, h=h, c=c: e.dma_start(out=G["hg_s"][c, h], in_=S[:, h, :]), R=[Sb], final=True)
                if KSUB < 6:
                    continue
                for c in range(nch):
                    cols, t, rows = cinfo(c)
                    rec.op("pe", lambda e, h=h, cols=cols, c=c, sal=sal: e.matmul(out=pN[:, c * 64:c * 64 + C], lhsT=sal[:, c, :], rhs=qt[:, h, cols], start=True, stop=True),
                           R=[salb, buf("qt")], W=[buf("pN")])
                if KSUB < 7:
                    continue
                o4 = oT[:, h, 0:N].rearrange("p (a two k) -> p a two k", two=2, k=64)
                rec.op("act", lambda e, o4=o4: e.activation(out=o4[:, :, 0, :], in_=pO[:, 0:N // 2].rearrange("p (a k) -> p a k", k=64), func=AF.Copy), R=[buf("pO")], W=[buf("oT")])
                rec.op("act", lambda e, o4=o4: e.activation(out=o4[:, :, 1, :], in_=pA[0][:, 0:N // 2].rearrange("p (a k) -> p a k", k=64), func=AF.Copy), R=[buf("pA0")], W=[buf("oT")])
                rec.op("dve", lambda e, h=h: e.tensor_tensor(out=oT[:, h, 0:N], in0=oT[:, h, 0:N], in1=pN[:, 0:N], op=ALU.add), R=[buf("pN"), buf("oT")], W=[buf("oT")])
            if (not samp) and t0 + N == L:
                for h in range(4):
                    rec.dma("pool", lambda e, h=h: e.dma_start(out=G["hg_p"][h], in_=S[:, h, :]), R=[buf("S%d" % h)], final=True)
            if KSTOP < 6:
                return
            for h in range(4):
                rec.op("act", lambda e, h=h: e.activation(out=sq[:, 0:N], in_=oT[:, h, 0:N], func=AF.Square), R=[buf("oT")], W=[buf("sq")])
                rec.op("pe", lambda e: e.matmul(out=pN[:, 0:N], lhsT=onesb[:], rhs=sq[:, 0:N], start=True, stop=True), R=[buf("onesb"), buf("sq")], W=[buf("pN")])
                rec.op("act", lambda e: e.activation(out=lnt[:, 0:N], in_=pN[:, 0:N], func=AF.Ln, scale=1.0 / 128, bias=G["epsb"][:, 0:1]), R=[buf("pN"), buf("epsb")], W=[buf("lnt")])
                rec.op("act", lambda e: e.activation(out=lnt[:, 0:N], in_=lnt[:, 0:N], func=AF.Exp, scale=-0.5), R=[buf("lnt")], W=[buf("lnt")])
                rec.op("dve", lambda e, h=h: e.scalar_tensor_tensor(out=tg[:, 0:N], in0=lnt[:, 0:N], scalar=vec[:, 32 + h:33 + h], in1=sg[:, h, 0:N], op0=ALU.mult, op1=ALU.mult),
                       R=[buf("lnt"), buf("vec"), buf("sg")], W=[buf("tg")])
                rec.op("dve", lambda e, h=h: e.tensor_tensor(out=oa[:, h, 0:N], in0=oT[:, h, 0:N], in1=tg[:, 0:N], op=ALU.mult), R=[buf("oT"), buf("tg")], W=[buf("oa")])
            rec.dma("pool", lambda e: e.dma_start(out=G["oaT"][:, :, t0:t0 + N].rearrange("h p n -> p h n"), in_=oa[:, :, 0:N]), R=[buf("oa")], W=[buf("oaT_d")])
        for (t0_, N_, samp_) in groups:
            if KSTOP >= 1:
                do_group(t0_, N_, samp_)
        flush(k, rec)


def phase2(G):
    k, rec, nc = G["k"], G["rec"], G["nc"]
    buf = k.buf
    L, NPG = k.L, k.NPG
    NT = L // 128
    NG = L // 512
    pS, pP, pO, pN, pX = G["pS"], G["pP"], G["pO"], G["pN"], G["pX"]
    onesb, cst = G["onesb"], G["cst"]
    KW = max(L, NPG * 128 + 128)
    NB = max(NT, NPG + 1)
    with ExitStack() as ph:
        sb = lambda n, s, d: ph.enter_context(nc.sbuf_tensor(n, s, d))
        kT = sb("kT", [128, 4, KW], BF16)
        V = sb("V", [128, NB, 512], BF16)
        ngs = sb("ngs", [128, NB, 4], F32)
        biasT = [sb("biasT%d" % i, [128, 4, NB], F32) for i in range(2)]
        qb = [sb("qb2_%d" % i, [128, 4, 512], BF16) for i in range(2)]
        sgb = [sb("sgb2_%d" % i, [128, 4, 512], BF16) for i in range(2)]
        ob = [sb("ob%d" % i, [128, 4, 512], BF16) for i in range(2)]
        PT = [sb("PT%d" % i, [128, 512], BF16) for i in range(2)]
        trib = sb("trib", [128, 128], BF16)
        m64b = sb("m64b", [128, 64], BF16)
        nq = sb("nq", [128, 4], F32)
        rl = sb("rl", [128, 512], F32)
        tg2 = sb("tg2", [128, 512], F32)
        rec.op("dve", lambda e: e.tensor_copy(out=trib[:], in_=G["tri"]), R=[buf("cst")], W=[buf("trib")])
        rec.op("dve", lambda e: e.tensor_copy(out=m64b[:], in_=cst[:, 256:320]), R=[buf("cst")], W=[buf("m64b")])
        for h in range(4):
            rec.dma("sp", lambda e, h=h: e.dma_start(out=kT[:, h, 0:L], in_=G["kbT"][h, :, 0:L]), R=[buf("kbT_d")], W=[buf("kT")])
        rec.dma("sp", lambda e: e.dma_start(out=V[:, 0:NT, :], in_=G["vbs"][0:L, :].rearrange("(t p) c -> p t c", p=128)), R=[buf("vbs_d")], W=[buf("V")])
        rec.dma("sp", lambda e: e.dma_start(out=ngs[:, 0:NT, :], in_=G["negc"][0:L, :].rearrange("(t p) h -> p t h", p=128)), R=[buf("negc_d")], W=[buf("ngs")])

        def attend(h, qt_, qtb, NQ, blist, bT, bTb):
            nbk = len(blist)
            for bi, (kc0, ks, vt, bj, c0, diag) in enumerate(blist):
                n = NQ - c0
                pb = (pS, pP)[bi % 2]
                pbb = buf(("pS", "pP")[bi % 2])
                P_ = PT[bi % 2]
                Pb = buf("PT%d" % (bi % 2))
                rec.op("pe", lambda e, pb=pb, kc0=kc0, ks=ks, c0=c0, n=n: e.matmul(out=pb[0:ks, 0:n], lhsT=kT[:, h, kc0:kc0 + ks], rhs=qt_[:, h, c0:NQ], start=True, stop=True),
                       R=[buf("kT"), qtb], W=[pbb])
                rec.op("act", lambda e, pb=pb, P_=P_, ks=ks, n=n, bj=bj: e.activation(out=P_[0:ks, 0:n], in_=pb[0:ks, 0:n], func=AF.Exp, scale=SCALE, bias=bT[0:ks, h, bj:bj + 1]),
                       R=[pbb, bTb], W=[Pb])
                if diag:
                    mk = trib if ks == 128 else m64b
                    rec.op("dve", lambda e, P_=P_, ks=ks, mk=mk: e.tensor_tensor(out=P_[0:ks, 0:ks], in0=P_[0:ks, 0:ks], in1=mk[0:ks, 0:ks], op=ALU.mult),
                           R=[Pb, buf("trib"), buf("m64b")], W=[Pb])
                rec.op("pe", lambda e, P_=P_, ks=ks, vt=vt, c0=c0, n=n, bi=bi: e.matmul(out=pO[:, c0:NQ], lhsT=V[0:ks, vt, h * 128:(h + 1) * 128], rhs=P_[0:ks, 0:n], start=(bi == 0), stop=(bi == nbk - 1)),
                       R=[buf("V"), Pb], W=[buf("pO")])
                rec.op("pe", lambda e, P_=P_, ks=ks, c0=c0, n=n, bi=bi: e.matmul(out=pN[:, c0:NQ], lhsT=onesb[0:ks, :], rhs=P_[0:ks, 0:n], start=(bi == 0), stop=(bi == nbk - 1)),
                       R=[buf("onesb"), Pb], W=[buf("pN")])

        def epilogue(h, NQ, sg_, sgbb, ob_, obb):
            rec.op("dve", lambda e: e.reciprocal(out=rl[:, 0:NQ], in_=pN[:, 0:NQ]), R=[buf("pN")], W=[buf("rl")])
            rec.op("dve", lambda e: e.tensor_tensor(out=tg2[:, 0:NQ], in0=rl[:, 0:NQ], in1=sg_[:, h, 0:NQ], op=ALU.mult), R=[buf("rl"), sgbb], W=[buf("tg2")])
            rec.op("dve", lambda e: e.tensor_tensor(out=ob_[:, h, 0:NQ], in0=pO[:, 0:NQ], in1=tg2[:, 0:NQ], op=ALU.mult), R=[buf("pO"), buf("tg2")], W=[obb])

        def load_q(i, c0, n):
            q_, qbb = qb[i % 2], buf("qb2_%d" % (i % 2))
            s_, sbb = sgb[i % 2], buf("sgb2_%d" % (i % 2))
            rec.dma("sp", lambda e: e.dma_start(out=q_[:, :, 0:n], in_=G["qbT"][:, :, c0:c0 + n].rearrange("h p n -> p h n")), R=[buf("qbT_d")], W=[qbb])
            rec.dma("sp", lambda e: e.dma_start(out=s_[:, :, 0:n], in_=G["sgbT"][:, :, c0:c0 + n].rearrange("h p n -> p h n")), R=[buf("sgbT_d")], W=[sbb])
            return q_, qbb, s_, sbb

        def prompt_group(I):
            t0 = 512 * I
            q_, qbb, s_, sbb = load_q(I, t0, 512)
            o_, obb = ob[I % 2], buf("ob%d" % (I % 2))
            bT, bTb = biasT[I % 2], buf("biasT%d" % (I % 2))
            nb = 4 * I + 4
            rec.op("pe", lambda e: e.matmul(out=pX[:, 0:4], lhsT=G["ones"][0:1, :], rhs=ngs[0:1, 4 * I, :], start=True, stop=True), R=[buf("cst"), buf("ngs")], W=[buf("pX")])
            rec.op("act", lambda e: e.activation(out=nq[:], in_=pX[:, 0:4], func=AF.Copy), R=[buf("pX")], W=[buf("nq")])
            for h in range(4):
                rec.op("dve", lambda e, h=h: e.tensor_scalar(out=bT[:, h, 0:nb], in0=ngs[:, 0:nb, h], scalar1=nq[:, h:h + 1], scalar2=None, op0=ALU.subtract),
                       R=[buf("ngs"), buf("nq")], W=[bTb])
            for h in range(4):
                blist = []
                for j in range(nb):
                    r = j - 4 * I
                    blist.append((128 * j, 128, j, j, 128 * max(r, 0), r >= 0))
                attend(h, q_, qbb, 512, blist, bT, bTb)
                epilogue(h, 512, s_, sbb, o_, obb)
            rec.dma("pool", lambda e: e.dma_start(out=G["obT"][:, :, t0:t0 + 512].rearrange("h p n -> p h n"), in_=o_[:, :, :]), R=[obb], W=[buf("obT_d")])

        for I in range(NG):
            prompt_group(I)

        pti = sb("pti", [128, NPG], I32)
        ptc = sb("ptc", [128, 1], I32)
        idxf = sb("idxf", [128, NPG], F32)
        idx = sb("idx", [128, NPG], I32)
        lfp = sb("lfp", [128, 512], F32)
        cum = sb("cum", [128, 4, 128], F32)
        T4 = sb("T4", [128, 4], F32)
        TR = sb("TR", [128, 4], F32)
        sufp = sb("sufp", [128, 4, 128], F32)
        kst = [sb("kst%d" % i, [128, 1024], F32) for i in range(4)]
        iot = cst[:, 1216:1217]
        ustr = cst[:, 1088:1216]

        def sample_seq(j):
            c0 = L + 64 * j
            rec.dma("sp", lambda e: e.dma_start(out=pti[:], in_=G["ptab"][j].partition_broadcast(128)), W=[buf("pti")])
            rec.dma("sp", lambda e: e.dma_start(out=ptc[0:NPG, :], in_=G["ptab"][j].rearrange("(n o) -> n o", o=1)), W=[buf("ptc")])
            rec.op("dve", lambda e: e.tensor_scalar(out=idxf[:], in0=pti[:], scalar1=128.0, scalar2=iot, op0=ALU.mult, op1=ALU.add), R=[buf("pti"), buf("cst")], W=[buf("idxf")])
            rec.op("dve", lambda e: e.tensor_copy(out=idx[:], in_=idxf[:]), R=[buf("idxf")], W=[buf("idx")])
            rec.dma("pool", lambda e: e.indirect_dma_start(out=lfp[0:NPG, :], out_offset=None, in_=G["clf"], in_offset=bass.IndirectOffsetOnAxis(ap=ptc[0:NPG, 0:1], axis=0)),
                    R=[buf("ptc")], W=[buf("lfp")])
            lf3 = lfp[0:NPG, :].rearrange("p (s h) -> p h s", h=4)
            bT, bTb = biasT[j % 2], buf("biasT%d" % (j % 2))
            for h in range(4):
                rec.op("dve", lambda e, h=h: e.tensor_tensor_scan(out=cum[0:NPG, h, :], data0=G["ones"][0:NPG, :], data1=lf3[:, h, :], initial=0.0, op0=ALU.mult, op1=ALU.add),
                       R=[buf("lfp"), buf("cst")], W=[buf("cum")])
            rec.op("dve", lambda e: e.tensor_copy(out=T4[0:NPG, :], in_=cum[0:NPG, :, 127]), R=[buf("cum")], W=[buf("T4")])
            rec.op("pe", lambda e: e.matmul(out=pX[0:NPG, 0:4], lhsT=ustr[0:NPG, 0:NPG], rhs=T4[0:NPG, :], start=True, stop=True), R=[buf("cst"), buf("T4")], W=[buf("pX")])
            rec.op("dve", lambda e: e.tensor_tensor(out=TR[0:NPG, :], in0=pX[0:NPG, 0:4], in1=T4[0:NPG, :], op=ALU.add), R=[buf("pX"), buf("T4")], W=[buf("TR")])
            for h in range(4):
                rec.op("dve", lambda e, h=h: e.tensor_scalar(out=sufp[0:NPG, h, :], in0=cum[0:NPG, h, :], scalar1=-1.0, scalar2=TR[0:NPG, h:h + 1], op0=ALU.mult, op1=ALU.add),
                       R=[buf("cum"), buf("TR")], W=[buf("sufp")])
                rec.op("pe", lambda e, h=h: e.transpose(out=pX[:, 0:NPG], in_=sufp[0:NPG, h, :], identity=G["ident"][0:NPG, 0:NPG]), R=[buf("sufp"), buf("cst")], W=[buf("pX")])
                rec.op("act", lambda e, h=h: e.activation(out=bT[:, h, 0:NPG], in_=pX[:, 0:NPG], func=AF.Copy), R=[buf("pX")], W=[bTb])
            rec.dma("sp", lambda e: e.dma_start(out=ngs[0:64, 0, :], in_=G["negc"][c0:c0 + 64, :]), R=[buf("negc_d")], W=[buf("ngs")])
            rec.op("dve", lambda e: e.tensor_copy(out=bT[0:64, :, NPG], in_=ngs[0:64, 0, :]), R=[buf("ngs")], W=[bTb])
            for i in range(NPG):
                ks_, ksb = kst[i % 4], buf("kst%d" % (i % 4))
                rec.dma("pool", lambda e, ks_=ks_, i=i: e.indirect_dma_start(out=ks_[:], out_offset=None, in_=G["ckv"], in_offset=bass.IndirectOffsetOnAxis(ap=idx[:, i:i + 1], axis=0)),
                        R=[buf("idx")], W=[ksb])
                for h in range(4):
                    rec.op("pe", lambda e, ks_=ks_, h=h: e.transpose(out=pX[:, h * 128:(h + 1) * 128], in_=ks_[:, h * 128:(h + 1) * 128], identity=G["ident"]), R=[ksb, buf("cst")], W=[buf("pX")])
                rec.op("act", lambda e, i=i: e.activation(out=kT[:, :, 128 * i:128 * i + 128], in_=pX[:].rearrange("p (h s) -> p h s", h=4), func=AF.Copy), R=[buf("pX")], W=[buf("kT")])
                rec.op("dve", lambda e, ks_=ks_, i=i: e.tensor_copy(out=V[:, i, :], in_=ks_[:, 512:1024]), R=[ksb], W=[buf("V")])
            rec.dma("sp", lambda e: e.dma_start(out=kT[:, :, 128 * NPG:128 * NPG + 64], in_=G["kbT"][:, :, c0:c0 + 64].rearrange("h p n -> p h n")), R=[buf("kbT_d")], W=[buf("kT")])
            rec.dma("sp", lambda e: e.dma_start(out=V[0:64, NPG, :], in_=G["vbs"][c0:c0 + 64, :]), R=[buf("vbs_d")], W=[buf("V")])
            q_, qbb, s_, sbb = load_q(j, c0, 64)
            o_, obb = ob[j % 2], buf("ob%d" % (j % 2))
            for h in range(4):
                blist = [(128 * i, 128, i, i, 0, False) for i in range(NPG)] + [(128 * NPG, 64, NPG, NPG, 0, True)]
                attend(h, q_, qbb, 64, blist, bT, bTb)
                epilogue(h, 64, s_, sbb, o_, obb)
            rec.dma("pool", lambda e: e.dma_start(out=G["obT"][:, :, c0:c0 + 64].rearrange("h p n -> p h n"), in_=o_[:, :, 0:64]), R=[obb], W=[buf("obT_d")])

        for j in range(4):
            sample_seq(j)
        flush(k, rec)


def phase3(G):
    k, rec, nc = G["k"], G["rec"], G["nc"]
    buf = k.buf
    L = k.L
    NG = L // 512
    pA, pT, pS, pP, pO, pN, pX = G["pA"], G["pT"], G["pS"], G["pP"], G["pO"], G["pN"], G["pX"]
    vec, m64, onesb, idb = G["vec"], G["m64"], G["onesb"], G["idb"]
    with ExitStack() as ph:
        sb = lambda n, s, d: ph.enter_context(nc.sbuf_tensor(n, s, d))
        W1 = load_weight_bf16(G, ph, G["w_out_e"], 1024, None, "W1")
        W2 = load_weight_bf16(G, ph, G["w_in_o"], 3088, 8, "W2")
        W3 = load_weight_bf16(G, ph, G["w_out_o"], 1024, None, "W3")
        wgf = sb("wgf", [16, 512], F32)
        wg = sb("wg", [16, 512], BF16)
        rec.dma("sp", lambda e: e.dma_start(out=wgf[:], in_=G["w_gate"]), W=[buf("wgf")])
        rec.op("dve", lambda e: e.tensor_copy(out=wg[:], in_=wgf[:]), R=[buf("wgf")], W=[buf("wg")])
        fnb = sb("fnb", [128, D], F32)
        rec.dma("sp", lambda e: e.dma_start(out=fnb[:], in_=G["fnorm"].partition_broadcast(128)), W=[buf("fnb")])
        G["junk"] = sb("junk3", [128, 1024], BF16)
        G["ss"] = sb("ss3", [128, 8], F32)
        G["epsb"] = sb("epsb3", [128, 1], F32)
        rec.op("pool", lambda e: e.memset(G["epsb"][:], EPS), W=[buf("epsb")])
        ss2 = sb("ss2", [128, 4], F32)
        oc = sb("oc", [128, 8, 512], BF16)
        xt = [sb("x3_%d" % i, [128, D], F32) for i in range(2)]
        x1 = sb("x1", [128, 4, D], F32)
        hb = [sb("hb3_%d" % i, [128, D], BF16) for i in range(1)]
        hT = sb("hT3", [128, KC, 512], BF16)
        tmp = {n: sb("u_" + n, [128, 512], F32) for n in ("lf", "b", "d", "eq", "ek")}
        qt = sb("qt3", [128, 4, 512], BF16)
        kt = sb("kt3", [128, 4, 512], BF16)
        sm = sb("sm3", [128, 4, 3, 8], F32)
        vtok = sb("vtok3", [128, 4, 1024], BF16)
        ktok = sb("ktok3", [128, 4, 4, 128], BF16)
        AT = [sb("AT3_%d" % i, [128, 4, 64], BF16) for i in range(2)]
        S = sb("S3", [128, 4, 256], F32)
        Sall = sb("Sall3", [128, 8, 256], BF16)
        kt2 = sb("kt23", [128, 512], BF16)
        oT = sb("oT3", [128, 8, 512], F32)
        sq = G["junk"][:].rearrange("p (e n) -> p e n", e=2)
        lnt = sb("lnt3", [128, 512], F32)
        sg = sb("sg3", [128, 8, 512], BF16)
        tg = sb("tg3", [128, 512], F32)
        oc2 = oc
        rT = sb("rT", [16, 512], BF16)
        rec.op("pool", lambda e: e.memset(S[:], 0.0), W=[buf("G%d" % h) for h in range(4)])
        pa_i = [0]

        def nextp():
            i = pa_i[0] % 2
            pa_i[0] += 1
            return pA[i], buf("pA%d" % i)

        def fm(Wt, wname, col, M, N, src, srcb):
            p, pb = nextp()
            for kc in range(KC):
                rec.op("pe", lambda e, kc=kc: e.matmul(out=p[0:M, 0:N], lhsT=Wt[:, kc, col:col + M], rhs=src[:, kc, 0:N], start=(kc == 0), stop=(kc == KC - 1)),
                       R=[buf(wname), srcb], W=[pb])
            return p, pb

        def tm(Wt, wname, col, ncols, t, src, srcb):
            p, pb = nextp()
            for kc in range(KC):
                rec.op("pe", lambda e, kc=kc: e.matmul(out=p[:, 0:ncols], lhsT=src[:, kc, t * 128:(t + 1) * 128], rhs=Wt[:, kc, col:col + ncols], start=(kc == 0), stop=(kc == KC - 1)),
                       R=[buf(wname), srcb], W=[pb])
            return p, pb

        xi = [0]

        def do_group(t0, N, samp):
            ntile = N // 128
            C = 64
            nch = N // C
            mid = 1 if samp else 31
            last = 3 if samp else C - 1
            rec.dma("sp", lambda e: e.dma_start(out=oc[:, 0:4, 0:N], in_=G["oaT"][:, :, t0:t0 + N].rearrange("h p n -> p h n")), R=[buf("oaT_d")], W=[buf("oc")])
            rec.dma("sp", lambda e: e.dma_start(out=oc[:, 4:8, 0:N], in_=G["obT"][:, :, t0:t0 + N].rearrange("h p n -> p h n")), R=[buf("obT_d")], W=[buf("oc")])
            for t in range(ntile):
                x_ = xt[xi[0] % 2]
                xb_ = buf("x3_%d" % (xi[0] % 2))
                h_ = hb[0]
                hb_ = buf("hb3_0")
                xi[0] += 1
                src = G["xs"][t * 128:(t + 1) * 128, :] if samp else G["xp"][t0 + t * 128:t0 + (t + 1) * 128, :]
                rec.dma("sp", lambda e, x_=x_, src=src: e.dma_start(out=x_[:], in_=src), W=[xb_])
                for hf in range(2):
                    p, pb = tm(W1, "W1", hf * 512, 512, t, oc, buf("oc"))
                    rec.op("dve", lambda e, p=p, t=t, hf=hf, x_=x_: e.tensor_tensor(out=x1[:, t, hf * 512:(hf + 1) * 512], in0=p[:, 0:512], in1=x_[:, hf * 512:(hf + 1) * 512], op=ALU.add),
                           R=[pb, xb_], W=[buf("x1_%d" % t)])
                norm_and_transpose(G, x1[:, t, :], buf("x1_%d" % t), t, h_, hb_, hT, buf("hT3"), t * 128)
            for t in range(ntile):
                for hf in range(2):
                    p, pb = tm(W2, "W2", 1024 + hf * 512, 512, t, hT, buf("hT3"))
                    rec.op("act", lambda e, p=p, t=t, hf=hf: e.activation(out=vtok[:, t, hf * 512:(hf + 1) * 512], in_=p[:, 0:512], func=AF.Copy), R=[pb], W=[buf("vtok3")])
            p, pb = fm(W2, "W2", 3072, 16, N, hT, buf("hT3"))
            rec.op("act", lambda e, p=p: e.activation(out=rT[:, 0:N], in_=p[0:16, 0:N], func=AF.Copy), R=[pb], W=[buf("rT")])
            for c8 in range(8):
                p, pb = fm(W2, "W2", 2048 + 128 * c8, 128, N, hT, buf("hT3"))
                rec.op("act", lambda e, p=p, c8=c8: e.activation(out=sg[:, c8, 0:N], in_=p[:, 0:N], func=AF.Silu), R=[pb], W=[buf("sg3")])
            for h in range(4):
                T_ = tmp
                p, pb = nextp()
                rec.op("pe", lambda e, p=p, h=h: e.matmul(out=p[:, 0:N], lhsT=wg[0:16, h * 128:(h + 1) * 128], rhs=rT[0:16, 0:N], start=True, stop=True), R=[buf("wg"), buf("rT")], W=[pb])
                rec.op("act", lambda e, p=p, h=h: e.activation(out=T_["lf"][:, 0:N], in_=p[:, 0:N], func=AF.Sigmoid, bias=vec[:, 36 + h:37 + h]), R=[pb, buf("vec")], W=[buf("u_lf")])
                rec.op("act", lambda e: e.activation(out=T_["lf"][:, 0:N], in_=T_["lf"][:, 0:N], func=AF.Ln), R=[buf("u_lf")], W=[buf("u_lf")])
                rec.op("dve", lambda e: e.tensor_scalar(out=T_["lf"][:, 0:N], in0=T_["lf"][:, 0:N], scalar1=1.0 / 16.0, scalar2=None, op0=ALU.mult), R=[buf("u_lf")], W=[buf("u_lf")])
                rec.op("dve", lambda e: e.tensor_tensor_scan(out=T_["b"][:, 0:N], data0=G["rst64"][:, 0:N], data1=T_["lf"][:, 0:N], initial=0.0, op0=ALU.mult, op1=ALU.add),
                       R=[buf("u_lf"), buf("cst")], W=[buf("u_b")])
                b3 = T_["b"][:, 0:N].rearrange("p (c k) -> p c k", k=C)
                d3 = T_["d"][:, 0:N].rearrange("p (c k) -> p c k", k=C)
                rec.op("dve", lambda e, b3=b3, d3=d3: e.tensor_tensor(out=d3, in0=b3, in1=b3[:, :, mid:mid + 1].to_broadcast([128, nch, C]), op=ALU.subtract), R=[buf("u_b")], W=[buf("u_d")])
                rec.op("act", lambda e: e.activation(out=T_["eq"][:, 0:N], in_=T_["d"][:, 0:N], func=AF.Exp), R=[buf("u_d")], W=[buf("u_eq")])
                rec.op("act", lambda e: e.activation(out=T_["ek"][:, 0:N], in_=T_["d"][:, 0:N], func=AF.Exp, scale=-1.0), R=[buf("u_d")], W=[buf("u_ek")])
                rec.op("act", lambda e, h=h, b3=b3: e.activation(out=sm[:, h, 0, 0:nch], in_=b3[:, :, mid], func=AF.Exp), R=[buf("u_b")], W=[buf("sm3")])
                rec.op("act", lambda e, h=h, b3=b3: e.activation(out=sm[:, h, 2, 0:nch], in_=b3[:, :, last], func=AF.Exp), R=[buf("u_b")], W=[buf("sm3")])
                rec.op("act", lambda e, h=h, d3=d3: e.activation(out=sm[:, h, 1, 0:nch], in_=d3[:, :, last], func=AF.Exp), R=[buf("u_d")], W=[buf("sm3")])
                p, pb = fm(W2, "W2", 128 * h, 128, N, hT, buf("hT3"))
                rec.op("dve", lambda e, p=p, h=h: e.scalar_tensor_tensor(out=qt[:, h, 0:N], in0=p[:, 0:N], scalar=SCALE, in1=T_["eq"][:, 0:N], op0=ALU.mult, op1=ALU.mult),
                       R=[pb, buf("u_eq")], W=[buf("qt3")])
                p, pb = fm(W2, "W2", 512 + 128 * h, 128, N, hT, buf("hT3"))
                rec.op("dve", lambda e, p=p, h=h: e.tensor_tensor(out=kt[:, h, 0:N], in0=p[:, 0:N], in1=T_["ek"][:, 0:N], op=ALU.mult), R=[pb, buf("u_ek")], W=[buf("kt3")])
                rec.op("dve", lambda e, h=h: e.tensor_tensor(out=kt2[:, 0:N].rearrange("p (c k) -> p c k", k=C), in0=kt[:, h, 0:N].rearrange("p (c k) -> p c k", k=C),
                                                           in1=sm[:, h, 1, 0:nch].unsqueeze(2).to_broadcast([128, nch, C]), op=ALU.mult), R=[buf("kt3"), buf("sm3")], W=[buf("kt23")])
                for t in range(ntile):
                    rec.op("pe", lambda e, h=h, t=t: e.transpose(out=pT[:, h * 128:(h + 1) * 128], in_=kt2[:, t * 128:(t + 1) * 128], identity=idb[:]), R=[buf("kt23"), buf("idb")], W=[buf("pT")])
                    rec.op("act", lambda e, h=h, t=t: e.activation(out=ktok[:, t, h, :], in_=pT[:, h * 128:(h + 1) * 128], func=AF.Copy), R=[buf("pT")], W=[buf("ktok3")])

            def cinfo(c):
                r0 = (c * C) % 128
                return slice(c * C, (c + 1) * C), (c * C) // 128, slice(r0, r0 + C)

            for h in range(4):
                Sb = buf("G%d" % h)
                at = AT[h % 2]
                atb = buf("AT3_%d" % (h % 2))
                sal = Sall
                salb = buf("Sall3")
                for c in range(nch):
                    cols, t, rows = cinfo(c)
                    rec.op("pe", lambda e, h=h, cols=cols, rows=rows, t=t: e.matmul(out=pS[rows, t * 64:t * 64 + C], lhsT=kt[:, h, cols], rhs=qt[:, h, cols], start=True, stop=True),
                           R=[buf("kt3"), buf("qt3")], W=[buf("pS")])
                rec.op("dve", lambda e, at=at: e.tensor_tensor(out=at[:, 0:ntile, :], in0=pS[:, 0:ntile * 64].rearrange("p (a b) -> p a b", b=64),
                                                              in1=m64[:, :].unsqueeze(1).to_broadcast([128, ntile, 64]), op=ALU.mult),
                       R=[buf("pS"), buf("m64")], W=[atb])
                for c in range(nch):
                    cols, t, rows = cinfo(c)
                    pb_, pbb_ = (pP, buf("pP")) if c % 2 == 0 else (pX, buf("pX"))
                    rec.op("pe", lambda e, h=h, rows=rows, t=t, pb_=pb_: e.matmul(out=pb_[:, 0:256], lhsT=ktok[rows, t, h, :], rhs=vtok[rows, t, h * 256:(h + 1) * 256], start=True, stop=True),
                           R=[buf("ktok3"), buf("vtok3")], W=[pbb_])
                    if samp:
                        rec.dma("sp", lambda e, h=h, c=c: e.dma_start(out=S[:, h, :], in_=G["st_g"][c, h]), W=[Sb])
                    rec.op("dve", lambda e, h=h, c=c: e.tensor_scalar(out=sal[:, c, :], in0=S[:, h, :], scalar1=sm[:, h, 0, c:c + 1], scalar2=None, op0=ALU.mult),
                           R=[Sb, buf("sm3")], W=[salb])
                    rec.op("dve", lambda e, h=h, c=c, pb_=pb_: e.scalar_tensor_tensor(out=S[:, h, :], in0=S[:, h, :], scalar=sm[:, h, 2, c:c + 1], in1=pb_[:, 0:256], op0=ALU.mult, op1=ALU.add),
                           R=[pbb_, Sb, buf("sm3")], W=[Sb])
                    if samp:
                        rec.dma("pool", lambda e, h=h, c=c: e.dma_start(out=G["gl_s"][c, h], in_=S[:, h, :]), R=[Sb], final=True)
                for e2 in range(2):
                    vs = slice(h * 256 + e2 * 128, h * 256 + (e2 + 1) * 128)
                    for c in range(nch):
                        cols, t, rows = cinfo(c)
                        po_, pob_ = (pO, buf("pO")) if c % 2 == 0 else (pA[0], buf("pA0"))
                        rec.op("pe", lambda e, vs=vs, rows=rows, t=t, at=at, c=c, po_=po_: e.matmul(out=po_[:, (c // 2) * 64:(c // 2) * 64 + C], lhsT=vtok[rows, t, vs], rhs=at[rows, t, :], start=True, stop=True),
                               R=[buf("vtok3"), atb], W=[pob_])
                    for c in range(nch):
                        cols, t, rows = cinfo(c)
                        rec.op("pe", lambda e, h=h, cols=cols, c=c, e2=e2: e.matmul(out=pN[:, c * 64:c * 64 + C], lhsT=sal[:, c, e2 * 128:(e2 + 1) * 128], rhs=qt[:, h, cols], start=True, stop=True),
                               R=[salb, buf("qt3")], W=[buf("pN")])
                    o4 = oT[:, 2 * h + e2, 0:N].rearrange("p (a two k) -> p a two k", two=2, k=64)
                    rec.op("act", lambda e, o4=o4: e.activation(out=o4[:, :, 0, :], in_=pO[:, 0:N // 2].rearrange("p (a k) -> p a k", k=64), func=AF.Copy), R=[buf("pO")], W=[buf("oT3")])
                    rec.op("act", lambda e, o4=o4: e.activation(out=o4[:, :, 1, :], in_=pA[0][:, 0:N // 2].rearrange("p (a k) -> p a k", k=64), func=AF.Copy), R=[buf("pA0")], W=[buf("oT3")])
                    rec.op("dve", lambda e, h=h, e2=e2: e.tensor_tensor(out=oT[:, 2 * h + e2, 0:N], in0=oT[:, 2 * h + e2, 0:N], in1=pN[:, 0:N], op=ALU.add), R=[buf("pN"), buf("oT3")], W=[buf("oT3")])
            if (not samp) and t0 + N == L:
                for h in range(4):
                    rec.dma("pool", lambda e, h=h: e.dma_start(out=G["gl_p"][h], in_=S[:, h, :]), R=[buf("G%d" % h)], final=True)
            for h in range(4):
                for e2 in range(2):
                    rec.op("act", lambda e, h=h, e2=e2: e.activation(out=sq[:, e2, 0:N], in_=oT[:, 2 * h + e2, 0:N], func=AF.Square), R=[buf("oT3")], W=[buf("junk")])
                for e2 in range(2):
                    rec.op("pe", lambda e, e2=e2: e.matmul(out=pN[:, 0:N], lhsT=onesb[:], rhs=sq[:, e2, 0:N], start=(e2 == 0), stop=(e2 == 1)), R=[buf("onesb"), buf("junk")], W=[buf("pN")])
                rec.op("act", lambda e: e.activation(out=lnt[:, 0:N], in_=pN[:, 0:N], func=AF.Ln, scale=1.0 / 256, bias=G["epsb"][:, 0:1]), R=[buf("pN"), buf("epsb")], W=[buf("lnt3")])
                rec.op("act", lambda e: e.activation(out=lnt[:, 0:N], in_=lnt[:, 0:N], func=AF.Exp, scale=-0.5), R=[buf("lnt3")], W=[buf("lnt3")])
                for e2 in range(2):
                    c8 = 2 * h + e2
                    rec.op("dve", lambda e, c8=c8: e.scalar_tensor_tensor(out=tg[:, 0:N], in0=lnt[:, 0:N], scalar=vec[:, 40 + c8:41 + c8], in1=sg[:, c8, 0:N], op0=ALU.mult, op1=ALU.mult),
                           R=[buf("lnt3"), buf("vec"), buf("sg3")], W=[buf("tg3")])
                    rec.op("dve", lambda e, c8=c8: e.tensor_tensor(out=oc2[:, c8, 0:N], in0=oT[:, c8, 0:N], in1=tg[:, 0:N], op=ALU.mult), R=[buf("oT3"), buf("tg3")], W=[buf("oc")])
            for t in range(ntile):
                xb1 = buf("x1_%d" % t)
                for hf in range(2):
                    p, pb = tm(W3, "W3", hf * 512, 512, t, oc2, buf("oc"))
                    rec.op("dve", lambda e, p=p, t=t, hf=hf: e.tensor_tensor(out=x1[:, t, hf * 512:(hf + 1) * 512], in0=p[:, 0:512], in1=x1[:, t, hf * 512:(hf + 1) * 512], op=ALU.add), R=[pb, xb1], W=[xb1])
                rec.op("act", lambda e, t=t: e.activation(out=G["junk"][:], in_=x1[:, t, :], func=AF.Square, accum_out=ss2[:, t:t + 1]), R=[xb1], W=[buf("junk"), buf("ss2")])
                rec.op("act", lambda e, t=t: e.activation(out=ss2[:, t:t + 1], in_=ss2[:, t:t + 1], func=AF.Ln, scale=1.0 / D, bias=G["epsb"][:, 0:1]), R=[buf("ss2"), buf("epsb")], W=[buf("ss2")])
                rec.op("act", lambda e, t=t: e.activation(out=ss2[:, t:t + 1], in_=ss2[:, t:t + 1], func=AF.Exp, scale=-0.5), R=[buf("ss2")], W=[buf("ss2")])
                rec.op("dve", lambda e, t=t: e.scalar_tensor_tensor(out=x1[:, t, :], in0=x1[:, t, :], scalar=ss2[:, t:t + 1], in1=fnb[:], op0=ALU.mult, op1=ALU.mult), R=[xb1, buf("ss2"), buf("fnb")], W=[xb1])
                dst = G["y_s"][t * 128:(t + 1) * 128, :] if samp else G["y_p"][t0 + t * 128:t0 + (t + 1) * 128, :]
                rec.dma("pool", lambda e, t=t, dst=dst: e.dma_start(out=dst, in_=x1[:, t, :]), R=[xb1], final=True)

        groups = [(g * 512, 512, False) for g in range(NG)] + [(L, 256, True)]
        for (t0_, N_, samp_) in groups:
            do_group(t0_, N_, samp_)
        flush(k, rec)


def make_consts():
    c = np.zeros((128, 1280), np.float32)
    p = np.arange(128)[:, None]
    j = np.arange(128)[None, :]
    c[:, 0:128] = np.eye(128)
    c[:, 128:256] = (p <= j)
    j64 = np.arange(64)[None, :]
    c[:, 256:320] = ((p % 64) <= j64)
    c[:, 320:448] = 1.0
    c[:, 448:576] = ((p // 64) == (j // 64)) & (p <= j)
    j512 = np.arange(512)[None, :]
    c[:, 576:1088] = (j512 % 64 != 0) * np.ones((128, 1))
    c[:, 1088:1216] = (p > j)
    c[:, 1216] = np.arange(128)
    return c

def pack_vecs(inp):
    v = np.zeros((128, 64), np.float32)
    f = lambda a, n: np.asarray(a, np.float32).reshape(n, 128).T
    v[:, 0:8] = f(inp['norm_even'][0], 8)
    v[:, 8:16] = f(inp['norm_odd'][0], 8)
    v[:, 16:24] = f(inp['final_norm'], 8)
    v[:, 24:28] = f(inp['lb_logits'][0], 4)
    v[:, 28:32] = f(inp['lb_logits'][1], 4)
    v[:, 32:36] = f(inp['hgrn_gain'][0], 4)
    v[:, 36:40] = f(inp['b_gla_gate'][0], 4)
    v[:, 40:48] = f(inp['gla_gain'][0], 8)
    return v

def make_ckv(inp):
    pool = inp['cache_fox_k'].shape[1]
    return np.concatenate([np.asarray(inp['cache_fox_k'][0]).reshape(pool * 128, 512),
                           np.asarray(inp['cache_fox_v'][0]).reshape(pool * 128, 512)], axis=1)


def core_inputs(inp, c, L, NPG, ckv=None):
    b = c // 4
    m = {}
    m['xp'] = np.ascontiguousarray(inp['x_prompt'][b, :L])
    xs = np.zeros((256, 1024), np.float32)
    for j in range(4):
        xs[64 * j:64 * j + 4] = inp['x_sample'][4 * c + j]
    m['xs'] = xs
    m['w_in_even'] = np.ascontiguousarray(inp['w_in_even'][0])
    m['w_out_even'] = np.ascontiguousarray(inp['w_out_even'][0])
    m['w_in_odd'] = np.ascontiguousarray(inp['w_in_odd'][0])
    m['w_out_odd'] = np.ascontiguousarray(inp['w_out_odd'][0])
    m['w_gla_gate'] = np.ascontiguousarray(inp['w_gla_gate'][0])
    m['vecs'] = pack_vecs(inp)
    m['fnorm'] = np.ascontiguousarray(np.asarray(inp['final_norm'], np.float32))
    m['bfox'] = np.ascontiguousarray(np.broadcast_to(np.asarray(inp['b_fox_f'][0], np.float32)[None, :], (128, 4)))
    m['consts'] = make_consts()
    m['st_hgrn'] = np.ascontiguousarray(inp['state_hgrn'][0, 4 * c:4 * c + 4])
    m['st_gla'] = np.ascontiguousarray(inp['state_gla'][0, 4 * c:4 * c + 4])
    m['ptab'] = np.ascontiguousarray(inp['page_table'][4 * c:4 * c + 4, :NPG]).astype(np.int32)
    pool = inp['cache_fox_k'].shape[1]
    m['cache_kv'] = ckv if ckv is not None else make_ckv(inp)
    m['cache_lf'] = np.asarray(inp['cache_fox_logf'][0]).reshape(pool, 512)
    return m


def assemble(results, L):
    f = lambda a: np.asarray(a, np.float32)
    idx = np.concatenate([np.arange(64 * j, 64 * j + 4) for j in range(4)])
    pc = [0, 4]
    y_p = np.stack([f(results[c]['y_p']) for c in pc])
    y_s = np.concatenate([f(results[c]['y_s'])[idx] for c in range(8)]).reshape(32, 4, 1024)
    npg = L // 128
    fk_p = np.stack([f(results[c]['fk_p']) for c in pc]).reshape(1, 2, npg, 128, 4, 128)
    fv_p = np.stack([f(results[c]['fv_p']) for c in pc]).reshape(1, 2, npg, 128, 4, 128)
    flf_p = np.stack([f(results[c]['flf_p']) for c in pc]).reshape(1, 2, npg, 128, 4)
    hg_p = np.stack([f(results[c]['hg_p']) for c in pc])[None]
    gl_p = np.stack([f(results[c]['gl_p']) for c in pc])[None]
    fk_s = np.concatenate([f(results[c]['fk_s'])[idx] for c in range(8)]).reshape(1, 32, 4, 4, 128)
    fv_s = np.concatenate([f(results[c]['fv_s'])[idx] for c in range(8)]).reshape(1, 32, 4, 4, 128)
    flf_s = np.concatenate([f(results[c]['flf_s'])[idx] for c in range(8)]).reshape(1, 32, 4, 4)
    hg_s = np.concatenate([f(results[c]['hg_s']) for c in range(8)])[None]
    gl_s = np.concatenate([f(results[c]['gl_s']) for c in range(8)])[None]
    return (y_p, y_s, fk_p, fv_p, flf_p, hg_p, gl_p, fk_s, fv_s, flf_s, hg_s, gl_s)


def kernel(**inputs):
    inp = {k_: np.asarray(v) for k_, v in inputs.items()}
    L = inp['x_prompt'].shape[1]
    NPG = inp['page_table'].shape[1]
    POOL = inp['cache_fox_k'].shape[1]
    nc = build(L, NPG, POOL, dbg=False, phases=(1, 2, 3))
    ckv = make_ckv(inp)
    maps = [core_inputs(inp, c, L, NPG, ckv) for c in range(8)]
    res = run_bass_kernel_spmd(nc, maps, core_ids=list(range(8)))
    return assemble(res.results, L)
```

```python
import numpy as np
from concourse.bass_utils import run_bass_kernel_spmd
from contextlib import ExitStack
import concourse.bass as bass
import concourse.mybir as mybir

F32 = mybir.dt.float32
BF16 = mybir.dt.bfloat16
I32 = mybir.dt.int32
U32 = mybir.dt.uint32
AF = mybir.ActivationFunctionType
ALU = mybir.AluOpType
AX = mybir.AxisListType


class Buf:
    __slots__ = ("w", "r", "name", "excl")

    def __init__(self, name=""):
        self.excl = False
        self.w = None
        self.r = {}
        self.name = name


class Eng:
    def __init__(self, name):
        self.name = name
        self.prog = []
        self.count = 0
        self.waited = {}


NDS = 12


class Rec:
    COMPUTE = ("pe", "act", "dve", "pool")

    def __init__(self, nc, stack):
        self.nc = nc
        self.e = {n: Eng(n) for n in ("pe", "act", "dve", "pool", "sp")}
        self.sems = {}
        for n in self.COMPUTE:
            self.sems[n] = stack.enter_context(nc.semaphore("s_" + n))
        self.dq = {}
        for q in ("sp", "pool", "act"):
            sl = []
            for i in range(NDS):
                key = "d_%s_%d" % (q, i)
                self.sems[key] = stack.enter_context(nc.semaphore(key))
                sl.append(key)
            self.dq[q] = dict(slots=sl, uses=[0] * NDS, n=0)
        self.final = []

    def _need(self, eng, deps):
        for k, v in deps.items():
            if eng.waited.get(k, 0) < v:
                eng.waited[k] = v
                eng.prog.append(("wait", k, v))

    def _collect(self, ename, R, W):
        deps = {}

        def add(d, kind):
            if d is None:
                return
            k, v, en = d
            if en == ename and ename in self.COMPUTE:
                if ename == "pe":
                    return
                if kind == "war":
                    return
            if deps.get(k, 0) < v:
                deps[k] = v

        for b in R:
            add(b.w, "raw")
        for b in W:
            add(b.w, "waw")
            for k, (v, en) in b.r.items():
                add((k, v, en), "war")
        return deps

    def _mark(self, tok, R, W):
        k, v, en = tok
        for b in R:
            b.r[k] = (v, en)
        for b in W:
            b.w = tok
            b.r = {}

    def op(self, ename, fn, R=(), W=()):
        eng = self.e[ename]
        deps = self._collect(ename, R, W)
        for b in R:
            if b.excl:
                for k2, (v2, en2) in b.r.items():
                    if en2 != ename and deps.get(k2, 0) < v2:
                        deps[k2] = v2
        self._need(eng, deps)
        eng.count += 1
        eng.prog.append(("op", fn, ename, 1))
        self._mark((ename, eng.count, ename), R, W)

    def dma(self, q, fn, R=(), W=(), final=False):
        eng = self.e[q]
        dq = self.dq[q]
        slot = dq["n"] % NDS
        dq["n"] += 1
        key = dq["slots"][slot]
        if dq["uses"][slot] > 0:
            self._need(eng, {key: 16 * dq["uses"][slot]})
        dq["uses"][slot] += 1
        val = 16 * dq["uses"][slot]
        deps = self._collect("dma_" + q, R, W)
        self._need(eng, deps)
        eng.prog.append(("op", fn, key, 16))
        self._mark((key, val, "dma_" + q), R, W)
        if final:
            self.final.append((key, val))

    def finish(self):
        eng = self.e["sp"]
        last = {}
        for k, v in self.final:
            last[k] = max(last.get(k, 0), v)
        for k, v in last.items():
            eng.prog.append(("wait", k, v))
        for q in ("pool", "act"):
            dq = self.dq[q]
            for i, u in enumerate(dq["uses"]):
                if u:
                    self.e[q].prog.append(("wait", dq["slots"][i], 16 * u))

    def replay(self, block):
        nc = self.nc
        sems = self.sems

        def run(engobj, prog):
            for it in prog:
                if it[0] == "wait":
                    engobj.wait_ge(sems[it[1]], it[2])
                else:
                    _, fn, key, inc = it
                    ins = fn(engobj)
                    ins.then_inc(sems[key], inc)

        @block.sync
        def _(e):
            run(e, self.e["sp"].prog)

        @block.tensor
        def _(e):
            run(e, self.e["pe"].prog)

        @block.scalar
        def _(e):
            run(e, self.e["act"].prog)

        @block.vector
        def _(e):
            run(e, self.e["dve"].prog)

        @block.gpsimd
        def _(e):
            run(e, self.e["pool"].prog)


import os
KSTOP = int(os.environ.get('KSTOP', '99'))
KSUB = int(os.environ.get('KSUB', '99'))

D = 1024
KC = 8
EPS = 1e-6
SCALE = 128 ** -0.5


class K:
    def __init__(self, L, NPG, POOL, dbg=False):
        self.L, self.NPG, self.POOL, self.dbg = L, NPG, POOL, dbg
        self.nc = bass.Bass("TRN2", target_bir_lowering=False)
        self.B = {}

    def buf(self, name):
        if name not in self.B:
            self.B[name] = Buf(name)
        return self.B[name]

    def din(self, name, shape, dt=F32):
        return self.nc.dram_tensor(name, list(shape), dt, kind="ExternalInput").ap()

    def dout(self, name, shape, dt=F32):
        return self.nc.dram_tensor(name, list(shape), dt, kind="ExternalOutput").ap()

    def dscr(self, name, shape, dt):
        return self.nc.dram_tensor(name, list(shape), dt, kind="Internal").ap()


def barrier(rec):
    tgt = {}
    for n in Rec.COMPUTE:
        if rec.e[n].count:
            tgt[n] = rec.e[n].count
    for q, dq in rec.dq.items():
        for i, u in enumerate(dq["uses"]):
            if u:
                tgt[dq["slots"][i]] = 16 * u
    for n in ("pe", "act", "dve", "pool", "sp"):
        d = {k: v for k, v in tgt.items() if k != n}
        rec._need(rec.e[n], d)


def flush(k, rec):
    barrier(rec)
    with k.nc.Block() as block:
        rec.replay(block)
    for e in rec.e.values():
        e.prog = []


def build(L, NPG, POOL, dbg=False, phases=(1, 2, 3)):
    k = K(L, NPG, POOL, dbg)
    nc = k.nc
    NG = L // 512
    NT = L // 128
    buf = k.buf
    xp = k.din("xp", [L, D])
    xs = k.din("xs", [256, D])
    w_in_e = k.din("w_in_even", [D, 4100])
    w_out_e = k.din("w_out_even", [D, D])
    w_in_o = k.din("w_in_odd", [D, 3088])
    w_out_o = k.din("w_out_odd", [D, D])
    w_gate = k.din("w_gla_gate", [16, 512])
    fnorm = k.din("fnorm", [D])
    vecs = k.din("vecs", [128, 64])
    bfox = k.din("bfox", [128, 4])
    consts = k.din("consts", [128, 1280])
    st_h = k.din("st_hgrn", [4, 4, 128, 128])
    st_g = k.din("st_gla", [4, 4, 128, 256])
    ptab = k.din("ptab", [4, NPG], I32)
    ckv = k.din("cache_kv", [POOL * 128, 1024])
    clf = k.din("cache_lf", [POOL, 512])

    y_p = k.dout("y_p", [L, D])
    y_s = k.dout("y_s", [256, D])
    fk_p = k.dout("fk_p", [L, 512])
    fv_p = k.dout("fv_p", [L, 512])
    flf_p = k.dout("flf_p", [L, 4])
    hg_p = k.dout("hg_p", [4, 128, 128])
    gl_p = k.dout("gl_p", [4, 128, 256])
    fk_s = k.dout("fk_s", [256, 512])
    fv_s = k.dout("fv_s", [256, 512])
    flf_s = k.dout("flf_s", [256, 4])
    hg_s = k.dout("hg_s", [4, 4, 128, 128])
    gl_s = k.dout("gl_s", [4, 4, 128, 256])

    LT = L + 256
    qbT = k.dscr("qbT", [4, 128, LT], BF16)
    kbT = k.dscr("kbT", [4, 128, LT], BF16)
    sgbT = k.dscr("sgbT", [4, 128, LT], BF16)
    vbs = k.dscr("vbs", [LT, 512], BF16)
    negc = k.dscr("negc", [LT, 4], F32)
    oaT = k.dscr("oaT", [4, 128, LT], BF16)
    obT = k.dscr("obT", [4, 128, LT], BF16)

    with ExitStack() as st:
        rec = Rec(nc, st)
        sb = lambda n, s, d: st.enter_context(nc.sbuf_tensor(n, s, d))
        ps = lambda n, s, d: st.enter_context(nc.psum_tensor(n, s, d))
        pA = [ps("pA%d" % i, [128, 512], F32) for i in range(2)]
        pT = ps("pT", [128, 1024], BF16)
        pS = ps("pS", [128, 512], F32)
        pP = ps("pP", [128, 512], F32)
        pO = ps("pO", [128, 512], F32)
        pN = ps("pN", [128, 512], F32)
        pX = ps("pX", [128, 512], F32)
        for nm in ("pA0", "pA1", "pT", "pS", "pP", "pO", "pN", "pX"):
            buf(nm).excl = True
        cst = sb("cst", [128, 1280], F32)
        vec = sb("vec", [128, 64], F32)
        bfx = sb("bfx", [128, 4], F32)
        idb = sb("idb", [128, 128], BF16)
        onesb = sb("onesb", [128, 128], BF16)
        m64 = sb("m64", [128, 64], F32)
        lbt = sb("lbt", [128, 4], F32)
        omlt = sb("omlt", [128, 4], F32)
        nomlt = sb("nomlt", [128, 4], F32)
        rec.dma("sp", lambda e: e.dma_start(out=cst[:], in_=consts), W=[buf("cst")])
        rec.dma("sp", lambda e: e.dma_start(out=vec[:], in_=vecs), W=[buf("vec")])
        rec.dma("sp", lambda e: e.dma_start(out=bfx[:], in_=bfox), W=[buf("bfx")])
        ident = cst[:, 0:128]
        tri = cst[:, 128:256]
        ones = cst[:, 320:448]
        bt32 = cst[:, 448:576]
        rst64 = cst[:, 576:1088]
        rst32 = cst[:, 1088:1216]
        rec.op("dve", lambda e: e.tensor_copy(out=idb[:], in_=ident), R=[buf("cst")], W=[buf("idb")])
        rec.op("dve", lambda e: e.tensor_copy(out=onesb[:], in_=ones), R=[buf("cst")], W=[buf("onesb")])
        rec.op("dve", lambda e: e.tensor_copy(out=m64[:], in_=cst[:, 256:320]), R=[buf("cst")], W=[buf("m64")])
        rec.op("dve", lambda e: e.tensor_sub(out=lbt[:], in0=vec[:, 24:28], in1=vec[:, 28:32]), R=[buf("vec")], W=[buf("lbt")])
        rec.op("act", lambda e: e.activation(out=lbt[:], in_=lbt[:], func=AF.Sigmoid), R=[buf("lbt")], W=[buf("lbt")])
        rec.op("dve", lambda e: e.tensor_scalar(out=omlt[:], in0=lbt[:], scalar1=-1.0, scalar2=1.0, op0=ALU.mult, op1=ALU.add),
               R=[buf("lbt")], W=[buf("omlt")])
        rec.op("dve", lambda e: e.tensor_scalar(out=nomlt[:], in0=omlt[:], scalar1=-1.0, scalar2=None, op0=ALU.mult),
               R=[buf("omlt")], W=[buf("nomlt")])

        G = dict(k=k, rec=rec, nc=nc, st=st, pA=pA, pT=pT, pS=pS, pP=pP, pO=pO, pN=pN, pX=pX, cst=cst, vec=vec, bfx=bfx,
                 idb=idb, onesb=onesb, m64=m64, lbt=lbt, omlt=omlt, nomlt=nomlt, ident=ident, tri=tri, ones=ones, bt32=bt32,
                 rst64=rst64)
        G.update(locals())
        if 1 in phases:
            phase1(G)
        if 2 in phases:
            phase2(G)
        if 3 in phases:
            phase3(G)
        if dbg:
            dbg_ob = k.dout("dbg_obT", [4, 128, LT], BF16)
            if 2 in phases:
                rec.dma("sp", lambda e: e.dma_start(out=dbg_ob, in_=obT), R=[buf("obT_d")], final=True)
            dbg_oa = k.dout("dbg_oaT", [4, 128, LT], BF16)
            rec.dma("sp", lambda e: e.dma_start(out=dbg_oa, in_=oaT), R=[buf("oaT_d")], final=True)
            dbg_nc = k.dout("dbg_negc", [LT, 4], F32)
            rec.dma("sp", lambda e: e.dma_start(out=dbg_nc, in_=negc), R=[buf("negc_d")], final=True)
        rec.finish()
        flush(k, rec)
    return nc


def load_weight_bf16(G, ph, wdram, ncols, gaincol, name):
    k, rec, nc, vec = G["k"], G["rec"], G["nc"], G["vec"]
    buf = k.buf
    W = ph.enter_context(nc.sbuf_tensor(name, [128, KC, ncols], BF16))
    with ExitStack() as tmp:
        stg = [tmp.enter_context(nc.sbuf_tensor(name + "_stg%d" % i, [128, ncols], F32)) for i in range(2)]
        wv = wdram.rearrange("(kc p) n -> p kc n", p=128)
        for kc in range(KC):
            s = stg[kc % 2]
            sbuf = buf(name + "_stg%d" % (kc % 2))
            rec.dma("sp", lambda e, s=s, kc=kc: e.dma_start(out=s[:], in_=wv[:, kc, :]), W=[sbuf])
            hc = (ncols // 2 + 3) // 4 * 4
            if gaincol is None:
                rec.op("dve", lambda e, s=s, kc=kc: e.tensor_copy(out=W[:, kc, 0:hc], in_=s[:, 0:hc]), R=[sbuf], W=[buf(name)])
                rec.op("act", lambda e, s=s, kc=kc: e.activation(out=W[:, kc, hc:ncols], in_=s[:, hc:ncols], func=AF.Copy), R=[sbuf], W=[buf(name + "_b")])
            else:
                gc = vec[:, gaincol + kc:gaincol + kc + 1]
                rec.op("dve", lambda e, s=s, kc=kc, gc=gc: e.tensor_scalar(out=W[:, kc, 0:hc], in0=s[:, 0:hc], scalar1=gc, scalar2=None, op0=ALU.mult), R=[sbuf, buf("vec")], W=[buf(name)])
                rec.op("act", lambda e, s=s, kc=kc, gc=gc: e.activation(out=W[:, kc, hc:ncols], in_=s[:, hc:ncols], func=AF.Copy, scale=gc), R=[sbuf, buf("vec")], W=[buf(name + "_b")])
        flush(k, rec)
    return W


def norm_and_transpose(G, xt, xtb, rstd_col, hb, hbb, hT, hTb, tcol, gain_in_w=True):
    rec, pT, idb = G["rec"], G["pT"], G["idb"]
    buf = G["k"].buf
    junk, ss = G["junk"], G["ss"]
    rec.op("act", lambda e: e.activation(out=junk[:], in_=xt[:], func=AF.Square, accum_out=ss[:, rstd_col:rstd_col + 1]),
           R=[xtb], W=[buf("junk"), buf("ss")])
    rec.op("act", lambda e: e.activation(out=ss[:, rstd_col:rstd_col + 1], in_=ss[:, rstd_col:rstd_col + 1], func=AF.Ln, scale=1.0 / D, bias=G["epsb"][:, 0:1]),
           R=[buf("ss"), buf("epsb")], W=[buf("ss")])
    rec.op("act", lambda e: e.activation(out=ss[:, rstd_col:rstd_col + 1], in_=ss[:, rstd_col:rstd_col + 1], func=AF.Exp, scale=-0.5),
           R=[buf("ss")], W=[buf("ss")])
    rec.op("dve", lambda e: e.tensor_scalar(out=hb[:], in0=xt[:], scalar1=ss[:, rstd_col:rstd_col + 1], scalar2=None, op0=ALU.mult),
           R=[xtb, buf("ss")], W=[hbb])
    for kc in range(KC):
        rec.op("pe", lambda e, kc=kc: e.transpose(out=pT[:, kc * 128:(kc + 1) * 128], in_=hb[:, kc * 128:(kc + 1) * 128], identity=idb[:]),
               R=[hbb, buf("idb")], W=[buf("pT")])
    rec.op("act", lambda e: e.activation(out=hT[:, :, tcol:tcol + 128], in_=pT[:].rearrange("p (kc t) -> p kc t", kc=KC), func=AF.Copy),
           R=[buf("pT")], W=[hTb])


def phase1(G):
    k, rec, nc = G["k"], G["rec"], G["nc"]
    buf = k.buf
    L = k.L
    NG = L // 512
    pA, pT, pS, pP, pO, pN, pX = G["pA"], G["pT"], G["pS"], G["pP"], G["pO"], G["pN"], G["pX"]
    vec, lbt, omlt, nomlt, m64, onesb, idb = G["vec"], G["lbt"], G["omlt"], G["nomlt"], G["m64"], G["onesb"], G["idb"]
    with ExitStack() as ph:
        sb = lambda n, s, d: ph.enter_context(nc.sbuf_tensor(n, s, d))
        W0 = load_weight_bf16(G, ph, G["w_in_e"], 4100, 0, "W0")
        G["junk"] = sb("junk", [128, 1024], BF16)
        G["ss"] = sb("ss", [128, 8], F32)
        G["epsb"] = sb("epsb", [128, 1], F32)
        rec.op("pool", lambda e: e.memset(G["epsb"][:], EPS), W=[buf("epsb")])
        xt = [sb("xt%d" % i, [128, D], F32) for i in range(3)]
        hb = [sb("hb%d" % i, [128, D], BF16) for i in range(2)]
        hT = sb("hT", [128, KC, 512], BF16)
        tmp = {n: sb("t_" + n, [128, 512], F32) for n in ("sig", "f", "omf", "lf", "b", "d", "eq", "ek")}
        qt = sb("qt", [128, 4, 512], BF16)
        kt = sb("kt", [128, 4, 512], BF16)
        sm = sb("sm", [128, 4, 3, 8], F32)
        vtok = sb("vtok", [128, 4, 512], BF16)
        ktok = sb("ktok", [128, 4, 4, 128], BF16)
        AT = [sb("AT%d" % i, [128, 4, 64], BF16) for i in range(2)]
        S = sb("S", [128, 4, 128], F32)
        Sall = [sb("Sall%d" % i, [128, 8, 128], BF16) for i in range(2)]
        kt2 = sb("kt2", [128, 512], BF16)
        oT = sb("oT", [128, 4, 512], F32)
        sq = sb("sq", [128, 512], BF16)
        lnt = sb("lnt", [128, 512], F32)
        sg = sb("sg", [128, 4, 512], BF16)
        tg = sb("tg", [128, 512], F32)
        oa = sb("oa", [128, 4, 512], BF16)
        qb = sb("qb", [128, 4, 512], BF16)
        kb = sb("kb", [128, 4, 512], BF16)
        sgb = sb("sgb", [128, 4, 512], BF16)
        ktm = [sb("ktm%d" % i, [128, 512], F32) for i in range(2)]
        vtm = [sb("vtm%d" % i, [128, 512], F32) for i in range(2)]
        vbf = sb("vbf", [128, 4, 512], BF16)
        lfb = sb("lfb", [128, 4, 4], F32)
        ncb = sb("ncb", [128, 4, 4], F32)
        tot = sb("tot", [128, 4], F32)
        rec.op("pool", lambda e: e.memset(tot[:], 0.0), W=[buf("tot")])
        rec.op("pool", lambda e: e.memset(S[:], 0.0), W=[buf("S%d" % h) for h in range(4)])

        pa_i = [0]

        def fm(col, N, hTN):
            p = pA[pa_i[0] % 2]
            pb = buf("pA%d" % (pa_i[0] % 2))
            pa_i[0] += 1
            for kc in range(KC):
                rec.op("pe", lambda e, kc=kc, p=p: e.matmul(out=p[:, 0:N], lhsT=W0[:, kc, col:col + 128], rhs=hTN[:, kc, 0:N],
                                                           start=(kc == 0), stop=(kc == KC - 1)),
                       R=[buf("W0"), buf("hT")], W=[pb])
            return p, pb

        def tm(col, ncols, t):
            p = pA[pa_i[0] % 2]
            pb = buf("pA%d" % (pa_i[0] % 2))
            pa_i[0] += 1
            for kc in range(KC):
                rec.op("pe", lambda e, kc=kc, p=p: e.matmul(out=p[:, 0:ncols], lhsT=hT[:, kc, t * 128:(t + 1) * 128], rhs=W0[:, kc, col:col + ncols],
                                                           start=(kc == 0), stop=(kc == KC - 1)),
                       R=[buf("W0"), buf("hT")], W=[pb])
            return p, pb

        groups = [(g * 512, 512, False) for g in range(NG)] + [(L, 256, True)]
        xi = [0]
        def do_group(t0, N, samp):
            ntile = N // 128
            C = 64
            nch = N // C
            mid = 1 if samp else 31
            last = 3 if samp else C - 1
            for t in range(ntile):
                x_ = xt[xi[0] % 3]
                xb_ = buf("xt%d" % (xi[0] % 3))
                h_ = hb[xi[0] % 2]
                hb_ = buf("hb%d" % (xi[0] % 2))
                xi[0] += 1
                src = G["xs"][t * 128:(t + 1) * 128, :] if samp else G["xp"][t0 + t * 128:t0 + (t + 1) * 128, :]
                rec.dma("sp", lambda e, x_=x_, src=src: e.dma_start(out=x_[:], in_=src), W=[xb_])
                norm_and_transpose(G, x_, xb_, t, h_, hb_, hT, buf("hT"), t * 128)
            if KSTOP < 2:
                return
            for t in range(ntile):
                r0 = t0 + t * 128
                p, pb = tm(1024, 512, t)
                rec.op("act", lambda e, p=p, t=t: e.activation(out=vtok[:, t, :], in_=p[:, 0:512], func=AF.Copy), R=[pb], W=[buf("vtok")])
                p, pb = tm(2560, 512, t)
                kk = ktm[t % 2]
                kkb = buf("ktm%d" % (t % 2))
                rec.op("act", lambda e, p=p, kk=kk: e.activation(out=kk[:], in_=p[:, 0:512], func=AF.Copy), R=[pb], W=[kkb])
                dst = G["fk_s"][t * 128:(t + 1) * 128, :] if samp else G["fk_p"][r0:r0 + 128, :]
                rec.dma("pool", lambda e, kk=kk, dst=dst: e.dma_start(out=dst, in_=kk[:]), R=[kkb], final=True)
                p, pb = tm(3072, 512, t)
                vv = vtm[t % 2]
                vvb = buf("vtm%d" % (t % 2))
                rec.op("act", lambda e, p=p, vv=vv: e.activation(out=vv[:], in_=p[:, 0:512], func=AF.Copy), R=[pb], W=[vvb])
                rec.op("dve", lambda e, p=p, t=t: e.tensor_copy(out=vbf[:, t, :], in_=p[:, 0:512]), R=[pb], W=[buf("vbf")])
                dst = G["fv_s"][t * 128:(t + 1) * 128, :] if samp else G["fv_p"][r0:r0 + 128, :]
                rec.dma("pool", lambda e, vv=vv, dst=dst: e.dma_start(out=dst, in_=vv[:]), R=[vvb], final=True)
                if KSUB < 1:
                    continue
                p, pb = tm(4096, 4, t)
                rec.op("dve", lambda e, p=p, t=t: e.tensor_tensor(out=lfb[:, t, :], in0=p[:, 0:4], in1=G["bfx"][:], op=ALU.add),
                       R=[pb, buf("bfx")], W=[buf("lfb")])
                rec.op("act", lambda e, t=t: e.activation(out=lfb[:, t, :], in_=lfb[:, t, :], func=AF.Sigmoid), R=[buf("lfb")], W=[buf("lfb")])
                rec.op("act", lambda e, t=t: e.activation(out=lfb[:, t, :], in_=lfb[:, t, :], func=AF.Ln), R=[buf("lfb")], W=[buf("lfb")])
                if KSUB < 2:
                    continue
                trim = G["bt32"] if samp else G["tri"]
                rec.op("pe", lambda e, t=t, trim=trim: e.matmul(out=pX[:, 0:4], lhsT=trim, rhs=lfb[:, t, :], start=True, stop=True),
                       R=[buf("cst"), buf("lfb")], W=[buf("pX")])
                if samp:
                    rec.op("dve", lambda e, t=t: e.tensor_scalar(out=ncb[:, t, :], in0=pX[:, 0:4], scalar1=-1.0, scalar2=None, op0=ALU.mult),
                           R=[buf("pX")], W=[buf("ncb")])
                else:
                    rec.op("dve", lambda e, t=t: e.scalar_tensor_tensor(out=ncb[:, t, :], in0=pX[:, 0:4], scalar=-1.0, in1=tot[:], op0=ALU.mult, op1=ALU.subtract),
                           R=[buf("pX"), buf("tot")], W=[buf("ncb")])
                    rec.op("pe", lambda e, t=t: e.matmul(out=pX[:, 8:12], lhsT=G["ones"], rhs=lfb[:, t, :], start=True, stop=True),
                           R=[buf("cst"), buf("lfb")], W=[buf("pX")])
                    rec.op("dve", lambda e: e.tensor_tensor(out=tot[:], in0=tot[:], in1=pX[:, 8:12], op=ALU.add),
                           R=[buf("pX"), buf("tot")], W=[buf("tot")])
            if KSUB < 3:
                return
            dst = (G["flf_s"] if samp else G["flf_p"][t0:t0 + N, :]).rearrange("(t p) h -> p t h", p=128)
            rec.dma("pool", lambda e, dst=dst: e.dma_start(out=dst, in_=lfb[:, 0:ntile, :]), R=[buf("lfb")], final=True)
            rec.dma("pool", lambda e: e.dma_start(out=G["negc"][t0:t0 + N, :].rearrange("(t p) h -> p t h", p=128), in_=ncb[:, 0:ntile, :]),
                    R=[buf("ncb")], W=[buf("negc_d")])
            rec.dma("pool", lambda e: e.dma_start(out=G["vbs"][t0:t0 + N, :].rearrange("(t p) c -> p t c", p=128), in_=vbf[:, 0:ntile, :]),
                    R=[buf("vbf")], W=[buf("vbs_d")])
            if KSTOP < 3:
                return
            for h in range(4):
                p, pb = fm(2048 + 128 * h, N, hT)
                rec.op("act", lambda e, p=p, h=h: e.activation(out=qb[:, h, 0:N], in_=p[:, 0:N], func=AF.Copy), R=[pb], W=[buf("qb")])
                p, pb = fm(2560 + 128 * h, N, hT)
                rec.op("dve", lambda e, p=p, h=h: e.tensor_copy(out=kb[:, h, 0:N], in_=p[:, 0:N]), R=[pb], W=[buf("kb")])
                p, pb = fm(3584 + 128 * h, N, hT)
                rec.op("act", lambda e, p=p, h=h: e.activation(out=sgb[:, h, 0:N], in_=p[:, 0:N], func=AF.Silu), R=[pb], W=[buf("sgb")])
            for (src_t, srcn, dstT) in ((qb, "qb", G["qbT"]), (kb, "kb", G["kbT"]), (sgb, "sgb", G["sgbT"])):
                rec.dma("pool", lambda e, src_t=src_t, dstT=dstT: e.dma_start(out=dstT[:, :, t0:t0 + N].rearrange("h p n -> p h n"), in_=src_t[:, :, 0:N]),
                        R=[buf(srcn)], W=[buf(srcn + "T_d")])
            if KSTOP < 4:
                return
            for h in range(4):
                T_ = tmp
                p, pb = fm(512 + 128 * h, N, hT)
                rec.op("act", lambda e, p=p: e.activation(out=T_["sig"][:, 0:N], in_=p[:, 0:N], func=AF.Sigmoid), R=[pb], W=[buf("t_sig")])
                rec.op("dve", lambda e, h=h: e.tensor_scalar(out=T_["f"][:, 0:N], in0=T_["sig"][:, 0:N], scalar1=omlt[:, h:h + 1], scalar2=lbt[:, h:h + 1],
                                                           op0=ALU.mult, op1=ALU.add), R=[buf("t_sig"), buf("omlt"), buf("lbt")], W=[buf("t_f")])
                rec.op("dve", lambda e, h=h: e.tensor_scalar(out=T_["omf"][:, 0:N], in0=T_["sig"][:, 0:N], scalar1=nomlt[:, h:h + 1], scalar2=omlt[:, h:h + 1],
                                                           op0=ALU.mult, op1=ALU.add), R=[buf("t_sig"), buf("omlt"), buf("nomlt")], W=[buf("t_omf")])
                rec.op("act", lambda e: e.activation(out=T_["lf"][:, 0:N], in_=T_["f"][:, 0:N], func=AF.Ln), R=[buf("t_f")], W=[buf("t_lf")])
                rmask = G["rst64"][:, 0:N]
                rec.op("dve", lambda e, rmask=rmask: e.tensor_tensor_scan(out=T_["b"][:, 0:N], data0=rmask, data1=T_["lf"][:, 0:N], initial=0.0,
                                                                         op0=ALU.mult, op1=ALU.add), R=[buf("t_lf"), buf("cst")], W=[buf("t_b")])
                b3 = T_["b"][:, 0:N].rearrange("p (c k) -> p c k", k=C)
                d3 = T_["d"][:, 0:N].rearrange("p (c k) -> p c k", k=C)
                rec.op("dve", lambda e, b3=b3, d3=d3: e.tensor_tensor(out=d3, in0=b3, in1=b3[:, :, mid:mid + 1].to_broadcast([128, nch, C]), op=ALU.subtract),
                       R=[buf("t_b")], W=[buf("t_d")])
                rec.op("act", lambda e: e.activation(out=T_["eq"][:, 0:N], in_=T_["d"][:, 0:N], func=AF.Exp), R=[buf("t_d")], W=[buf("t_eq")])
                rec.op("act", lambda e: e.activation(out=T_["ek"][:, 0:N], in_=T_["d"][:, 0:N], func=AF.Exp, scale=-1.0), R=[buf("t_d")], W=[buf("t_ek")])
                rec.op("act", lambda e, h=h, b3=b3: e.activation(out=sm[:, h, 0, 0:nch], in_=b3[:, :, mid], func=AF.Exp), R=[buf("t_b")], W=[buf("sm")])
                rec.op("act", lambda e, h=h, b3=b3: e.activation(out=sm[:, h, 2, 0:nch], in_=b3[:, :, last], func=AF.Exp), R=[buf("t_b")], W=[buf("sm")])
                rec.op("act", lambda e, h=h, d3=d3: e.activation(out=sm[:, h, 1, 0:nch], in_=d3[:, :, last], func=AF.Exp), R=[buf("t_d")], W=[buf("sm")])
                p, pb = fm(0 + 128 * h, N, hT)
                rec.op("dve", lambda e, p=p, h=h: e.tensor_tensor(out=qt[:, h, 0:N], in0=p[:, 0:N], in1=T_["eq"][:, 0:N], op=ALU.mult),
                       R=[pb, buf("t_eq")], W=[buf("qt")])
                rec.op("dve", lambda e, h=h: e.tensor_tensor(out=kt[:, h, 0:N], in0=T_["omf"][:, 0:N], in1=T_["ek"][:, 0:N], op=ALU.mult),
                       R=[buf("t_omf"), buf("t_ek")], W=[buf("kt")])
                p, pb = fm(1536 + 128 * h, N, hT)
                rec.op("act", lambda e, p=p, h=h: e.activation(out=sg[:, h, 0:N], in_=p[:, 0:N], func=AF.Silu), R=[pb], W=[buf("sg")])
                rec.op("dve", lambda e, h=h: e.tensor_tensor(out=kt2[:, 0:N].rearrange("p (c k) -> p c k", k=C), in0=kt[:, h, 0:N].rearrange("p (c k) -> p c k", k=C),
                                                           in1=sm[:, h, 1, 0:nch].unsqueeze(2).to_broadcast([128, nch, C]), op=ALU.mult), R=[buf("kt"), buf("sm")], W=[buf("kt2")])
                for t in range(ntile):
                    rec.op("pe", lambda e, h=h, t=t: e.transpose(out=pT[:, h * 128:(h + 1) * 128], in_=kt2[:, t * 128:(t + 1) * 128], identity=idb[:]),
                           R=[buf("kt2"), buf("idb")], W=[buf("pT")])
                    rec.op("act", lambda e, h=h, t=t: e.activation(out=ktok[:, t, h, :], in_=pT[:, h * 128:(h + 1) * 128], func=AF.Copy),
                           R=[buf("pT")], W=[buf("ktok")])
            if KSTOP < 5:
                return
            hg_state_in = G["st_h"]

            def cinfo(c):
                r0 = (c * C) % 128
                return slice(c * C, (c + 1) * C), (c * C) // 128, slice(r0, r0 + C)

            for h in range(4):
                Sb = buf("S%d" % h)
                at = AT[h % 2]
                atb = buf("AT%d" % (h % 2))
                sal = Sall[h % 2]
                salb = buf("Sall%d" % (h % 2))
                hs = slice(h * 128, (h + 1) * 128)
                for c in range(nch):
                    cols, t, rows = cinfo(c)
                    rec.op("pe", lambda e, h=h, cols=cols, rows=rows, t=t: e.matmul(out=pS[rows, t * 64:t * 64 + C], lhsT=kt[:, h, cols], rhs=qt[:, h, cols], start=True, stop=True),
                           R=[buf("kt"), buf("qt")], W=[buf("pS")])
                if KSUB < 2:
                    continue
                rec.op("dve", lambda e, at=at: e.tensor_tensor(out=at[:, 0:ntile, :], in0=pS[:, 0:ntile * 64].rearrange("p (a b) -> p a b", b=64),
                                                              in1=m64[:, :].unsqueeze(1).to_broadcast([128, ntile, 64]), op=ALU.mult),
                       R=[buf("pS"), buf("m64")], W=[atb])
                if KSUB < 3:
                    continue
                for c in range(nch):
                    cols, t, rows = cinfo(c)
                    po_, pob_ = (pO, buf("pO")) if c % 2 == 0 else (pA[0], buf("pA0"))
                    rec.op("pe", lambda e, hs=hs, rows=rows, t=t, at=at, c=c, po_=po_: e.matmul(out=po_[:, (c // 2) * 64:(c // 2) * 64 + C], lhsT=vtok[rows, t, hs], rhs=at[rows, t, :], start=True, stop=True),
                           R=[buf("vtok"), atb], W=[pob_])
                if KSUB < 4:
                    continue
                for c in range(nch):
                    cols, t, rows = cinfo(c)
                    pb_, pbb_ = (pP, buf("pP")) if c % 2 == 0 else (pX, buf("pX"))
                    rec.op("pe", lambda e, h=h, hs=hs, rows=rows, t=t, c=c, pb_=pb_: e.matmul(out=pb_[:, (c // 2) * 128:(c // 2 + 1) * 128], lhsT=ktok[rows, t, h, :], rhs=vtok[rows, t, hs], start=True, stop=True),
                           R=[buf("ktok"), buf("vtok")], W=[pbb_])
                if KSUB < 5:
                    continue
                for c in range(nch):
                    pb_, pbb_ = (pP, buf("pP")) if c % 2 == 0 else (pX, buf("pX"))
                    if samp:
                        rec.dma("sp", lambda e, h=h, c=c: e.dma_start(out=S[:, h, :], in_=hg_state_in[c, h]), W=[Sb])
                    rec.op("dve", lambda e, h=h, c=c, sal=sal: e.tensor_scalar(out=sal[:, c, :], in0=S[:, h, :], scalar1=sm[:, h, 0, c:c + 1], scalar2=None, op0=ALU.mult),
                           R=[Sb, buf("sm")], W=[salb])
                    rec.op("dve", lambda e, h=h, c=c, pb_=pb_: e.scalar_tensor_tensor(out=S[:, h, :], in0=S[:, h, :], scalar=sm[:, h, 2, c:c + 1], in1=pb_[:, (c // 2) * 128:(c // 2 + 1) * 128], op0=ALU.mult, op1=ALU.add),
                           R=[pbb_, Sb, buf("sm")], W=[Sb])
                    if samp:
                        rec.dma("pool", lambda e, h=h, c=c: e.dma_start(out=G["hg_s"][c, h], in_=S[:, h, :]), R=[Sb], final=True)
                if KSUB < 6:
                    continue
                for c in range(nch):
                    cols, t, rows = cinfo(c)
                    rec.op("pe", lambda e, h=h, cols=cols, c=c, sal=sal: e.matmul(out=pN[:, c * 64:c * 64 + C], lhsT=sal[:, c, :], rhs=qt[:, h, cols], start=True, stop=True),
                           R=[salb, buf("qt")], W=[buf("pN")])
                if KSUB < 7:
                    continue
                o4 = oT[:, h, 0:N].rearrange("p (a two k) -> p a two k", two=2, k=64)
                rec.op("act", lambda e, o4=o4: e.activation(out=o4[:, :, 0, :], in_=pO[:, 0:N // 2].rearrange("p (a k) -> p a k", k=64), func=AF.Copy), R=[buf("pO")], W=[buf("oT")])
                rec.op("act", lambda e, o4=o4: e.activation(out=o4[:, :, 1, :], in_=pA[0][:, 0:N // 2].rearrange("p (a k) -> p a k", k=64), func=AF.Copy), R=[buf("pA0")], W=[buf("oT")])
                rec.op("dve", lambda e, h=h: e.tensor_tensor(out=oT[:, h, 0:N], in0=oT[:, h, 0:N], in1=pN[:, 0:N], op=ALU.add), R=[buf("pN"), buf("oT")], W=[buf("oT")])
            if (not samp) and t0 + N == L:
                for h in range(4):
                    rec.dma("pool", lambda e, h=h: e.dma_start(out=G["hg_p"][h], in_=S[:, h, :]), R=[buf("S%d" % h)], final=True)
            if KSTOP < 6:
                return
            for h in range(4):
                rec.op("act", lambda e, h=h: e.activation(out=sq[:, 0:N], in_=oT[:, h, 0:N], func=AF.Square), R=[buf("oT")], W=[buf("sq")])
                rec.op("pe", lambda e: e.matmul(out=pN[:, 0:N], lhsT=onesb[:], rhs=sq[:, 0:N], start=True, stop=True), R=[buf("onesb"), buf("sq")], W=[buf("pN")])
                rec.op("act", lambda e: e.activation(out=lnt[:, 0:N], in_=pN[:, 0:N], func=AF.Ln, scale=1.0 / 128, bias=G["epsb"][:, 0:1]), R=[buf("pN"), buf("epsb")], W=[buf("lnt")])
                rec.op("act", lambda e: e.activation(out=lnt[:, 0:N], in_=lnt[:, 0:N], func=AF.Exp, scale=-0.5), R=[buf("lnt")], W=[buf("lnt")])
                rec.op("dve", lambda e, h=h: e.scalar_tensor_tensor(out=tg[:, 0:N], in0=lnt[:, 0:N], scalar=vec[:, 32 + h:33 + h], in1=sg[:, h, 0:N], op0=ALU.mult, op1=ALU.mult),
                       R=[buf("lnt"), buf("vec"), buf("sg")], W=[buf("tg")])
                rec.op("dve", lambda e, h=h: e.tensor_tensor(out=oa[:, h, 0:N], in0=oT[:, h, 0:N], in1=tg[:, 0:N], op=ALU.mult), R=[buf("oT"), buf("tg")], W=[buf("oa")])
            rec.dma("pool", lambda e: e.dma_start(out=G["oaT"][:, :, t0:t0 + N].rearrange("h p n -> p h n"), in_=oa[:, :, 0:N]), R=[buf("oa")], W=[buf("oaT_d")])
        for (t0_, N_, samp_) in groups:
            if KSTOP >= 1:
                do_group(t0_, N_, samp_)
        flush(k, rec)


def phase2(G):
    k, rec, nc = G["k"], G["rec"], G["nc"]
    buf = k.buf
    L, NPG = k.L, k.NPG
    NT = L // 128
    NG = L // 512
    pS, pP, pO, pN, pX = G["pS"], G["pP"], G["pO"], G["pN"], G["pX"]
    onesb, cst = G["onesb"], G["cst"]
    KW = max(L, NPG * 128 + 128)
    NB = max(NT, NPG + 1)
    with ExitStack() as ph:
        sb = lambda n, s, d: ph.enter_context(nc.sbuf_tensor(n, s, d))
        kT = sb("kT", [128, 4, KW], BF16)
        V = sb("V", [128, NB, 512], BF16)
        ngs = sb("ngs", [128, NB, 4], F32)
        biasT = [sb("biasT%d" % i, [128, 4, NB], F32) for i in range(2)]
        qb = [sb("qb2_%d" % i, [128, 4, 512], BF16) for i in range(2)]
        sgb = [sb("sgb2_%d" % i, [128, 4, 512], BF16) for i in range(2)]
        ob = [sb("ob%d" % i, [128, 4, 512], BF16) for i in range(2)]
        PT = [sb("PT%d" % i, [128, 512], BF16) for i in range(2)]
        trib = sb("trib", [128, 128], BF16)
        m64b = sb("m64b", [128, 64], BF16)
        nq = sb("nq", [128, 4], F32)
        rl = sb("rl", [128, 512], F32)
        tg2 = sb("tg2", [128, 512], F32)
        rec.op("dve", lambda e: e.tensor_copy(out=trib[:], in_=G["tri"]), R=[buf("cst")], W=[buf("trib")])
        rec.op("dve", lambda e: e.tensor_copy(out=m64b[:], in_=cst[:, 256:320]), R=[buf("cst")], W=[buf("m64b")])
        for h in range(4):
            rec.dma("sp", lambda e, h=h: e.dma_start(out=kT[:, h, 0:L], in_=G["kbT"][h, :, 0:L]), R=[buf("kbT_d")], W=[buf("kT")])
        rec.dma("sp", lambda e: e.dma_start(out=V[:, 0:NT, :], in_=G["vbs"][0:L, :].rearrange("(t p) c -> p t c", p=128)), R=[buf("vbs_d")], W=[buf("V")])
        rec.dma("sp", lambda e: e.dma_start(out=ngs[:, 0:NT, :], in_=G["negc"][0:L, :].rearrange("(t p) h -> p t h", p=128)), R=[buf("negc_d")], W=[buf("ngs")])

        Pacc = [sb("Pacc%d" % i, [128, 512], F32) for i in range(2)]
        acc_i = [0]

        def attend(h, qt_, qtb, NQ, blist, bT, bTb):
            nbk = len(blist)
            assert blist[0][1] == 128 and blist[0][4] == 0
            for bi, (kc0, ks, vt, bj, c0, diag) in enumerate(blist):
                n = NQ - c0
                pb = (pS, pP)[bi % 2]
                pbb = buf(("pS", "pP")[bi % 2])
                P_ = PT[bi % 2]
                Pb = buf("PT%d" % (bi % 2))
                rec.op("pe", lambda e, pb=pb, kc0=kc0, ks=ks, c0=c0, n=n: e.matmul(out=pb[0:ks, 0:n], lhsT=kT[:, h, kc0:kc0 + ks], rhs=qt_[:, h, c0:NQ], start=True, stop=True),
                       R=[buf("kT"), qtb], W=[pbb])
                rec.op("act", lambda e, pb=pb, P_=P_, ks=ks, n=n, bj=bj: e.activation(out=P_[0:ks, 0:n], in_=pb[0:ks, 0:n], func=AF.Exp, scale=SCALE, bias=bT[0:ks, h, bj:bj + 1]),
                       R=[pbb, bTb], W=[Pb])
                if diag:
                    mk = trib if ks == 128 else m64b
                    rec.op("dve", lambda e, P_=P_, ks=ks, mk=mk: e.tensor_tensor(out=P_[0:ks, 0:ks], in0=P_[0:ks, 0:ks], in1=mk[0:ks, 0:ks], op=ALU.mult),
                           R=[Pb, buf("trib"), buf("m64b")], W=[Pb])
                rec.op("pe", lambda e, P_=P_, ks=ks, vt=vt, c0=c0, n=n, bi=bi: e.matmul(out=pO[:, c0:NQ], lhsT=V[0:ks, vt, h * 128:(h + 1) * 128], rhs=P_[0:ks, 0:n], start=(bi == 0), stop=(bi == nbk - 1)),
                       R=[buf("V"), Pb], W=[buf("pO")])
                pacc, paccb = Pacc[acc_i[0] % 2], buf("Pacc%d" % (acc_i[0] % 2))
                if bi == 0:
                    rec.op("dve", lambda e, P_=P_, pacc=pacc: e.tensor_copy(out=pacc[:, 0:NQ], in_=P_[:, 0:NQ]), R=[Pb], W=[paccb])
                else:
                    rec.op("dve", lambda e, P_=P_, pacc=pacc, ks=ks, c0=c0, n=n: e.tensor_tensor(out=pacc[0:ks, c0:NQ], in0=pacc[0:ks, c0:NQ], in1=P_[0:ks, 0:n], op=ALU.add),
                           R=[Pb, paccb], W=[paccb])
            pacc, paccb = Pacc[acc_i[0] % 2], buf("Pacc%d" % (acc_i[0] % 2))
            acc_i[0] += 1
            rec.op("pe", lambda e, pacc=pacc: e.matmul(out=pN[:, 0:NQ], lhsT=G["ones"], rhs=pacc[:, 0:NQ], start=True, stop=True), R=[buf("cst"), paccb], W=[buf("pN")])

        def epilogue(h, NQ, sg_, sgbb, ob_, obb):
            rec.op("dve", lambda e: e.reciprocal(out=rl[:, 0:NQ], in_=pN[:, 0:NQ]), R=[buf("pN")], W=[buf("rl")])
            rec.op("dve", lambda e: e.tensor_tensor(out=tg2[:, 0:NQ], in0=rl[:, 0:NQ], in1=sg_[:, h, 0:NQ], op=ALU.mult), R=[buf("rl"), sgbb], W=[buf("tg2")])
            rec.op("dve", lambda e: e.tensor_tensor(out=ob_[:, h, 0:NQ], in0=pO[:, 0:NQ], in1=tg2[:, 0:NQ], op=ALU.mult), R=[buf("pO"), buf("tg2")], W=[obb])

        def load_q(i, c0, n):
            q_, qbb = qb[i % 2], buf("qb2_%d" % (i % 2))
            s_, sbb = sgb[i % 2], buf("sgb2_%d" % (i % 2))
            rec.dma("sp", lambda e: e.dma_start(out=q_[:, :, 0:n], in_=G["qbT"][:, :, c0:c0 + n].rearrange("h p n -> p h n")), R=[buf("qbT_d")], W=[qbb])
            rec.dma("sp", lambda e: e.dma_start(out=s_[:, :, 0:n], in_=G["sgbT"][:, :, c0:c0 + n].rearrange("h p n -> p h n")), R=[buf("sgbT_d")], W=[sbb])
            return q_, qbb, s_, sbb

        def prompt_group(I):
            t0 = 512 * I
            q_, qbb, s_, sbb = load_q(I, t0, 512)
            o_, obb = ob[I % 2], buf("ob%d" % (I % 2))
            bT, bTb = biasT[I % 2], buf("biasT%d" % (I % 2))
            nb = 4 * I + 4
            rec.op("pe", lambda e: e.matmul(out=pX[:, 0:4], lhsT=G["ones"][0:1, :], rhs=ngs[0:1, 4 * I, :], start=True, stop=True), R=[buf("cst"), buf("ngs")], W=[buf("pX")])
            rec.op("act", lambda e: e.activation(out=nq[:], in_=pX[:, 0:4], func=AF.Copy), R=[buf("pX")], W=[buf("nq")])
            for h in range(4):
                rec.op("dve", lambda e, h=h: e.tensor_scalar(out=bT[:, h, 0:nb], in0=ngs[:, 0:nb, h], scalar1=nq[:, h:h + 1], scalar2=None, op0=ALU.subtract),
                       R=[buf("ngs"), buf("nq")], W=[bTb])
            for h in range(4):
                blist = []
                for j in range(nb):
                    r = j - 4 * I
                    blist.append((128 * j, 128, j, j, 128 * max(r, 0), r >= 0))
                attend(h, q_, qbb, 512, blist, bT, bTb)
                epilogue(h, 512, s_, sbb, o_, obb)
            rec.dma("pool", lambda e: e.dma_start(out=G["obT"][:, :, t0:t0 + 512].rearrange("h p n -> p h n"), in_=o_[:, :, :]), R=[obb], W=[buf("obT_d")])

        for I in range(NG):
            prompt_group(I)

        pti = sb("pti", [128, NPG], I32)
        ptc = sb("ptc", [128, 1], I32)
        idxf = sb("idxf", [128, NPG], F32)
        idx = sb("idx", [128, NPG], I32)
        lfp = sb("lfp", [128, 512], F32)
        cum = sb("cum", [128, 4, 128], F32)
        T4 = sb("T4", [128, 4], F32)
        TR = sb("TR", [128, 4], F32)
        sufp = sb("sufp", [128, 4, 128], F32)
        kst = [sb("kst%d" % i, [128, 1024], F32) for i in range(4)]
        kbf = [sb("kbf%d" % i, [128, 512], BF16) for i in range(2)]
        iot = cst[:, 1216:1217]
        ustr = cst[:, 1088:1216]

        def sample_seq(j):
            c0 = L + 64 * j
            rec.dma("sp", lambda e: e.dma_start(out=pti[:], in_=G["ptab"][j].partition_broadcast(128)), W=[buf("pti")])
            rec.dma("sp", lambda e: e.dma_start(out=ptc[0:NPG, :], in_=G["ptab"][j].rearrange("(n o) -> n o", o=1)), W=[buf("ptc")])
            rec.op("dve", lambda e: e.tensor_scalar(out=idxf[:], in0=pti[:], scalar1=128.0, scalar2=iot, op0=ALU.mult, op1=ALU.add), R=[buf("pti"), buf("cst")], W=[buf("idxf")])
            rec.op("dve", lambda e: e.tensor_copy(out=idx[:], in_=idxf[:]), R=[buf("idxf")], W=[buf("idx")])
            rec.dma("pool", lambda e: e.indirect_dma_start(out=lfp[0:NPG, :], out_offset=None, in_=G["clf"], in_offset=bass.IndirectOffsetOnAxis(ap=ptc[0:NPG, 0:1], axis=0)),
                    R=[buf("ptc")], W=[buf("lfp")])
            lf3 = lfp[0:NPG, :].rearrange("p (s h) -> p h s", h=4)
            bT, bTb = biasT[j % 2], buf("biasT%d" % (j % 2))
            for h in range(4):
                rec.op("dve", lambda e, h=h: e.tensor_tensor_scan(out=cum[0:NPG, h, :], data0=G["ones"][0:NPG, :], data1=lf3[:, h, :], initial=0.0, op0=ALU.mult, op1=ALU.add),
                       R=[buf("lfp"), buf("cst")], W=[buf("cum")])
            rec.op("dve", lambda e: e.tensor_copy(out=T4[0:NPG, :], in_=cum[0:NPG, :, 127]), R=[buf("cum")], W=[buf("T4")])
            rec.op("pe", lambda e: e.matmul(out=pX[0:NPG, 0:4], lhsT=ustr[0:NPG, 0:NPG], rhs=T4[0:NPG, :], start=True, stop=True), R=[buf("cst"), buf("T4")], W=[buf("pX")])
            rec.op("dve", lambda e: e.tensor_tensor(out=TR[0:NPG, :], in0=pX[0:NPG, 0:4], in1=T4[0:NPG, :], op=ALU.add), R=[buf("pX"), buf("T4")], W=[buf("TR")])
            for h in range(4):
                rec.op("dve", lambda e, h=h: e.tensor_scalar(out=sufp[0:NPG, h, :], in0=cum[0:NPG, h, :], scalar1=-1.0, scalar2=TR[0:NPG, h:h + 1], op0=ALU.mult, op1=ALU.add),
                       R=[buf("cum"), buf("TR")], W=[buf("sufp")])
                rec.op("pe", lambda e, h=h: e.transpose(out=pX[:, 0:NPG], in_=sufp[0:NPG, h, :], identity=G["ident"][0:NPG, 0:NPG]), R=[buf("sufp"), buf("cst")], W=[buf("pX")])
                rec.op("act", lambda e, h=h: e.activation(out=bT[:, h, 0:NPG], in_=pX[:, 0:NPG], func=AF.Copy), R=[buf("pX")], W=[bTb])
            rec.dma("sp", lambda e: e.dma_start(out=ngs[0:64, 0, :], in_=G["negc"][c0:c0 + 64, :]), R=[buf("negc_d")], W=[buf("ngs")])
            rec.op("dve", lambda e: e.tensor_copy(out=bT[0:64, :, NPG], in_=ngs[0:64, 0, :]), R=[buf("ngs")], W=[bTb])
            for i in range(NPG):
                ks_, ksb = kst[i % 4], buf("kst%d" % (i % 4))
                rec.dma("pool", lambda e, ks_=ks_, i=i: e.indirect_dma_start(out=ks_[:], out_offset=None, in_=G["ckv"], in_offset=bass.IndirectOffsetOnAxis(ap=idx[:, i:i + 1], axis=0)),
                        R=[buf("idx")], W=[ksb])
                kb_, kbb_ = kbf[i % 2], buf("kbf%d" % (i % 2))
                rec.op("dve", lambda e, ks_=ks_, kb_=kb_: e.tensor_copy(out=kb_[:], in_=ks_[:, 0:512]), R=[ksb], W=[kbb_])
                for h in range(4):
                    rec.op("pe", lambda e, kb_=kb_, h=h: e.transpose(out=G["pT"][:, h * 128:(h + 1) * 128], in_=kb_[:, h * 128:(h + 1) * 128], identity=G["idb"][:]), R=[kbb_, buf("idb")], W=[buf("pT")])
                rec.op("act", lambda e, i=i: e.activation(out=kT[:, :, 128 * i:128 * i + 128], in_=G["pT"][:, 0:512].rearrange("p (h s) -> p h s", h=4), func=AF.Copy), R=[buf("pT")], W=[buf("kT")])
                rec.op("dve", lambda e, ks_=ks_, i=i: e.tensor_copy(out=V[:, i, :], in_=ks_[:, 512:1024]), R=[ksb], W=[buf("V")])
            rec.dma("sp", lambda e: e.dma_start(out=kT[:, :, 128 * NPG:128 * NPG + 64], in_=G["kbT"][:, :, c0:c0 + 64].rearrange("h p n -> p h n")), R=[buf("kbT_d")], W=[buf("kT")])
            rec.dma("sp", lambda e: e.dma_start(out=V[0:64, NPG, :], in_=G["vbs"][c0:c0 + 64, :]), R=[buf("vbs_d")], W=[buf("V")])
            q_, qbb, s_, sbb = load_q(j, c0, 64)
            o_, obb = ob[j % 2], buf("ob%d" % (j % 2))
            for h in range(4):
                blist = [(128 * i, 128, i, i, 0, False) for i in range(NPG)] + [(128 * NPG, 64, NPG, NPG, 0, True)]
                attend(h, q_, qbb, 64, blist, bT, bTb)
                epilogue(h, 64, s_, sbb, o_, obb)
            rec.dma("pool", lambda e: e.dma_start(out=G["obT"][:, :, c0:c0 + 64].rearrange("h p n -> p h n"), in_=o_[:, :, 0:64]), R=[obb], W=[buf("obT_d")])

        for j in range(4):
            sample_seq(j)
        flush(k, rec)


def phase3(G):
    k, rec, nc = G["k"], G["rec"], G["nc"]
    buf = k.buf
    L = k.L
    NG = L // 512
    pA, pT, pS, pP, pO, pN, pX = G["pA"], G["pT"], G["pS"], G["pP"], G["pO"], G["pN"], G["pX"]
    vec, m64, onesb, idb = G["vec"], G["m64"], G["onesb"], G["idb"]
    with ExitStack() as ph:
        sb = lambda n, s, d: ph.enter_context(nc.sbuf_tensor(n, s, d))
        W1 = load_weight_bf16(G, ph, G["w_out_e"], 1024, None, "W1")
        W2 = load_weight_bf16(G, ph, G["w_in_o"], 3088, 8, "W2")
        W3 = load_weight_bf16(G, ph, G["w_out_o"], 1024, None, "W3")
        wgf = sb("wgf", [16, 512], F32)
        wg = sb("wg", [16, 512], BF16)
        rec.dma("sp", lambda e: e.dma_start(out=wgf[:], in_=G["w_gate"]), W=[buf("wgf")])
        rec.op("dve", lambda e: e.tensor_copy(out=wg[:], in_=wgf[:]), R=[buf("wgf")], W=[buf("wg")])
        fnb = sb("fnb", [128, D], F32)
        rec.dma("sp", lambda e: e.dma_start(out=fnb[:], in_=G["fnorm"].partition_broadcast(128)), W=[buf("fnb")])
        G["junk"] = sb("junk3", [128, 1024], BF16)
        G["ss"] = sb("ss3", [128, 8], F32)
        G["epsb"] = sb("epsb3", [128, 1], F32)
        rec.op("pool", lambda e: e.memset(G["epsb"][:], EPS), W=[buf("epsb")])
        ss2 = sb("ss2", [128, 4], F32)
        oc = sb("oc", [128, 8, 512], BF16)
        xt = [sb("x3_%d" % i, [128, D], F32) for i in range(2)]
        x1 = sb("x1", [128, 4, D], F32)
        hb = [sb("hb3_%d" % i, [128, D], BF16) for i in range(1)]
        hT = sb("hT3", [128, KC, 512], BF16)
        tmp = {n: sb("u_" + n, [128, 512], F32) for n in ("lf", "b", "d", "eq", "ek")}
        qt = sb("qt3", [128, 4, 512], BF16)
        kt = sb("kt3", [128, 4, 512], BF16)
        sm = sb("sm3", [128, 4, 3, 8], F32)
        vtok = sb("vtok3", [128, 4, 1024], BF16)
        ktok = sb("ktok3", [128, 4, 4, 128], BF16)
        AT = [sb("AT3_%d" % i, [128, 4, 64], BF16) for i in range(2)]
        S = sb("S3", [128, 4, 256], F32)
        Sall = sb("Sall3", [128, 8, 256], BF16)
        kt2 = sb("kt23", [128, 512], BF16)
        oT = sb("oT3", [128, 8, 512], F32)
        sq = G["junk"][:].rearrange("p (e n) -> p e n", e=2)
        lnt = sb("lnt3", [128, 512], F32)
        sg = sb("sg3", [128, 8, 512], BF16)
        tg = sb("tg3", [128, 512], F32)
        oc2 = oc
        rT = sb("rT", [16, 512], BF16)
        rec.op("pool", lambda e: e.memset(S[:], 0.0), W=[buf("G%d" % h) for h in range(4)])
        pa_i = [0]

        def nextp():
            i = pa_i[0] % 2
            pa_i[0] += 1
            return pA[i], buf("pA%d" % i)

        def fm(Wt, wname, col, M, N, src, srcb):
            p, pb = nextp()
            for kc in range(KC):
                rec.op("pe", lambda e, kc=kc: e.matmul(out=p[0:M, 0:N], lhsT=Wt[:, kc, col:col + M], rhs=src[:, kc, 0:N], start=(kc == 0), stop=(kc == KC - 1)),
                       R=[buf(wname), srcb], W=[pb])
            return p, pb

        def tm(Wt, wname, col, ncols, t, src, srcb):
            p, pb = nextp()
            for kc in range(KC):
                rec.op("pe", lambda e, kc=kc: e.matmul(out=p[:, 0:ncols], lhsT=src[:, kc, t * 128:(t + 1) * 128], rhs=Wt[:, kc, col:col + ncols], start=(kc == 0), stop=(kc == KC - 1)),
                       R=[buf(wname), srcb], W=[pb])
            return p, pb

        xi = [0]

        def do_group(t0, N, samp):
            ntile = N // 128
            C = 64
            nch = N // C
            mid = 1 if samp else 31
            last = 3 if samp else C - 1
            rec.dma("sp", lambda e: e.dma_start(out=oc[:, 0:4, 0:N], in_=G["oaT"][:, :, t0:t0 + N].rearrange("h p n -> p h n")), R=[buf("oaT_d")], W=[buf("oc")])
            rec.dma("sp", lambda e: e.dma_start(out=oc[:, 4:8, 0:N], in_=G["obT"][:, :, t0:t0 + N].rearrange("h p n -> p h n")), R=[buf("obT_d")], W=[buf("oc")])
            for t in range(ntile):
                x_ = xt[xi[0] % 2]
                xb_ = buf("x3_%d" % (xi[0] % 2))
                h_ = hb[0]
                hb_ = buf("hb3_0")
                xi[0] += 1
                src = G["xs"][t * 128:(t + 1) * 128, :] if samp else G["xp"][t0 + t * 128:t0 + (t + 1) * 128, :]
                rec.dma("sp", lambda e, x_=x_, src=src: e.dma_start(out=x_[:], in_=src), W=[xb_])
                for hf in range(2):
                    p, pb = tm(W1, "W1", hf * 512, 512, t, oc, buf("oc"))
                    rec.op("dve", lambda e, p=p, t=t, hf=hf, x_=x_: e.tensor_tensor(out=x1[:, t, hf * 512:(hf + 1) * 512], in0=p[:, 0:512], in1=x_[:, hf * 512:(hf + 1) * 512], op=ALU.add),
                           R=[pb, xb_], W=[buf("x1_%d" % t)])
                norm_and_transpose(G, x1[:, t, :], buf("x1_%d" % t), t, h_, hb_, hT, buf("hT3"), t * 128)
            for t in range(ntile):
                for hf in range(2):
                    p, pb = tm(W2, "W2", 1024 + hf * 512, 512, t, hT, buf("hT3"))
                    rec.op("act", lambda e, p=p, t=t, hf=hf: e.activation(out=vtok[:, t, hf * 512:(hf + 1) * 512], in_=p[:, 0:512], func=AF.Copy), R=[pb], W=[buf("vtok3")])
            p, pb = fm(W2, "W2", 3072, 16, N, hT, buf("hT3"))
            rec.op("act", lambda e, p=p: e.activation(out=rT[:, 0:N], in_=p[0:16, 0:N], func=AF.Copy), R=[pb], W=[buf("rT")])
            for c8 in range(8):
                p, pb = fm(W2, "W2", 2048 + 128 * c8, 128, N, hT, buf("hT3"))
                rec.op("act", lambda e, p=p, c8=c8: e.activation(out=sg[:, c8, 0:N], in_=p[:, 0:N], func=AF.Silu), R=[pb], W=[buf("sg3")])
            for h in range(4):
                T_ = tmp
                p, pb = nextp()
                rec.op("pe", lambda e, p=p, h=h: e.matmul(out=p[:, 0:N], lhsT=wg[0:16, h * 128:(h + 1) * 128], rhs=rT[0:16, 0:N], start=True, stop=True), R=[buf("wg"), buf("rT")], W=[pb])
                rec.op("act", lambda e, p=p, h=h: e.activation(out=T_["lf"][:, 0:N], in_=p[:, 0:N], func=AF.Sigmoid, bias=vec[:, 36 + h:37 + h]), R=[pb, buf("vec")], W=[buf("u_lf")])
                rec.op("act", lambda e: e.activation(out=T_["lf"][:, 0:N], in_=T_["lf"][:, 0:N], func=AF.Ln), R=[buf("u_lf")], W=[buf("u_lf")])
                rec.op("dve", lambda e: e.tensor_scalar(out=T_["lf"][:, 0:N], in0=T_["lf"][:, 0:N], scalar1=1.0 / 16.0, scalar2=None, op0=ALU.mult), R=[buf("u_lf")], W=[buf("u_lf")])
                rec.op("dve", lambda e: e.tensor_tensor_scan(out=T_["b"][:, 0:N], data0=G["rst64"][:, 0:N], data1=T_["lf"][:, 0:N], initial=0.0, op0=ALU.mult, op1=ALU.add),
                       R=[buf("u_lf"), buf("cst")], W=[buf("u_b")])
                b3 = T_["b"][:, 0:N].rearrange("p (c k) -> p c k", k=C)
                d3 = T_["d"][:, 0:N].rearrange("p (c k) -> p c k", k=C)
                rec.op("dve", lambda e, b3=b3, d3=d3: e.tensor_tensor(out=d3, in0=b3, in1=b3[:, :, mid:mid + 1].to_broadcast([128, nch, C]), op=ALU.subtract), R=[buf("u_b")], W=[buf("u_d")])
                rec.op("act", lambda e: e.activation(out=T_["eq"][:, 0:N], in_=T_["d"][:, 0:N], func=AF.Exp), R=[buf("u_d")], W=[buf("u_eq")])
                rec.op("act", lambda e: e.activation(out=T_["ek"][:, 0:N], in_=T_["d"][:, 0:N], func=AF.Exp, scale=-1.0), R=[buf("u_d")], W=[buf("u_ek")])
                rec.op("act", lambda e, h=h, b3=b3: e.activation(out=sm[:, h, 0, 0:nch], in_=b3[:, :, mid], func=AF.Exp), R=[buf("u_b")], W=[buf("sm3")])
                rec.op("act", lambda e, h=h, b3=b3: e.activation(out=sm[:, h, 2, 0:nch], in_=b3[:, :, last], func=AF.Exp), R=[buf("u_b")], W=[buf("sm3")])
                rec.op("act", lambda e, h=h, d3=d3: e.activation(out=sm[:, h, 1, 0:nch], in_=d3[:, :, last], func=AF.Exp), R=[buf("u_d")], W=[buf("sm3")])
                p, pb = fm(W2, "W2", 128 * h, 128, N, hT, buf("hT3"))
                rec.op("dve", lambda e, p=p, h=h: e.scalar_tensor_tensor(out=qt[:, h, 0:N], in0=p[:, 0:N], scalar=SCALE, in1=T_["eq"][:, 0:N], op0=ALU.mult, op1=ALU.mult),
                       R=[pb, buf("u_eq")], W=[buf("qt3")])
                p, pb = fm(W2, "W2", 512 + 128 * h, 128, N, hT, buf("hT3"))
                rec.op("dve", lambda e, p=p, h=h: e.tensor_tensor(out=kt[:, h, 0:N], in0=p[:, 0:N], in1=T_["ek"][:, 0:N], op=ALU.mult), R=[pb, buf("u_ek")], W=[buf("kt3")])
                rec.op("dve", lambda e, h=h: e.tensor_tensor(out=kt2[:, 0:N].rearrange("p (c k) -> p c k", k=C), in0=kt[:, h, 0:N].rearrange("p (c k) -> p c k", k=C),
                                                           in1=sm[:, h, 1, 0:nch].unsqueeze(2).to_broadcast([128, nch, C]), op=ALU.mult), R=[buf("kt3"), buf("sm3")], W=[buf("kt23")])
                for t in range(ntile):
                    rec.op("pe", lambda e, h=h, t=t: e.transpose(out=pT[:, h * 128:(h + 1) * 128], in_=kt2[:, t * 128:(t + 1) * 128], identity=idb[:]), R=[buf("kt23"), buf("idb")], W=[buf("pT")])
                    rec.op("act", lambda e, h=h, t=t: e.activation(out=ktok[:, t, h, :], in_=pT[:, h * 128:(h + 1) * 128], func=AF.Copy), R=[buf("pT")], W=[buf("ktok3")])

            def cinfo(c):
                r0 = (c * C) % 128
                return slice(c * C, (c + 1) * C), (c * C) // 128, slice(r0, r0 + C)

            for h in range(4):
                Sb = buf("G%d" % h)
                at = AT[h % 2]
                atb = buf("AT3_%d" % (h % 2))
                sal = Sall
                salb = buf("Sall3")
                for c in range(nch):
                    cols, t, rows = cinfo(c)
                    rec.op("pe", lambda e, h=h, cols=cols, rows=rows, t=t: e.matmul(out=pS[rows, t * 64:t * 64 + C], lhsT=kt[:, h, cols], rhs=qt[:, h, cols], start=True, stop=True),
                           R=[buf("kt3"), buf("qt3")], W=[buf("pS")])
                rec.op("dve", lambda e, at=at: e.tensor_tensor(out=at[:, 0:ntile, :], in0=pS[:, 0:ntile * 64].rearrange("p (a b) -> p a b", b=64),
                                                              in1=m64[:, :].unsqueeze(1).to_broadcast([128, ntile, 64]), op=ALU.mult),
                       R=[buf("pS"), buf("m64")], W=[atb])
                for c in range(nch):
                    cols, t, rows = cinfo(c)
                    pb_, pbb_ = (pP, buf("pP")) if c % 2 == 0 else (pX, buf("pX"))
                    rec.op("pe", lambda e, h=h, rows=rows, t=t, pb_=pb_: e.matmul(out=pb_[:, 0:256], lhsT=ktok[rows, t, h, :], rhs=vtok[rows, t, h * 256:(h + 1) * 256], start=True, stop=True),
                           R=[buf("ktok3"), buf("vtok3")], W=[pbb_])
                    if samp:
                        rec.dma("sp", lambda e, h=h, c=c: e.dma_start(out=S[:, h, :], in_=G["st_g"][c, h]), W=[Sb])
                    rec.op("dve", lambda e, h=h, c=c: e.tensor_scalar(out=sal[:, c, :], in0=S[:, h, :], scalar1=sm[:, h, 0, c:c + 1], scalar2=None, op0=ALU.mult),
                           R=[Sb, buf("sm3")], W=[salb])
                    rec.op("dve", lambda e, h=h, c=c, pb_=pb_: e.scalar_tensor_tensor(out=S[:, h, :], in0=S[:, h, :], scalar=sm[:, h, 2, c:c + 1], in1=pb_[:, 0:256], op0=ALU.mult, op1=ALU.add),
                           R=[pbb_, Sb, buf("sm3")], W=[Sb])
                    if samp:
                        rec.dma("pool", lambda e, h=h, c=c: e.dma_start(out=G["gl_s"][c, h], in_=S[:, h, :]), R=[Sb], final=True)
                for e2 in range(2):
                    vs = slice(h * 256 + e2 * 128, h * 256 + (e2 + 1) * 128)
                    for c in range(nch):
                        cols, t, rows = cinfo(c)
                        po_, pob_ = (pO, buf("pO")) if c % 2 == 0 else (pA[0], buf("pA0"))
                        rec.op("pe", lambda e, vs=vs, rows=rows, t=t, at=at, c=c, po_=po_: e.matmul(out=po_[:, (c // 2) * 64:(c // 2) * 64 + C], lhsT=vtok[rows, t, vs], rhs=at[rows, t, :], start=True, stop=True),
                               R=[buf("vtok3"), atb], W=[pob_])
                    for c in range(nch):
                        cols, t, rows = cinfo(c)
                        rec.op("pe", lambda e, h=h, cols=cols, c=c, e2=e2: e.matmul(out=pN[:, c * 64:c * 64 + C], lhsT=sal[:, c, e2 * 128:(e2 + 1) * 128], rhs=qt[:, h, cols], start=True, stop=True),
                               R=[salb, buf("qt3")], W=[buf("pN")])
                    o4 = oT[:, 2 * h + e2, 0:N].rearrange("p (a two k) -> p a two k", two=2, k=64)
                    rec.op("act", lambda e, o4=o4: e.activation(out=o4[:, :, 0, :], in_=pO[:, 0:N // 2].rearrange("p (a k) -> p a k", k=64), func=AF.Copy), R=[buf("pO")], W=[buf("oT3")])
                    rec.op("act", lambda e, o4=o4: e.activation(out=o4[:, :, 1, :], in_=pA[0][:, 0:N // 2].rearrange("p (a k) -> p a k", k=64), func=AF.Copy), R=[buf("pA0")], W=[buf("oT3")])
                    rec.op("dve", lambda e, h=h, e2=e2: e.tensor_tensor(out=oT[:, 2 * h + e2, 0:N], in0=oT[:, 2 * h + e2, 0:N], in1=pN[:, 0:N], op=ALU.add), R=[buf("pN"), buf("oT3")], W=[buf("oT3")])
            if (not samp) and t0 + N == L:
                for h in range(4):
                    rec.dma("pool", lambda e, h=h: e.dma_start(out=G["gl_p"][h], in_=S[:, h, :]), R=[buf("G%d" % h)], final=True)
            for h in range(4):
                for e2 in range(2):
                    rec.op("act", lambda e, h=h, e2=e2: e.activation(out=sq[:, e2, 0:N], in_=oT[:, 2 * h + e2, 0:N], func=AF.Square), R=[buf("oT3")], W=[buf("junk")])
                for e2 in range(2):
                    rec.op("pe", lambda e, e2=e2: e.matmul(out=pN[:, 0:N], lhsT=onesb[:], rhs=sq[:, e2, 0:N], start=(e2 == 0), stop=(e2 == 1)), R=[buf("onesb"), buf("junk")], W=[buf("pN")])
                rec.op("act", lambda e: e.activation(out=lnt[:, 0:N], in_=pN[:, 0:N], func=AF.Ln, scale=1.0 / 256, bias=G["epsb"][:, 0:1]), R=[buf("pN"), buf("epsb")], W=[buf("lnt3")])
                rec.op("act", lambda e: e.activation(out=lnt[:, 0:N], in_=lnt[:, 0:N], func=AF.Exp, scale=-0.5), R=[buf("lnt3")], W=[buf("lnt3")])
                for e2 in range(2):
                    c8 = 2 * h + e2
                    rec.op("dve", lambda e, c8=c8: e.scalar_tensor_tensor(out=tg[:, 0:N], in0=lnt[:, 0:N], scalar=vec[:, 40 + c8:41 + c8], in1=sg[:, c8, 0:N], op0=ALU.mult, op1=ALU.mult),
                           R=[buf("lnt3"), buf("vec"), buf("sg3")], W=[buf("tg3")])
                    rec.op("dve", lambda e, c8=c8: e.tensor_tensor(out=oc2[:, c8, 0:N], in0=oT[:, c8, 0:N], in1=tg[:, 0:N], op=ALU.mult), R=[buf("oT3"), buf("tg3")], W=[buf("oc")])
            for t in range(ntile):
                xb1 = buf("x1_%d" % t)
                for hf in range(2):
                    p, pb = tm(W3, "W3", hf * 512, 512, t, oc2, buf("oc"))
                    rec.op("dve", lambda e, p=p, t=t, hf=hf: e.tensor_tensor(out=x1[:, t, hf * 512:(hf + 1) * 512], in0=p[:, 0:512], in1=x1[:, t, hf * 512:(hf + 1) * 512], op=ALU.add), R=[pb, xb1], W=[xb1])
                rec.op("act", lambda e, t=t: e.activation(out=G["junk"][:], in_=x1[:, t, :], func=AF.Square, accum_out=ss2[:, t:t + 1]), R=[xb1], W=[buf("junk"), buf("ss2")])
                rec.op("act", lambda e, t=t: e.activation(out=ss2[:, t:t + 1], in_=ss2[:, t:t + 1], func=AF.Ln, scale=1.0 / D, bias=G["epsb"][:, 0:1]), R=[buf("ss2"), buf("epsb")], W=[buf("ss2")])
                rec.op("act", lambda e, t=t: e.activation(out=ss2[:, t:t + 1], in_=ss2[:, t:t + 1], func=AF.Exp, scale=-0.5), R=[buf("ss2")], W=[buf("ss2")])
                rec.op("dve", lambda e, t=t: e.scalar_tensor_tensor(out=x1[:, t, :], in0=x1[:, t, :], scalar=ss2[:, t:t + 1], in1=fnb[:], op0=ALU.mult, op1=ALU.mult), R=[xb1, buf("ss2"), buf("fnb")], W=[xb1])
                dst = G["y_s"][t * 128:(t + 1) * 128, :] if samp else G["y_p"][t0 + t * 128:t0 + (t + 1) * 128, :]
                rec.dma("pool", lambda e, t=t, dst=dst: e.dma_start(out=dst, in_=x1[:, t, :]), R=[xb1], final=True)

        groups = [(g * 512, 512, False) for g in range(NG)] + [(L, 256, True)]
        for (t0_, N_, samp_) in groups:
            do_group(t0_, N_, samp_)
        flush(k, rec)


def make_consts():
    c = np.zeros((128, 1280), np.float32)
    p = np.arange(128)[:, None]
    j = np.arange(128)[None, :]
    c[:, 0:128] = np.eye(128)
    c[:, 128:256] = (p <= j)
    j64 = np.arange(64)[None, :]
    c[:, 256:320] = ((p % 64) <= j64)
    c[:, 320:448] = 1.0
    c[:, 448:576] = ((p // 64) == (j // 64)) & (p <= j)
    j512 = np.arange(512)[None, :]
    c[:, 576:1088] = (j512 % 64 != 0) * np.ones((128, 1))
    c[:, 1088:1216] = (p > j)
    c[:, 1216] = np.arange(128)
    return c

def pack_vecs(inp):
    v = np.zeros((128, 64), np.float32)
    f = lambda a, n: np.asarray(a, np.float32).reshape(n, 128).T
    v[:, 0:8] = f(inp['norm_even'][0], 8)
    v[:, 8:16] = f(inp['norm_odd'][0], 8)
    v[:, 16:24] = f(inp['final_norm'], 8)
    v[:, 24:28] = f(inp['lb_logits'][0], 4)
    v[:, 28:32] = f(inp['lb_logits'][1], 4)
    v[:, 32:36] = f(inp['hgrn_gain'][0], 4)
    v[:, 36:40] = f(inp['b_gla_gate'][0], 4)
    v[:, 40:48] = f(inp['gla_gain'][0], 8)
    return v

def make_ckv(inp):
    pool = inp['cache_fox_k'].shape[1]
    return np.concatenate([np.asarray(inp['cache_fox_k'][0]).reshape(pool * 128, 512),
                           np.asarray(inp['cache_fox_v'][0]).reshape(pool * 128, 512)], axis=1)


def core_inputs(inp, c, L, NPG, ckv=None):
    b = c // 4
    m = {}
    m['xp'] = np.ascontiguousarray(inp['x_prompt'][b, :L])
    xs = np.zeros((256, 1024), np.float32)
    for j in range(4):
        xs[64 * j:64 * j + 4] = inp['x_sample'][4 * c + j]
    m['xs'] = xs
    m['w_in_even'] = np.ascontiguousarray(inp['w_in_even'][0])
    m['w_out_even'] = np.ascontiguousarray(inp['w_out_even'][0])
    m['w_in_odd'] = np.ascontiguousarray(inp['w_in_odd'][0])
    m['w_out_odd'] = np.ascontiguousarray(inp['w_out_odd'][0])
    m['w_gla_gate'] = np.ascontiguousarray(inp['w_gla_gate'][0])
    m['vecs'] = pack_vecs(inp)
    m['fnorm'] = np.ascontiguousarray(np.asarray(inp['final_norm'], np.float32))
    m['bfox'] = np.ascontiguousarray(np.broadcast_to(np.asarray(inp['b_fox_f'][0], np.float32)[None, :], (128, 4)))
    m['consts'] = make_consts()
    m['st_hgrn'] = np.ascontiguousarray(inp['state_hgrn'][0, 4 * c:4 * c + 4])
    m['st_gla'] = np.ascontiguousarray(inp['state_gla'][0, 4 * c:4 * c + 4])
    m['ptab'] = np.ascontiguousarray(inp['page_table'][4 * c:4 * c + 4, :NPG]).astype(np.int32)
    pool = inp['cache_fox_k'].shape[1]
    m['cache_kv'] = ckv if ckv is not None else make_ckv(inp)
    m['cache_lf'] = np.asarray(inp['cache_fox_logf'][0]).reshape(pool, 512)
    return m


def assemble(results, L):
    f = lambda a: np.asarray(a, np.float32)
    idx = np.concatenate([np.arange(64 * j, 64 * j + 4) for j in range(4)])
    pc = [0, 4]
    y_p = np.stack([f(results[c]['y_p']) for c in pc])
    y_s = np.concatenate([f(results[c]['y_s'])[idx] for c in range(8)]).reshape(32, 4, 1024)
    npg = L // 128
    fk_p = np.stack([f(results[c]['fk_p']) for c in pc]).reshape(1, 2, npg, 128, 4, 128)
    fv_p = np.stack([f(results[c]['fv_p']) for c in pc]).reshape(1, 2, npg, 128, 4, 128)
    flf_p = np.stack([f(results[c]['flf_p']) for c in pc]).reshape(1, 2, npg, 128, 4)
    hg_p = np.stack([f(results[c]['hg_p']) for c in pc])[None]
    gl_p = np.stack([f(results[c]['gl_p']) for c in pc])[None]
    fk_s = np.concatenate([f(results[c]['fk_s'])[idx] for c in range(8)]).reshape(1, 32, 4, 4, 128)
    fv_s = np.concatenate([f(results[c]['fv_s'])[idx] for c in range(8)]).reshape(1, 32, 4, 4, 128)
    flf_s = np.concatenate([f(results[c]['flf_s'])[idx] for c in range(8)]).reshape(1, 32, 4, 4)
    hg_s = np.concatenate([f(results[c]['hg_s']) for c in range(8)])[None]
    gl_s = np.concatenate([f(results[c]['gl_s']) for c in range(8)])[None]
    return (y_p, y_s, fk_p, fv_p, flf_p, hg_p, gl_p, fk_s, fv_s, flf_s, hg_s, gl_s)


def kernel(**inputs):
    inp = {k_: np.asarray(v) for k_, v in inputs.items()}
    L = inp['x_prompt'].shape[1]
    NPG = inp['page_table'].shape[1]
    POOL = inp['cache_fox_k'].shape[1]
    nc = build(L, NPG, POOL, dbg=False, phases=(1, 2, 3))
    ckv = make_ckv(inp)
    maps = [core_inputs(inp, c, L, NPG, ckv) for c in range(8)]
    res = run_bass_kernel_spmd(nc, maps, core_ids=list(range(8)))
    return assemble(res.results, L)
```

```python
import numpy as np
from concourse.bass_utils import run_bass_kernel_spmd
from contextlib import ExitStack
import concourse.bass as bass
import concourse.mybir as mybir

F32 = mybir.dt.float32
BF16 = mybir.dt.bfloat16
I32 = mybir.dt.int32
U32 = mybir.dt.uint32
AF = mybir.ActivationFunctionType
ALU = mybir.AluOpType
AX = mybir.AxisListType


class Buf:
    __slots__ = ("w", "r", "name", "excl")

    def __init__(self, name=""):
        self.excl = False
        self.w = None
        self.r = {}
        self.name = name


class Eng:
    def __init__(self, name):
        self.name = name
        self.prog = []
        self.count = 0
        self.waited = {}


NDS = 12


class Rec:
    COMPUTE = ("pe", "act", "dve", "pool")

    def __init__(self, nc, stack):
        self.nc = nc
        self.e = {n: Eng(n) for n in ("pe", "act", "dve", "pool", "sp")}
        self.sems = {}
        for n in self.COMPUTE:
            self.sems[n] = stack.enter_context(nc.semaphore("s_" + n))
        self.dq = {}
        for q in ("sp", "pool", "act"):
            sl = []
            for i in range(NDS):
                key = "d_%s_%d" % (q, i)
                self.sems[key] = stack.enter_context(nc.semaphore(key))
                sl.append(key)
            self.dq[q] = dict(slots=sl, uses=[0] * NDS, n=0)
        self.final = []

    def _need(self, eng, deps):
        for k, v in deps.items():
            if eng.waited.get(k, 0) < v:
                eng.waited[k] = v
                eng.prog.append(("wait", k, v))

    def _collect(self, ename, R, W):
        deps = {}

        def add(d, kind):
            if d is None:
                return
            k, v, en = d
            if en == ename and ename in self.COMPUTE:
                if ename == "pe":
                    return
                if kind == "war":
                    return
            if deps.get(k, 0) < v:
                deps[k] = v

        for b in R:
            add(b.w, "raw")
        for b in W:
            add(b.w, "waw")
            for k, (v, en) in b.r.items():
                add((k, v, en), "war")
        return deps

    def _mark(self, tok, R, W):
        k, v, en = tok
        for b in R:
            b.r[k] = (v, en)
        for b in W:
            b.w = tok
            b.r = {}

    def op(self, ename, fn, R=(), W=()):
        eng = self.e[ename]
        deps = self._collect(ename, R, W)
        for b in R:
            if b.excl:
                for k2, (v2, en2) in b.r.items():
                    if en2 != ename and deps.get(k2, 0) < v2:
                        deps[k2] = v2
        self._need(eng, deps)
        eng.count += 1
        eng.prog.append(("op", fn, ename, 1))
        self._mark((ename, eng.count, ename), R, W)

    def dma(self, q, fn, R=(), W=(), final=False):
        eng = self.e[q]
        dq = self.dq[q]
        slot = dq["n"] % NDS
        dq["n"] += 1
        key = dq["slots"][slot]
        if dq["uses"][slot] > 0:
            self._need(eng, {key: 16 * dq["uses"][slot]})
        dq["uses"][slot] += 1
        val = 16 * dq["uses"][slot]
        deps = self._collect("dma_" + q, R, W)
        self._need(eng, deps)
        eng.prog.append(("op", fn, key, 16))
        self._mark((key, val, "dma_" + q), R, W)
        if final:
            self.final.append((key, val))

    def finish(self):
        eng = self.e["sp"]
        last = {}
        for k, v in self.final:
            last[k] = max(last.get(k, 0), v)
        for k, v in last.items():
            eng.prog.append(("wait", k, v))
        for q in ("pool", "act"):
            dq = self.dq[q]
            for i, u in enumerate(dq["uses"]):
                if u:
                    self.e[q].prog.append(("wait", dq["slots"][i], 16 * u))

    def replay(self, block):
        nc = self.nc
        sems = self.sems

        def run(engobj, prog):
            for it in prog:
                if it[0] == "wait":
                    engobj.wait_ge(sems[it[1]], it[2])
                else:
                    _, fn, key, inc = it
                    ins = fn(engobj)
                    ins.then_inc(sems[key], inc)

        @block.sync
        def _(e):
            run(e, self.e["sp"].prog)

        @block.tensor
        def _(e):
            run(e, self.e["pe"].prog)

        @block.scalar
        def _(e):
            run(e, self.e["act"].prog)

        @block.vector
        def _(e):
            run(e, self.e["dve"].prog)

        @block.gpsimd
        def _(e):
            run(e, self.e["pool"].prog)


import os
KSTOP = int(os.environ.get('KSTOP', '99'))
KSUB = int(os.environ.get('KSUB', '99'))

D = 1024
KC = 8
EPS = 1e-6
SCALE = 128 ** -0.5


class K:
    def __init__(self, L, NPG, POOL, dbg=False):
        self.L, self.NPG, self.POOL, self.dbg = L, NPG, POOL, dbg
        self.nc = bass.Bass("TRN2", target_bir_lowering=False)
        self.B = {}

    def buf(self, name):
        if name not in self.B:
            self.B[name] = Buf(name)
        return self.B[name]

    def din(self, name, shape, dt=F32):
        return self.nc.dram_tensor(name, list(shape), dt, kind="ExternalInput").ap()

    def dout(self, name, shape, dt=F32):
        return self.nc.dram_tensor(name, list(shape), dt, kind="ExternalOutput").ap()

    def dscr(self, name, shape, dt):
        return self.nc.dram_tensor(name, list(shape), dt, kind="Internal").ap()


def barrier(rec):
    tgt = {}
    for n in Rec.COMPUTE:
        if rec.e[n].count:
            tgt[n] = rec.e[n].count
    for q, dq in rec.dq.items():
        for i, u in enumerate(dq["uses"]):
            if u:
                tgt[dq["slots"][i]] = 16 * u
    for n in ("pe", "act", "dve", "pool", "sp"):
        d = {k: v for k, v in tgt.items() if k != n}
        rec._need(rec.e[n], d)


def flush(k, rec):
    barrier(rec)
    with k.nc.Block() as block:
        rec.replay(block)
    for e in rec.e.values():
        e.prog = []


def build(L, NPG, POOL, dbg=False, phases=(1, 2, 3)):
    k = K(L, NPG, POOL, dbg)
    nc = k.nc
    NG = L // 512
    NT = L // 128
    buf = k.buf
    xp = k.din("xp", [L, D])
    xs = k.din("xs", [256, D])
    w_in_e = k.din("w_in_even", [D, 4100])
    w_out_e = k.din("w_out_even", [D, D])
    w_in_o = k.din("w_in_odd", [D, 3088])
    w_out_o = k.din("w_out_odd", [D, D])
    w_gate = k.din("w_gla_gate", [16, 512])
    fnorm = k.din("fnorm", [D])
    vecs = k.din("vecs", [128, 64])
    bfox = k.din("bfox", [128, 4])
    consts = k.din("consts", [128, 1280])
    st_h = k.din("st_hgrn", [4, 4, 128, 128])
    st_g = k.din("st_gla", [4, 4, 128, 256])
    ptab = k.din("ptab", [4, NPG], I32)
    ckv = k.din("cache_kv", [POOL * 128, 1024])
    clf = k.din("cache_lf", [POOL, 512])

    y_p = k.dout("y_p", [L, D])
    y_s = k.dout("y_s", [256, D])
    fk_p = k.dout("fk_p", [L, 512])
    fv_p = k.dout("fv_p", [L, 512])
    flf_p = k.dout("flf_p", [L, 4])
    hg_p = k.dout("hg_p", [4, 128, 128])
    gl_p = k.dout("gl_p", [4, 128, 256])
    fk_s = k.dout("fk_s", [256, 512])
    fv_s = k.dout("fv_s", [256, 512])
    flf_s = k.dout("flf_s", [256, 4])
    hg_s = k.dout("hg_s", [4, 4, 128, 128])
    gl_s = k.dout("gl_s", [4, 4, 128, 256])

    LT = L + 256
    qbT = k.dscr("qbT", [4, 128, LT], BF16)
    kbT = k.dscr("kbT", [4, 128, LT], BF16)
    sgbT = k.dscr("sgbT", [4, 128, LT], BF16)
    vbs = k.dscr("vbs", [LT, 512], BF16)
    negc = k.dscr("negc", [LT, 4], F32)
    oaT = k.dscr("oaT", [4, 128, LT], BF16)
    obT = k.dscr("obT", [4, 128, LT], BF16)

    with ExitStack() as st:
        rec = Rec(nc, st)
        sb = lambda n, s, d: st.enter_context(nc.sbuf_tensor(n, s, d))
        ps = lambda n, s, d: st.enter_context(nc.psum_tensor(n, s, d))
        pA = [ps("pA%d" % i, [128, 512], F32) for i in range(2)]
        pT = ps("pT", [128, 1024], BF16)
        pS = ps("pS", [128, 512], F32)
        pP = ps("pP", [128, 512], F32)
        pO = ps("pO", [128, 512], F32)
        pN = ps("pN", [128, 512], F32)
        pX = ps("pX", [128, 512], F32)
        for nm in ("pA0", "pA1", "pT", "pS", "pP", "pO", "pN", "pX"):
            buf(nm).excl = True
        cst = sb("cst", [128, 1280], F32)
        vec = sb("vec", [128, 64], F32)
        bfx = sb("bfx", [128, 4], F32)
        idb = sb("idb", [128, 128], BF16)
        onesb = sb("onesb", [128, 128], BF16)
        m64 = sb("m64", [128, 64], F32)
        lbt = sb("lbt", [128, 4], F32)
        omlt = sb("omlt", [128, 4], F32)
        nomlt = sb("nomlt", [128, 4], F32)
        rec.dma("sp", lambda e: e.dma_start(out=cst[:], in_=consts), W=[buf("cst")])
        rec.dma("sp", lambda e: e.dma_start(out=vec[:], in_=vecs), W=[buf("vec")])
        rec.dma("sp", lambda e: e.dma_start(out=bfx[:], in_=bfox), W=[buf("bfx")])
        ident = cst[:, 0:128]
        tri = cst[:, 128:256]
        ones = cst[:, 320:448]
        bt32 = cst[:, 448:576]
        rst64 = cst[:, 576:1088]
        rst32 = cst[:, 1088:1216]
        rec.op("dve", lambda e: e.tensor_copy(out=idb[:], in_=ident), R=[buf("cst")], W=[buf("idb")])
        rec.op("dve", lambda e: e.tensor_copy(out=onesb[:], in_=ones), R=[buf("cst")], W=[buf("onesb")])
        rec.op("dve", lambda e: e.tensor_copy(out=m64[:], in_=cst[:, 256:320]), R=[buf("cst")], W=[buf("m64")])
        rec.op("dve", lambda e: e.tensor_sub(out=lbt[:], in0=vec[:, 24:28], in1=vec[:, 28:32]), R=[buf("vec")], W=[buf("lbt")])
        rec.op("act", lambda e: e.activation(out=lbt[:], in_=lbt[:], func=AF.Sigmoid), R=[buf("lbt")], W=[buf("lbt")])
        rec.op("dve", lambda e: e.tensor_scalar(out=omlt[:], in0=lbt[:], scalar1=-1.0, scalar2=1.0, op0=ALU.mult, op1=ALU.add),
               R=[buf("lbt")], W=[buf("omlt")])
        rec.op("dve", lambda e: e.tensor_scalar(out=nomlt[:], in0=omlt[:], scalar1=-1.0, scalar2=None, op0=ALU.mult),
               R=[buf("omlt")], W=[buf("nomlt")])

        G = dict(k=k, rec=rec, nc=nc, st=st, pA=pA, pT=pT, pS=pS, pP=pP, pO=pO, pN=pN, pX=pX, cst=cst, vec=vec, bfx=bfx,
                 idb=idb, onesb=onesb, m64=m64, lbt=lbt, omlt=omlt, nomlt=nomlt, ident=ident, tri=tri, ones=ones, bt32=bt32,
                 rst64=rst64)
        G.update(locals())
        if 1 in phases:
            phase1(G)
        if 2 in phases:
            phase2(G)
        if 3 in phases:
            phase3(G)
        if dbg:
            dbg_ob = k.dout("dbg_obT", [4, 128, LT], BF16)
            if 2 in phases:
                rec.dma("sp", lambda e: e.dma_start(out=dbg_ob, in_=obT), R=[buf("obT_d")], final=True)
            dbg_oa = k.dout("dbg_oaT", [4, 128, LT], BF16)
            rec.dma("sp", lambda e: e.dma_start(out=dbg_oa, in_=oaT), R=[buf("oaT_d")], final=True)
            dbg_nc = k.dout("dbg_negc", [LT, 4], F32)
            rec.dma("sp", lambda e: e.dma_start(out=dbg_nc, in_=negc), R=[buf("negc_d")], final=True)
        rec.finish()
        flush(k, rec)
    return nc


def load_weight_bf16(G, ph, wdram, ncols, gaincol, name):
    k, rec, nc, vec = G["k"], G["rec"], G["nc"], G["vec"]
    buf = k.buf
    W = ph.enter_context(nc.sbuf_tensor(name, [128, KC, ncols], BF16))
    with ExitStack() as tmp:
        stg = [tmp.enter_context(nc.sbuf_tensor(name + "_stg%d" % i, [128, ncols], F32)) for i in range(2)]
        wv = wdram.rearrange("(kc p) n -> p kc n", p=128)
        for kc in range(KC):
            s = stg[kc % 2]
            sbuf = buf(name + "_stg%d" % (kc % 2))
            rec.dma("sp", lambda e, s=s, kc=kc: e.dma_start(out=s[:], in_=wv[:, kc, :]), W=[sbuf])
            hc = (ncols // 2 + 3) // 4 * 4
            if gaincol is None:
                rec.op("dve", lambda e, s=s, kc=kc: e.tensor_copy(out=W[:, kc, 0:hc], in_=s[:, 0:hc]), R=[sbuf], W=[buf(name)])
                rec.op("act", lambda e, s=s, kc=kc: e.activation(out=W[:, kc, hc:ncols], in_=s[:, hc:ncols], func=AF.Copy), R=[sbuf], W=[buf(name + "_b")])
            else:
                gc = vec[:, gaincol + kc:gaincol + kc + 1]
                rec.op("dve", lambda e, s=s, kc=kc, gc=gc: e.tensor_scalar(out=W[:, kc, 0:hc], in0=s[:, 0:hc], scalar1=gc, scalar2=None, op0=ALU.mult), R=[sbuf, buf("vec")], W=[buf(name)])
                rec.op("act", lambda e, s=s, kc=kc, gc=gc: e.activation(out=W[:, kc, hc:ncols], in_=s[:, hc:ncols], func=AF.Copy, scale=gc), R=[sbuf, buf("vec")], W=[buf(name + "_b")])
        flush(k, rec)
    return W


def norm_and_transpose(G, xt, xtb, rstd_col, hb, hbb, hT, hTb, tcol, gain_in_w=True):
    rec, pT, idb = G["rec"], G["pT"], G["idb"]
    buf = G["k"].buf
    junk, ss = G["junk"], G["ss"]
    rec.op("act", lambda e: e.activation(out=junk[:], in_=xt[:], func=AF.Square, accum_out=ss[:, rstd_col:rstd_col + 1]),
           R=[xtb], W=[buf("junk"), buf("ss")])
    rec.op("act", lambda e: e.activation(out=ss[:, rstd_col:rstd_col + 1], in_=ss[:, rstd_col:rstd_col + 1], func=AF.Ln, scale=1.0 / D, bias=G["epsb"][:, 0:1]),
           R=[buf("ss"), buf("epsb")], W=[buf("ss")])
    rec.op("act", lambda e: e.activation(out=ss[:, rstd_col:rstd_col + 1], in_=ss[:, rstd_col:rstd_col + 1], func=AF.Exp, scale=-0.5),
           R=[buf("ss")], W=[buf("ss")])
    rec.op("dve", lambda e: e.tensor_scalar(out=hb[:], in0=xt[:], scalar1=ss[:, rstd_col:rstd_col + 1], scalar2=None, op0=ALU.mult),
           R=[xtb, buf("ss")], W=[hbb])
    for kc in range(KC):
        rec.op("pe", lambda e, kc=kc: e.transpose(out=pT[:, kc * 128:(kc + 1) * 128], in_=hb[:, kc * 128:(kc + 1) * 128], identity=idb[:]),
               R=[hbb, buf("idb")], W=[buf("pT")])
    rec.op("act", lambda e: e.activation(out=hT[:, :, tcol:tcol + 128], in_=pT[:].rearrange("p (kc t) -> p kc t", kc=KC), func=AF.Copy),
           R=[buf("pT")], W=[hTb])


def phase1(G):
    k, rec, nc = G["k"], G["rec"], G["nc"]
    buf = k.buf
    L = k.L
    NG = L // 512
    pA, pT, pS, pP, pO, pN, pX = G["pA"], G["pT"], G["pS"], G["pP"], G["pO"], G["pN"], G["pX"]
    vec, lbt, omlt, nomlt, m64, onesb, idb = G["vec"], G["lbt"], G["omlt"], G["nomlt"], G["m64"], G["onesb"], G["idb"]
    with ExitStack() as ph:
        sb = lambda n, s, d: ph.enter_context(nc.sbuf_tensor(n, s, d))
        W0 = load_weight_bf16(G, ph, G["w_in_e"], 4100, 0, "W0")
        G["junk"] = sb("junk", [128, 1024], BF16)
        G["ss"] = sb("ss", [128, 8], F32)
        G["epsb"] = sb("epsb", [128, 1], F32)
        rec.op("pool", lambda e: e.memset(G["epsb"][:], EPS), W=[buf("epsb")])
        xt = [sb("xt%d" % i, [128, D], F32) for i in range(3)]
        hb = [sb("hb%d" % i, [128, D], BF16) for i in range(2)]
        hT = sb("hT", [128, KC, 512], BF16)
        tmp = {n: sb("t_" + n, [128, 512], F32) for n in ("sig", "f", "omf", "lf", "b", "d", "eq", "ek")}
        qt = sb("qt", [128, 4, 512], BF16)
        kt = sb("kt", [128, 4, 512], BF16)
        sm = sb("sm", [128, 4, 3, 8], F32)
        vtok = sb("vtok", [128, 4, 512], BF16)
        ktok = sb("ktok", [128, 4, 4, 128], BF16)
        AT = [sb("AT%d" % i, [128, 4, 64], BF16) for i in range(2)]
        S = sb("S", [128, 4, 128], F32)
        Sall = [sb("Sall%d" % i, [128, 8, 128], BF16) for i in range(2)]
        kt2 = sb("kt2", [128, 512], BF16)
        oT = sb("oT", [128, 4, 512], F32)
        sq = sb("sq", [128, 512], BF16)
        lnt = sb("lnt", [128, 512], F32)
        sg = sb("sg", [128, 4, 512], BF16)
        tg = sb("tg", [128, 512], F32)
        oa = sb("oa", [128, 4, 512], BF16)
        qb = sb("qb", [128, 4, 512], BF16)
        kb = sb("kb", [128, 4, 512], BF16)
        sgb = sb("sgb", [128, 4, 512], BF16)
        ktm = [sb("ktm%d" % i, [128, 512], F32) for i in range(2)]
        vtm = [sb("vtm%d" % i, [128, 512], F32) for i in range(2)]
        vbf = sb("vbf", [128, 4, 512], BF16)
        lfb = sb("lfb", [128, 4, 4], F32)
        ncb = sb("ncb", [128, 4, 4], F32)
        tot = sb("tot", [128, 4], F32)
        rec.op("pool", lambda e: e.memset(tot[:], 0.0), W=[buf("tot")])
        rec.op("pool", lambda e: e.memset(S[:], 0.0), W=[buf("S%d" % h) for h in range(4)])

        pa_i = [0]

        def fm(col, N, hTN):
            p = pA[pa_i[0] % 2]
            pb = buf("pA%d" % (pa_i[0] % 2))
            pa_i[0] += 1
            for kc in range(KC):
                rec.op("pe", lambda e, kc=kc, p=p: e.matmul(out=p[:, 0:N], lhsT=W0[:, kc, col:col + 128], rhs=hTN[:, kc, 0:N],
                                                           start=(kc == 0), stop=(kc == KC - 1)),
                       R=[buf("W0"), buf("hT")], W=[pb])
            return p, pb

        def tm(col, ncols, t):
            p = pA[pa_i[0] % 2]
            pb = buf("pA%d" % (pa_i[0] % 2))
            pa_i[0] += 1
            for kc in range(KC):
                rec.op("pe", lambda e, kc=kc, p=p: e.matmul(out=p[:, 0:ncols], lhsT=hT[:, kc, t * 128:(t + 1) * 128], rhs=W0[:, kc, col:col + ncols],
                                                           start=(kc == 0), stop=(kc == KC - 1)),
                       R=[buf("W0"), buf("hT")], W=[pb])
            return p, pb

        groups = [(g * 512, 512, False) for g in range(NG)] + [(L, 256, True)]
        xi = [0]
        def do_group(t0, N, samp):
            ntile = N // 128
            C = 64
            nch = N // C
            mid = 1 if samp else 31
            last = 3 if samp else C - 1
            for t in range(ntile):
                x_ = xt[xi[0] % 3]
                xb_ = buf("xt%d" % (xi[0] % 3))
                h_ = hb[xi[0] % 2]
                hb_ = buf("hb%d" % (xi[0] % 2))
                xi[0] += 1
                src = G["xs"][t * 128:(t + 1) * 128, :] if samp else G["xp"][t0 + t * 128:t0 + (t + 1) * 128, :]
                rec.dma("sp", lambda e, x_=x_, src=src: e.dma_start(out=x_[:], in_=src), W=[xb_])
                norm_and_transpose(G, x_, xb_, t, h_, hb_, hT, buf("hT"), t * 128)
            if KSTOP < 2:
                return
            for t in range(ntile):
                r0 = t0 + t * 128
                p, pb = tm(1024, 512, t)
                rec.op("act", lambda e, p=p, t=t: e.activation(out=vtok[:, t, :], in_=p[:, 0:512], func=AF.Copy), R=[pb], W=[buf("vtok")])
                p, pb = tm(2560, 512, t)
                kk = ktm[t % 2]
                kkb = buf("ktm%d" % (t % 2))
                rec.op("act", lambda e, p=p, kk=kk: e.activation(out=kk[:], in_=p[:, 0:512], func=AF.Copy), R=[pb], W=[kkb])
                dst = G["fk_s"][t * 128:(t + 1) * 128, :] if samp else G["fk_p"][r0:r0 + 128, :]
                rec.dma("pool", lambda e, kk=kk, dst=dst: e.dma_start(out=dst, in_=kk[:]), R=[kkb], final=True)
                p, pb = tm(3072, 512, t)
                vv = vtm[t % 2]
                vvb = buf("vtm%d" % (t % 2))
                rec.op("act", lambda e, p=p, vv=vv: e.activation(out=vv[:], in_=p[:, 0:512], func=AF.Copy), R=[pb], W=[vvb])
                rec.op("dve", lambda e, p=p, t=t: e.tensor_copy(out=vbf[:, t, :], in_=p[:, 0:512]), R=[pb], W=[buf("vbf")])
                dst = G["fv_s"][t * 128:(t + 1) * 128, :] if samp else G["fv_p"][r0:r0 + 128, :]
                rec.dma("pool", lambda e, vv=vv, dst=dst: e.dma_start(out=dst, in_=vv[:]), R=[vvb], final=True)
                if KSUB < 1:
                    continue
                p, pb = tm(4096, 4, t)
                rec.op("dve", lambda e, p=p, t=t: e.tensor_tensor(out=lfb[:, t, :], in0=p[:, 0:4], in1=G["bfx"][:], op=ALU.add),
                       R=[pb, buf("bfx")], W=[buf("lfb")])
                rec.op("act", lambda e, t=t: e.activation(out=lfb[:, t, :], in_=lfb[:, t, :], func=AF.Sigmoid), R=[buf("lfb")], W=[buf("lfb")])
                rec.op("act", lambda e, t=t: e.activation(out=lfb[:, t, :], in_=lfb[:, t, :], func=AF.Ln), R=[buf("lfb")], W=[buf("lfb")])
                if KSUB < 2:
                    continue
                trim = G["bt32"] if samp else G["tri"]
                rec.op("pe", lambda e, t=t, trim=trim: e.matmul(out=pX[:, 0:4], lhsT=trim, rhs=lfb[:, t, :], start=True, stop=True),
                       R=[buf("cst"), buf("lfb")], W=[buf("pX")])
                if samp:
                    rec.op("dve", lambda e, t=t: e.tensor_scalar(out=ncb[:, t, :], in0=pX[:, 0:4], scalar1=-1.0, scalar2=None, op0=ALU.mult),
                           R=[buf("pX")], W=[buf("ncb")])
                else:
                    rec.op("dve", lambda e, t=t: e.scalar_tensor_tensor(out=ncb[:, t, :], in0=pX[:, 0:4], scalar=-1.0, in1=tot[:], op0=ALU.mult, op1=ALU.subtract),
                           R=[buf("pX"), buf("tot")], W=[buf("ncb")])
                    rec.op("pe", lambda e, t=t: e.matmul(out=pX[:, 8:12], lhsT=G["ones"], rhs=lfb[:, t, :], start=True, stop=True),
                           R=[buf("cst"), buf("lfb")], W=[buf("pX")])
                    rec.op("dve", lambda e: e.tensor_tensor(out=tot[:], in0=tot[:], in1=pX[:, 8:12], op=ALU.add),
                           R=[buf("pX"), buf("tot")], W=[buf("tot")])
            if KSUB < 3:
                return
            dst = (G["flf_s"] if samp else G["flf_p"][t0:t0 + N, :]).rearrange("(t p) h -> p t h", p=128)
            rec.dma("pool", lambda e, dst=dst: e.dma_start(out=dst, in_=lfb[:, 0:ntile, :]), R=[buf("lfb")], final=True)
            rec.dma("pool", lambda e: e.dma_start(out=G["negc"][t0:t0 + N, :].rearrange("(t p) h -> p t h", p=128), in_=ncb[:, 0:ntile, :]),
                    R=[buf("ncb")], W=[buf("negc_d")])
            rec.dma("pool", lambda e: e.dma_start(out=G["vbs"][t0:t0 + N, :].rearrange("(t p) c -> p t c", p=128), in_=vbf[:, 0:ntile, :]),
                    R=[buf("vbf")], W=[buf("vbs_d")])
            if KSTOP < 3:
                return
            for h in range(4):
                p, pb = fm(2048 + 128 * h, N, hT)
                rec.op("act", lambda e, p=p, h=h: e.activation(out=qb[:, h, 0:N], in_=p[:, 0:N], func=AF.Copy), R=[pb], W=[buf("qb")])
                p, pb = fm(2560 + 128 * h, N, hT)
                rec.op("dve", lambda e, p=p, h=h: e.tensor_copy(out=kb[:, h, 0:N], in_=p[:, 0:N]), R=[pb], W=[buf("kb")])
                p, pb = fm(3584 + 128 * h, N, hT)
                rec.op("act", lambda e, p=p, h=h: e.activation(out=sgb[:, h, 0:N], in_=p[:, 0:N], func=AF.Silu), R=[pb], W=[buf("sgb")])
            for (src_t, srcn, dstT) in ((qb, "qb", G["qbT"]), (kb, "kb", G["kbT"]), (sgb, "sgb", G["sgbT"])):
                rec.dma("pool", lambda e, src_t=src_t, dstT=dstT: e.dma_start(out=dstT[:, :, t0:t0 + N].rearrange("h p n -> p h n"), in_=src_t[:, :, 0:N]),
                        R=[buf(srcn)], W=[buf(srcn + "T_d")])
            if KSTOP < 4:
                return
            for h in range(4):
                T_ = tmp
                p, pb = fm(512 + 128 * h, N, hT)
                rec.op("act", lambda e, p=p: e.activation(out=T_["sig"][:, 0:N], in_=p[:, 0:N], func=AF.Sigmoid), R=[pb], W=[buf("t_sig")])
                rec.op("dve", lambda e, h=h: e.tensor_scalar(out=T_["f"][:, 0:N], in0=T_["sig"][:, 0:N], scalar1=omlt[:, h:h + 1], scalar2=lbt[:, h:h + 1],
                                                           op0=ALU.mult, op1=ALU.add), R=[buf("t_sig"), buf("omlt"), buf("lbt")], W=[buf("t_f")])
                rec.op("dve", lambda e, h=h: e.tensor_scalar(out=T_["omf"][:, 0:N], in0=T_["sig"][:, 0:N], scalar1=nomlt[:, h:h + 1], scalar2=omlt[:, h:h + 1],
                                                           op0=ALU.mult, op1=ALU.add), R=[buf("t_sig"), buf("omlt"), buf("nomlt")], W=[buf("t_omf")])
                rec.op("act", lambda e: e.activation(out=T_["lf"][:, 0:N], in_=T_["f"][:, 0:N], func=AF.Ln), R=[buf("t_f")], W=[buf("t_lf")])
                rmask = G["rst64"][:, 0:N]
                rec.op("dve", lambda e, rmask=rmask: e.tensor_tensor_scan(out=T_["b"][:, 0:N], data0=rmask, data1=T_["lf"][:, 0:N], initial=0.0,
                                                                         op0=ALU.mult, op1=ALU.add), R=[buf("t_lf"), buf("cst")], W=[buf("t_b")])
                b3 = T_["b"][:, 0:N].rearrange("p (c k) -> p c k", k=C)
                d3 = T_["d"][:, 0:N].rearrange("p (c k) -> p c k", k=C)
                rec.op("dve", lambda e, b3=b3, d3=d3: e.tensor_tensor(out=d3, in0=b3, in1=b3[:, :, mid:mid + 1].to_broadcast([128, nch, C]), op=ALU.subtract),
                       R=[buf("t_b")], W=[buf("t_d")])
                rec.op("act", lambda e: e.activation(out=T_["eq"][:, 0:N], in_=T_["d"][:, 0:N], func=AF.Exp), R=[buf("t_d")], W=[buf("t_eq")])
                rec.op("act", lambda e: e.activation(out=T_["ek"][:, 0:N], in_=T_["d"][:, 0:N], func=AF.Exp, scale=-1.0), R=[buf("t_d")], W=[buf("t_ek")])
                rec.op("act", lambda e, h=h, b3=b3: e.activation(out=sm[:, h, 0, 0:nch], in_=b3[:, :, mid], func=AF.Exp), R=[buf("t_b")], W=[buf("sm")])
                rec.op("act", lambda e, h=h, b3=b3: e.activation(out=sm[:, h, 2, 0:nch], in_=b3[:, :, last], func=AF.Exp), R=[buf("t_b")], W=[buf("sm")])
                rec.op("act", lambda e, h=h, d3=d3: e.activation(out=sm[:, h, 1, 0:nch], in_=d3[:, :, last], func=AF.Exp), R=[buf("t_d")], W=[buf("sm")])
                p, pb = fm(0 + 128 * h, N, hT)
                rec.op("dve", lambda e, p=p, h=h: e.tensor_tensor(out=qt[:, h, 0:N], in0=p[:, 0:N], in1=T_["eq"][:, 0:N], op=ALU.mult),
                       R=[pb, buf("t_eq")], W=[buf("qt")])
                rec.op("dve", lambda e, h=h: e.tensor_tensor(out=kt[:, h, 0:N], in0=T_["omf"][:, 0:N], in1=T_["ek"][:, 0:N], op=ALU.mult),
                       R=[buf("t_omf"), buf("t_ek")], W=[buf("kt")])
                p, pb = fm(1536 + 128 * h, N, hT)
                rec.op("act", lambda e, p=p, h=h: e.activation(out=sg[:, h, 0:N], in_=p[:, 0:N], func=AF.Silu), R=[pb], W=[buf("sg")])
                rec.op("dve", lambda e, h=h: e.tensor_tensor(out=kt2[:, 0:N].rearrange("p (c k) -> p c k", k=C), in0=kt[:, h, 0:N].rearrange("p (c k) -> p c k", k=C),
                                                           in1=sm[:, h, 1, 0:nch].unsqueeze(2).to_broadcast([128, nch, C]), op=ALU.mult), R=[buf("kt"), buf("sm")], W=[buf("kt2")])
                for t in range(ntile):
                    rec.op("pe", lambda e, h=h, t=t: e.transpose(out=pT[:, h * 128:(h + 1) * 128], in_=kt2[:, t * 128:(t + 1) * 128], identity=idb[:]),
                           R=[buf("kt2"), buf("idb")], W=[buf("pT")])
                    rec.op("act", lambda e, h=h, t=t: e.activation(out=ktok[:, t, h, :], in_=pT[:, h * 128:(h + 1) * 128], func=AF.Copy),
                           R=[buf("pT")], W=[buf("ktok")])
            if KSTOP < 5:
                return
            hg_state_in = G["st_h"]

            def cinfo(c):
                r0 = (c * C) % 128
                return slice(c * C, (c + 1) * C), (c * C) // 128, slice(r0, r0 + C)

            for h in range(4):
                Sb = buf("S%d" % h)
                at = AT[h % 2]
                atb = buf("AT%d" % (h % 2))
                sal = Sall[h % 2]
                salb = buf("Sall%d" % (h % 2))
                hs = slice(h * 128, (h + 1) * 128)
                for c in range(nch):
                    cols, t, rows = cinfo(c)
                    rec.op("pe", lambda e, h=h, cols=cols, rows=rows, t=t: e.matmul(out=pS[rows, t * 64:t * 64 + C], lhsT=kt[:, h, cols], rhs=qt[:, h, cols], start=True, stop=True),
                           R=[buf("kt"), buf("qt")], W=[buf("pS")])
                if KSUB < 2:
                    continue
                rec.op("dve", lambda e, at=at: e.tensor_tensor(out=at[:, 0:ntile, :], in0=pS[:, 0:ntile * 64].rearrange("p (a b) -> p a b", b=64),
                                                              in1=m64[:, :].unsqueeze(1).to_broadcast([128, ntile, 64]), op=ALU.mult),
                       R=[buf("pS"), buf("m64")], W=[atb])
                if KSUB < 3:
                    continue
                for c in range(nch):
                    cols, t, rows = cinfo(c)
                    po_, pob_ = (pO, buf("pO")) if c % 2 == 0 else (pA[0], buf("pA0"))
                    rec.op("pe", lambda e, hs=hs, rows=rows, t=t, at=at, c=c, po_=po_: e.matmul(out=po_[:, (c // 2) * 64:(c // 2) * 64 + C], lhsT=vtok[rows, t, hs], rhs=at[rows, t, :], start=True, stop=True),
                           R=[buf("vtok"), atb], W=[pob_])
                if KSUB < 4:
                    continue
                for c in range(nch):
                    cols, t, rows = cinfo(c)
                    pb_, pbb_ = (pP, buf("pP")) if c % 2 == 0 else (pX, buf("pX"))
                    rec.op("pe", lambda e, h=h, hs=hs, rows=rows, t=t, c=c, pb_=pb_: e.matmul(out=pb_[:, (c // 2) * 128:(c // 2 + 1) * 128], lhsT=ktok[rows, t, h, :], rhs=vtok[rows, t, hs], start=True, stop=True),
                           R=[buf("ktok"), buf("vtok")], W=[pbb_])
                if KSUB < 5:
                    continue
                for c in range(nch):
                    pb_, pbb_ = (pP, buf("pP")) if c % 2 == 0 else (pX, buf("pX"))
                    if samp:
                        rec.dma("sp", lambda e, h=h, c=c: e.dma_start(out=S[:, h, :], in_=hg_state_in[c, h]), W=[Sb])
                    rec.op("dve", lambda e, h=h, c=c, sal=sal: e.tensor_scalar(out=sal[:, c, :], in0=S[:, h, :], scalar1=sm[:, h, 0, c:c + 1], scalar2=None, op0=ALU.mult),
                           R=[Sb, buf("sm")], W=[salb])
                    rec.op("dve", lambda e, h=h, c=c, pb_=pb_: e.scalar_tensor_tensor(out=S[:, h, :], in0=S[:, h, :], scalar=sm[:, h, 2, c:c + 1], in1=pb_[:, (c // 2) * 128:(c // 2 + 1) * 128], op0=ALU.mult, op1=ALU.add),
                           R=[pbb_, Sb, buf("sm")], W=[Sb])
                    if samp:
                        rec.dma("pool", lambda e, h=h, c=c: e.dma_start(out=G["hg_s"][c, h], in_=S[:, h, :]), R=[Sb], final=True)
                if KSUB < 6:
                    continue
                for c in range(nch):
                    cols, t, rows = cinfo(c)
                    rec.op("pe", lambda e, h=h, cols=cols, c=c, sal=sal: e.matmul(out=pN[:, c * 64:c * 64 + C], lhsT=sal[:, c, :], rhs=qt[:, h, cols], start=True, stop=True),
                           R=[salb, buf("qt")], W=[buf("pN")])
                if KSUB < 7:
                    continue
                o4 = oT[:, h, 0:N].rearrange("p (a two k) -> p a two k", two=2, k=64)
                rec.op("act", lambda e, o4=o4: e.activation(out=o4[:, :, 0, :], in_=pO[:, 0:N // 2].rearrange("p (a k) -> p a k", k=64), func=AF.Copy), R=[buf("pO")], W=[buf("oT")])
                rec.op("act", lambda e, o4=o4: e.activation(out=o4[:, :, 1, :], in_=pA[0][:, 0:N // 2].rearrange("p (a k) -> p a k", k=64), func=AF.Copy), R=[buf("pA0")], W=[buf("oT")])
                rec.op("dve", lambda e, h=h: e.tensor_tensor(out=oT[:, h, 0:N], in0=oT[:, h, 0:N], in1=pN[:, 0:N], op=ALU.add), R=[buf("pN"), buf("oT")], W=[buf("oT")])
            if (not samp) and t0 + N == L:
                for h in range(4):
                    rec.dma("pool", lambda e, h=h: e.dma_start(out=G["hg_p"][h], in_=S[:, h, :]), R=[buf("S%d" % h)], final=True)
            if KSTOP < 6:
                return
            for h in range(4):
                rec.op("act", lambda e, h=h: e.activation(out=sq[:, 0:N], in_=oT[:, h, 0:N], func=AF.Square), R=[buf("oT")], W=[buf("sq")])
                rec.op("pe", lambda e: e.matmul(out=pN[:, 0:N], lhsT=onesb[:], rhs=sq[:, 0:N], start=True, stop=True), R=[buf("onesb"), buf("sq")], W=[buf("pN")])
                rec.op("act", lambda e: e.activation(out=lnt[:, 0:N], in_=pN[:, 0:N], func=AF.Ln, scale=1.0 / 128, bias=G["epsb"][:, 0:1]), R=[buf("pN"), buf("epsb")], W=[buf("lnt")])
                rec.op("act", lambda e: e.activation(out=lnt[:, 0:N], in_=lnt[:, 0:N], func=AF.Exp, scale=-0.5), R=[buf("lnt")], W=[buf("lnt")])
                rec.op("dve", lambda e, h=h: e.scalar_tensor_tensor(out=tg[:, 0:N], in0=lnt[:, 0:N], scalar=vec[:, 32 + h:33 + h], in1=sg[:, h, 0:N], op0=ALU.mult, op1=ALU.mult),
                       R=[buf("lnt"), buf("vec"), buf("sg")], W=[buf("tg")])
                rec.op("dve", lambda e, h=h: e.tensor_tensor(out=oa[:, h, 0:N], in0=oT[:, h, 0:N], in1=tg[:, 0:N], op=ALU.mult), R=[buf("oT"), buf("tg")], W=[buf("oa")])
            rec.dma("pool", lambda e: e.dma_start(out=G["oaT"][:, :, t0:t0 + N].rearrange("h p n -> p h n"), in_=oa[:, :, 0:N]), R=[buf("oa")], W=[buf("oaT_d")])
        for (t0_, N_, samp_) in groups:
            if KSTOP >= 1:
                do_group(t0_, N_, samp_)
        flush(k, rec)


def phase2(G):
    k, rec, nc = G["k"], G["rec"], G["nc"]
    buf = k.buf
    L, NPG = k.L, k.NPG
    NT = L // 128
    NG = L // 512
    pS, pP, pO, pN, pX = G["pS"], G["pP"], G["pO"], G["pN"], G["pX"]
    onesb, cst = G["onesb"], G["cst"]
    KW = max(L, NPG * 128 + 128)
    NB = max(NT, NPG + 1)
    with ExitStack() as ph:
        sb = lambda n, s, d: ph.enter_context(nc.sbuf_tensor(n, s, d))
        kT = sb("kT", [128, 4, KW], BF16)
        V = sb("V", [128, NB, 512], BF16)
        ngs = sb("ngs", [128, NB, 4], F32)
        biasT = [sb("biasT%d" % i, [128, 4, NB], F32) for i in range(2)]
        qb = [sb("qb2_%d" % i, [128, 4, 512], BF16) for i in range(2)]
        sgb = [sb("sgb2_%d" % i, [128, 4, 512], BF16) for i in range(2)]
        ob = [sb("ob%d" % i, [128, 4, 512], BF16) for i in range(2)]
        PT = [sb("PT%d" % i, [128, 512], BF16) for i in range(2)]
        trib = sb("trib", [128, 128], BF16)
        m64b = sb("m64b", [128, 64], BF16)
        nq = sb("nq", [128, 4], F32)
        rl = sb("rl", [128, 512], F32)
        tg2 = sb("tg2", [128, 512], F32)
        rec.op("dve", lambda e: e.tensor_copy(out=trib[:], in_=G["tri"]), R=[buf("cst")], W=[buf("trib")])
        rec.op("dve", lambda e: e.tensor_copy(out=m64b[:], in_=cst[:, 256:320]), R=[buf("cst")], W=[buf("m64b")])
        for h in range(4):
            rec.dma("sp", lambda e, h=h: e.dma_start(out=kT[:, h, 0:L], in_=G["kbT"][h, :, 0:L]), R=[buf("kbT_d")], W=[buf("kT")])
        rec.dma("sp", lambda e: e.dma_start(out=V[:, 0:NT, :], in_=G["vbs"][0:L, :].rearrange("(t p) c -> p t c", p=128)), R=[buf("vbs_d")], W=[buf("V")])
        rec.dma("sp", lambda e: e.dma_start(out=ngs[:, 0:NT, :], in_=G["negc"][0:L, :].rearrange("(t p) h -> p t h", p=128)), R=[buf("negc_d")], W=[buf("ngs")])

        Pacc = [sb("Pacc%d" % i, [128, 512], F32) for i in range(2)]
        acc_i = [0]

        def attend(h, qt_, qtb, NQ, blist, bT, bTb):
            nbk = len(blist)
            assert blist[0][1] == 128 and blist[0][4] == 0
            def _vars(bi):
                kc0, ks, vt, bj, c0, diag = blist[bi]
                n = NQ - c0
                pb = (pS, pP)[bi % 2]
                pbb = buf(("pS", "pP")[bi % 2])
                P_ = PT[bi % 2]
                Pb = buf("PT%d" % (bi % 2))
                return kc0, ks, vt, bj, c0, diag, n, pb, pbb, P_, Pb

            def emit_scores(bi):
                kc0, ks, vt, bj, c0, diag, n, pb, pbb, P_, Pb = _vars(bi)
                rec.op("pe", lambda e, pb=pb, kc0=kc0, ks=ks, c0=c0, n=n: e.matmul(out=pb[0:ks, 0:n], lhsT=kT[:, h, kc0:kc0 + ks], rhs=qt_[:, h, c0:NQ], start=True, stop=True),
                       R=[buf("kT"), qtb], W=[pbb])

            def emit_rest(bi):
                kc0, ks, vt, bj, c0, diag, n, pb, pbb, P_, Pb = _vars(bi)
                rec.op("act", lambda e, pb=pb, P_=P_, ks=ks, n=n, bj=bj: e.activation(out=P_[0:ks, 0:n], in_=pb[0:ks, 0:n], func=AF.Exp, scale=SCALE, bias=bT[0:ks, h, bj:bj + 1]),
                       R=[pbb, bTb], W=[Pb])
                if diag:
                    mk = trib if ks == 128 else m64b
                    rec.op("dve", lambda e, P_=P_, ks=ks, mk=mk: e.tensor_tensor(out=P_[0:ks, 0:ks], in0=P_[0:ks, 0:ks], in1=mk[0:ks, 0:ks], op=ALU.mult),
                           R=[Pb, buf("trib"), buf("m64b")], W=[Pb])
                rec.op("pe", lambda e, P_=P_, ks=ks, vt=vt, c0=c0, n=n, bi=bi: e.matmul(out=pO[:, c0:NQ], lhsT=V[0:ks, vt, h * 128:(h + 1) * 128], rhs=P_[0:ks, 0:n], start=(bi == 0), stop=(bi == nbk - 1)),
                       R=[buf("V"), Pb], W=[buf("pO")])
                pacc, paccb = Pacc[acc_i[0] % 2], buf("Pacc%d" % (acc_i[0] % 2))
                if bi == 0:
                    rec.op("dve", lambda e, P_=P_, pacc=pacc: e.tensor_copy(out=pacc[:, 0:NQ], in_=P_[:, 0:NQ]), R=[Pb], W=[paccb])
                else:
                    rec.op("dve", lambda e, P_=P_, pacc=pacc, ks=ks, c0=c0, n=n: e.tensor_tensor(out=pacc[0:ks, c0:NQ], in0=pacc[0:ks, c0:NQ], in1=P_[0:ks, 0:n], op=ALU.add),
                           R=[Pb, paccb], W=[paccb])

            emit_scores(0)
            for bi in range(nbk):
                if bi + 1 < nbk:
                    emit_scores(bi + 1)
                emit_rest(bi)
            pacc, paccb = Pacc[acc_i[0] % 2], buf("Pacc%d" % (acc_i[0] % 2))
            acc_i[0] += 1
            rec.op("pe", lambda e, pacc=pacc: e.matmul(out=pN[:, 0:NQ], lhsT=G["ones"], rhs=pacc[:, 0:NQ], start=True, stop=True), R=[buf("cst"), paccb], W=[buf("pN")])

        def epilogue(h, NQ, sg_, sgbb, ob_, obb):
            rec.op("dve", lambda e: e.reciprocal(out=rl[:, 0:NQ], in_=pN[:, 0:NQ]), R=[buf("pN")], W=[buf("rl")])
            rec.op("dve", lambda e: e.tensor_tensor(out=tg2[:, 0:NQ], in0=rl[:, 0:NQ], in1=sg_[:, h, 0:NQ], op=ALU.mult), R=[buf("rl"), sgbb], W=[buf("tg2")])
            rec.op("dve", lambda e: e.tensor_tensor(out=ob_[:, h, 0:NQ], in0=pO[:, 0:NQ], in1=tg2[:, 0:NQ], op=ALU.mult), R=[buf("pO"), buf("tg2")], W=[obb])

        def load_q(i, c0, n):
            q_, qbb = qb[i % 2], buf("qb2_%d" % (i % 2))
            s_, sbb = sgb[i % 2], buf("sgb2_%d" % (i % 2))
            rec.dma("sp", lambda e: e.dma_start(out=q_[:, :, 0:n], in_=G["qbT"][:, :, c0:c0 + n].rearrange("h p n -> p h n")), R=[buf("qbT_d")], W=[qbb])
            rec.dma("sp", lambda e: e.dma_start(out=s_[:, :, 0:n], in_=G["sgbT"][:, :, c0:c0 + n].rearrange("h p n -> p h n")), R=[buf("sgbT_d")], W=[sbb])
            return q_, qbb, s_, sbb

        def prompt_group(I):
            t0 = 512 * I
            q_, qbb, s_, sbb = load_q(I, t0, 512)
            o_, obb = ob[I % 2], buf("ob%d" % (I % 2))
            bT, bTb = biasT[I % 2], buf("biasT%d" % (I % 2))
            nb = 4 * I + 4
            rec.op("pe", lambda e: e.matmul(out=pX[:, 0:4], lhsT=G["ones"][0:1, :], rhs=ngs[0:1, 4 * I, :], start=True, stop=True), R=[buf("cst"), buf("ngs")], W=[buf("pX")])
            rec.op("act", lambda e: e.activation(out=nq[:], in_=pX[:, 0:4], func=AF.Copy), R=[buf("pX")], W=[buf("nq")])
            for h in range(4):
                rec.op("dve", lambda e, h=h: e.tensor_scalar(out=bT[:, h, 0:nb], in0=ngs[:, 0:nb, h], scalar1=nq[:, h:h + 1], scalar2=None, op0=ALU.subtract),
                       R=[buf("ngs"), buf("nq")], W=[bTb])
            for h in range(4):
                blist = []
                for j in range(nb):
                    r = j - 4 * I
                    blist.append((128 * j, 128, j, j, 128 * max(r, 0), r >= 0))
                attend(h, q_, qbb, 512, blist, bT, bTb)
                epilogue(h, 512, s_, sbb, o_, obb)
            rec.dma("pool", lambda e: e.dma_start(out=G["obT"][:, :, t0:t0 + 512].rearrange("h p n -> p h n"), in_=o_[:, :, :]), R=[obb], W=[buf("obT_d")])

        for I in range(NG):
            prompt_group(I)

        pti = sb("pti", [128, NPG], I32)
        ptc = sb("ptc", [128, 1], I32)
        idxf = sb("idxf", [128, NPG], F32)
        idx = sb("idx", [128, NPG], I32)
        lfp = sb("lfp", [128, 512], F32)
        cum = sb("cum", [128, 4, 128], F32)
        T4 = sb("T4", [128, 4], F32)
        TR = sb("TR", [128, 4], F32)
        sufp = sb("sufp", [128, 4, 128], F32)
        kst = [sb("kst%d" % i, [128, 1024], F32) for i in range(4)]
        kbf = [sb("kbf%d" % i, [128, 512], BF16) for i in range(2)]
        iot = cst[:, 1216:1217]
        ustr = cst[:, 1088:1216]

        def sample_seq(j):
            c0 = L + 64 * j
            rec.dma("sp", lambda e: e.dma_start(out=pti[:], in_=G["ptab"][j].partition_broadcast(128)), W=[buf("pti")])
            rec.dma("sp", lambda e: e.dma_start(out=ptc[0:NPG, :], in_=G["ptab"][j].rearrange("(n o) -> n o", o=1)), W=[buf("ptc")])
            rec.op("dve", lambda e: e.tensor_scalar(out=idxf[:], in0=pti[:], scalar1=128.0, scalar2=iot, op0=ALU.mult, op1=ALU.add), R=[buf("pti"), buf("cst")], W=[buf("idxf")])
            rec.op("dve", lambda e: e.tensor_copy(out=idx[:], in_=idxf[:]), R=[buf("idxf")], W=[buf("idx")])
            rec.dma("pool", lambda e: e.indirect_dma_start(out=lfp[0:NPG, :], out_offset=None, in_=G["clf"], in_offset=bass.IndirectOffsetOnAxis(ap=ptc[0:NPG, 0:1], axis=0)),
                    R=[buf("ptc")], W=[buf("lfp")])
            lf3 = lfp[0:NPG, :].rearrange("p (s h) -> p h s", h=4)
            bT, bTb = biasT[j % 2], buf("biasT%d" % (j % 2))
            for h in range(4):
                rec.op("dve", lambda e, h=h: e.tensor_tensor_scan(out=cum[0:NPG, h, :], data0=G["ones"][0:NPG, :], data1=lf3[:, h, :], initial=0.0, op0=ALU.mult, op1=ALU.add),
                       R=[buf("lfp"), buf("cst")], W=[buf("cum")])
            rec.op("dve", lambda e: e.tensor_copy(out=T4[0:NPG, :], in_=cum[0:NPG, :, 127]), R=[buf("cum")], W=[buf("T4")])
            rec.op("pe", lambda e: e.matmul(out=pX[0:NPG, 0:4], lhsT=ustr[0:NPG, 0:NPG], rhs=T4[0:NPG, :], start=True, stop=True), R=[buf("cst"), buf("T4")], W=[buf("pX")])
            rec.op("dve", lambda e: e.tensor_tensor(out=TR[0:NPG, :], in0=pX[0:NPG, 0:4], in1=T4[0:NPG, :], op=ALU.add), R=[buf("pX"), buf("T4")], W=[buf("TR")])
            for h in range(4):
                rec.op("dve", lambda e, h=h: e.tensor_scalar(out=sufp[0:NPG, h, :], in0=cum[0:NPG, h, :], scalar1=-1.0, scalar2=TR[0:NPG, h:h + 1], op0=ALU.mult, op1=ALU.add),
                       R=[buf("cum"), buf("TR")], W=[buf("sufp")])
                rec.op("pe", lambda e, h=h: e.transpose(out=pX[:, 0:NPG], in_=sufp[0:NPG, h, :], identity=G["ident"][0:NPG, 0:NPG]), R=[buf("sufp"), buf("cst")], W=[buf("pX")])
                rec.op("act", lambda e, h=h: e.activation(out=bT[:, h, 0:NPG], in_=pX[:, 0:NPG], func=AF.Copy), R=[buf("pX")], W=[bTb])
            rec.dma("sp", lambda e: e.dma_start(out=ngs[0:64, 0, :], in_=G["negc"][c0:c0 + 64, :]), R=[buf("negc_d")], W=[buf("ngs")])
            rec.op("dve", lambda e: e.tensor_copy(out=bT[0:64, :, NPG], in_=ngs[0:64, 0, :]), R=[buf("ngs")], W=[bTb])
            for i in range(NPG):
                ks_, ksb = kst[i % 4], buf("kst%d" % (i % 4))
                rec.dma("pool", lambda e, ks_=ks_, i=i: e.indirect_dma_start(out=ks_[:], out_offset=None, in_=G["ckv"], in_offset=bass.IndirectOffsetOnAxis(ap=idx[:, i:i + 1], axis=0)),
                        R=[buf("idx")], W=[ksb])
                kb_, kbb_ = kbf[i % 2], buf("kbf%d" % (i % 2))
                rec.op("dve", lambda e, ks_=ks_, kb_=kb_: e.tensor_copy(out=kb_[:], in_=ks_[:, 0:512]), R=[ksb], W=[kbb_])
                for h in range(4):
                    rec.op("pe", lambda e, kb_=kb_, h=h: e.transpose(out=G["pT"][:, h * 128:(h + 1) * 128], in_=kb_[:, h * 128:(h + 1) * 128], identity=G["idb"][:]), R=[kbb_, buf("idb")], W=[buf("pT")])
                rec.op("act", lambda e, i=i: e.activation(out=kT[:, :, 128 * i:128 * i + 128], in_=G["pT"][:, 0:512].rearrange("p (h s) -> p h s", h=4), func=AF.Copy), R=[buf("pT")], W=[buf("kT")])
                rec.op("dve", lambda e, ks_=ks_, i=i: e.tensor_copy(out=V[:, i, :], in_=ks_[:, 512:1024]), R=[ksb], W=[buf("V")])
            rec.dma("sp", lambda e: e.dma_start(out=kT[:, :, 128 * NPG:128 * NPG + 64], in_=G["kbT"][:, :, c0:c0 + 64].rearrange("h p n -> p h n")), R=[buf("kbT_d")], W=[buf("kT")])
            rec.dma("sp", lambda e: e.dma_start(out=V[0:64, NPG, :], in_=G["vbs"][c0:c0 + 64, :]), R=[buf("vbs_d")], W=[buf("V")])
            q_, qbb, s_, sbb = load_q(j, c0, 64)
            o_, obb = ob[j % 2], buf("ob%d" % (j % 2))
            for h in range(4):
                blist = [(128 * i, 128, i, i, 0, False) for i in range(NPG)] + [(128 * NPG, 64, NPG, NPG, 0, True)]
                attend(h, q_, qbb, 64, blist, bT, bTb)
                epilogue(h, 64, s_, sbb, o_, obb)
            rec.dma("pool", lambda e: e.dma_start(out=G["obT"][:, :, c0:c0 + 64].rearrange("h p n -> p h n"), in_=o_[:, :, 0:64]), R=[obb], W=[buf("obT_d")])

        for j in range(4):
            sample_seq(j)
        flush(k, rec)


def phase3(G):
    k, rec, nc = G["k"], G["rec"], G["nc"]
    buf = k.buf
    L = k.L
    NG = L // 512
    pA, pT, pS, pP, pO, pN, pX = G["pA"], G["pT"], G["pS"], G["pP"], G["pO"], G["pN"], G["pX"]
    vec, m64, onesb, idb = G["vec"], G["m64"], G["onesb"], G["idb"]
    with ExitStack() as ph:
        sb = lambda n, s, d: ph.enter_context(nc.sbuf_tensor(n, s, d))
        W1 = load_weight_bf16(G, ph, G["w_out_e"], 1024, None, "W1")
        W2 = load_weight_bf16(G, ph, G["w_in_o"], 3088, 8, "W2")
        W3 = load_weight_bf16(G, ph, G["w_out_o"], 1024, None, "W3")
        wgf = sb("wgf", [16, 512], F32)
        wg = sb("wg", [16, 512], BF16)
        rec.dma("sp", lambda e: e.dma_start(out=wgf[:], in_=G["w_gate"]), W=[buf("wgf")])
        rec.op("dve", lambda e: e.tensor_copy(out=wg[:], in_=wgf[:]), R=[buf("wgf")], W=[buf("wg")])
        fnb = sb("fnb", [128, D], F32)
        rec.dma("sp", lambda e: e.dma_start(out=fnb[:], in_=G["fnorm"].partition_broadcast(128)), W=[buf("fnb")])
        G["junk"] = sb("junk3", [128, 1024], BF16)
        G["ss"] = sb("ss3", [128, 8], F32)
        G["epsb"] = sb("epsb3", [128, 1], F32)
        rec.op("pool", lambda e: e.memset(G["epsb"][:], EPS), W=[buf("epsb")])
        ss2 = sb("ss2", [128, 4], F32)
        oc = sb("oc", [128, 8, 512], BF16)
        xt = [sb("x3_%d" % i, [128, D], F32) for i in range(2)]
        x1 = sb("x1", [128, 4, D], F32)
        hb = [sb("hb3_%d" % i, [128, D], BF16) for i in range(1)]
        hT = sb("hT3", [128, KC, 512], BF16)
        tmp = {n: sb("u_" + n, [128, 512], F32) for n in ("lf", "b", "d", "eq", "ek")}
        qt = sb("qt3", [128, 4, 512], BF16)
        kt = sb("kt3", [128, 4, 512], BF16)
        sm = sb("sm3", [128, 4, 3, 8], F32)
        vtok = sb("vtok3", [128, 4, 1024], BF16)
        ktok = sb("ktok3", [128, 4, 4, 128], BF16)
        AT = [sb("AT3_%d" % i, [128, 4, 64], BF16) for i in range(2)]
        S = sb("S3", [128, 4, 256], F32)
        Sall = sb("Sall3", [128, 8, 256], BF16)
        kt2 = sb("kt23", [128, 512], BF16)
        oT = sb("oT3", [128, 8, 512], F32)
        sq = G["junk"][:].rearrange("p (e n) -> p e n", e=2)
        lnt = sb("lnt3", [128, 512], F32)
        sg = sb("sg3", [128, 8, 512], BF16)
        tg = sb("tg3", [128, 512], F32)
        oc2 = oc
        rT = sb("rT", [16, 512], BF16)
        rec.op("pool", lambda e: e.memset(S[:], 0.0), W=[buf("G%d" % h) for h in range(4)])
        pa_i = [0]

        def nextp():
            i = pa_i[0] % 2
            pa_i[0] += 1
            return pA[i], buf("pA%d" % i)

        def fm(Wt, wname, col, M, N, src, srcb):
            p, pb = nextp()
            for kc in range(KC):
                rec.op("pe", lambda e, kc=kc: e.matmul(out=p[0:M, 0:N], lhsT=Wt[:, kc, col:col + M], rhs=src[:, kc, 0:N], start=(kc == 0), stop=(kc == KC - 1)),
                       R=[buf(wname), srcb], W=[pb])
            return p, pb

        def tm(Wt, wname, col, ncols, t, src, srcb):
            p, pb = nextp()
            for kc in range(KC):
                rec.op("pe", lambda e, kc=kc: e.matmul(out=p[:, 0:ncols], lhsT=src[:, kc, t * 128:(t + 1) * 128], rhs=Wt[:, kc, col:col + ncols], start=(kc == 0), stop=(kc == KC - 1)),
                       R=[buf(wname), srcb], W=[pb])
            return p, pb

        xi = [0]

        def do_group(t0, N, samp):
            ntile = N // 128
            C = 64
            nch = N // C
            mid = 1 if samp else 31
            last = 3 if samp else C - 1
            rec.dma("sp", lambda e: e.dma_start(out=oc[:, 0:4, 0:N], in_=G["oaT"][:, :, t0:t0 + N].rearrange("h p n -> p h n")), R=[buf("oaT_d")], W=[buf("oc")])
            rec.dma("sp", lambda e: e.dma_start(out=oc[:, 4:8, 0:N], in_=G["obT"][:, :, t0:t0 + N].rearrange("h p n -> p h n")), R=[buf("obT_d")], W=[buf("oc")])
            for t in range(ntile):
                x_ = xt[xi[0] % 2]
                xb_ = buf("x3_%d" % (xi[0] % 2))
                h_ = hb[0]
                hb_ = buf("hb3_0")
                xi[0] += 1
                src = G["xs"][t * 128:(t + 1) * 128, :] if samp else G["xp"][t0 + t * 128:t0 + (t + 1) * 128, :]
                rec.dma("sp", lambda e, x_=x_, src=src: e.dma_start(out=x_[:], in_=src), W=[xb_])
                for hf in range(2):
                    p, pb = tm(W1, "W1", hf * 512, 512, t, oc, buf("oc"))
                    rec.op("dve", lambda e, p=p, t=t, hf=hf, x_=x_: e.tensor_tensor(out=x1[:, t, hf * 512:(hf + 1) * 512], in0=p[:, 0:512], in1=x_[:, hf * 512:(hf + 1) * 512], op=ALU.add),
                           R=[pb, xb_], W=[buf("x1_%d" % t)])
                norm_and_transpose(G, x1[:, t, :], buf("x1_%d" % t), t, h_, hb_, hT, buf("hT3"), t * 128)
            for t in range(ntile):
                for hf in range(2):
                    p, pb = tm(W2, "W2", 1024 + hf * 512, 512, t, hT, buf("hT3"))
                    rec.op("act", lambda e, p=p, t=t, hf=hf: e.activation(out=vtok[:, t, hf * 512:(hf + 1) * 512], in_=p[:, 0:512], func=AF.Copy), R=[pb], W=[buf("vtok3")])
            p, pb = fm(W2, "W2", 3072, 16, N, hT, buf("hT3"))
            rec.op("act", lambda e, p=p: e.activation(out=rT[:, 0:N], in_=p[0:16, 0:N], func=AF.Copy), R=[pb], W=[buf("rT")])
            for c8 in range(8):
                p, pb = fm(W2, "W2", 2048 + 128 * c8, 128, N, hT, buf("hT3"))
                rec.op("act", lambda e, p=p, c8=c8: e.activation(out=sg[:, c8, 0:N], in_=p[:, 0:N], func=AF.Silu), R=[pb], W=[buf("sg3")])
            for h in range(4):
                T_ = tmp
                p, pb = nextp()
                rec.op("pe", lambda e, p=p, h=h: e.matmul(out=p[:, 0:N], lhsT=wg[0:16, h * 128:(h + 1) * 128], rhs=rT[0:16, 0:N], start=True, stop=True), R=[buf("wg"), buf("rT")], W=[pb])
                rec.op("act", lambda e, p=p, h=h: e.activation(out=T_["lf"][:, 0:N], in_=p[:, 0:N], func=AF.Sigmoid, bias=vec[:, 36 + h:37 + h]), R=[pb, buf("vec")], W=[buf("u_lf")])
                rec.op("act", lambda e: e.activation(out=T_["lf"][:, 0:N], in_=T_["lf"][:, 0:N], func=AF.Ln), R=[buf("u_lf")], W=[buf("u_lf")])
                rec.op("dve", lambda e: e.tensor_scalar(out=T_["lf"][:, 0:N], in0=T_["lf"][:, 0:N], scalar1=1.0 / 16.0, scalar2=None, op0=ALU.mult), R=[buf("u_lf")], W=[buf("u_lf")])
                rec.op("dve", lambda e: e.tensor_tensor_scan(out=T_["b"][:, 0:N], data0=G["rst64"][:, 0:N], data1=T_["lf"][:, 0:N], initial=0.0, op0=ALU.mult, op1=ALU.add),
                       R=[buf("u_lf"), buf("cst")], W=[buf("u_b")])
                b3 = T_["b"][:, 0:N].rearrange("p (c k) -> p c k", k=C)
                d3 = T_["d"][:, 0:N].rearrange("p (c k) -> p c k", k=C)
                rec.op("dve", lambda e, b3=b3, d3=d3: e.tensor_tensor(out=d3, in0=b3, in1=b3[:, :, mid:mid + 1].to_broadcast([128, nch, C]), op=ALU.subtract), R=[buf("u_b")], W=[buf("u_d")])
                rec.op("act", lambda e: e.activation(out=T_["eq"][:, 0:N], in_=T_["d"][:, 0:N], func=AF.Exp), R=[buf("u_d")], W=[buf("u_eq")])
                rec.op("act", lambda e: e.activation(out=T_["ek"][:, 0:N], in_=T_["d"][:, 0:N], func=AF.Exp, scale=-1.0), R=[buf("u_d")], W=[buf("u_ek")])
                rec.op("act", lambda e, h=h, b3=b3: e.activation(out=sm[:, h, 0, 0:nch], in_=b3[:, :, mid], func=AF.Exp), R=[buf("u_b")], W=[buf("sm3")])
                rec.op("act", lambda e, h=h, b3=b3: e.activation(out=sm[:, h, 2, 0:nch], in_=b3[:, :, last], func=AF.Exp), R=[buf("u_b")], W=[buf("sm3")])
                rec.op("act", lambda e, h=h, d3=d3: e.activation(out=sm[:, h, 1, 0:nch], in_=d3[:, :, last], func=AF.Exp), R=[buf("u_d")], W=[buf("sm3")])
                p, pb = fm(W2, "W2", 128 * h, 128, N, hT, buf("hT3"))
                rec.op("dve", lambda e, p=p, h=h: e.scalar_tensor_tensor(out=qt[:, h, 0:N], in0=p[:, 0:N], scalar=SCALE, in1=T_["eq"][:, 0:N], op0=ALU.mult, op1=ALU.mult),
                       R=[pb, buf("u_eq")], W=[buf("qt3")])
                p, pb = fm(W2, "W2", 512 + 128 * h, 128, N, hT, buf("hT3"))
                rec.op("dve", lambda e, p=p, h=h: e.tensor_tensor(out=kt[:, h, 0:N], in0=p[:, 0:N], in1=T_["ek"][:, 0:N], op=ALU.mult), R=[pb, buf("u_ek")], W=[buf("kt3")])
                rec.op("dve", lambda e, h=h: e.tensor_tensor(out=kt2[:, 0:N].rearrange("p (c k) -> p c k", k=C), in0=kt[:, h, 0:N].rearrange("p (c k) -> p c k", k=C),
                                                           in1=sm[:, h, 1, 0:nch].unsqueeze(2).to_broadcast([128, nch, C]), op=ALU.mult), R=[buf("kt3"), buf("sm3")], W=[buf("kt23")])
                for t in range(ntile):
                    rec.op("pe", lambda e, h=h, t=t: e.transpose(out=pT[:, h * 128:(h + 1) * 128], in_=kt2[:, t * 128:(t + 1) * 128], identity=idb[:]), R=[buf("kt23"), buf("idb")], W=[buf("pT")])
                    rec.op("act", lambda e, h=h, t=t: e.activation(out=ktok[:, t, h, :], in_=pT[:, h * 128:(h + 1) * 128], func=AF.Copy), R=[buf("pT")], W=[buf("ktok3")])

            def cinfo(c):
                r0 = (c * C) % 128
                return slice(c * C, (c + 1) * C), (c * C) // 128, slice(r0, r0 + C)

            for h in range(4):
                Sb = buf("G%d" % h)
                at = AT[h % 2]
                atb = buf("AT3_%d" % (h % 2))
                sal = Sall
                salb = buf("Sall3")
                for c in range(nch):
                    cols, t, rows = cinfo(c)
                    rec.op("pe", lambda e, h=h, cols=cols, rows=rows, t=t: e.matmul(out=pS[rows, t * 64:t * 64 + C], lhsT=kt[:, h, cols], rhs=qt[:, h, cols], start=True, stop=True),
                           R=[buf("kt3"), buf("qt3")], W=[buf("pS")])
                rec.op("dve", lambda e, at=at: e.tensor_tensor(out=at[:, 0:ntile, :], in0=pS[:, 0:ntile * 64].rearrange("p (a b) -> p a b", b=64),
                                                              in1=m64[:, :].unsqueeze(1).to_broadcast([128, ntile, 64]), op=ALU.mult),
                       R=[buf("pS"), buf("m64")], W=[atb])
                for c in range(nch):
                    cols, t, rows = cinfo(c)
                    pb_, pbb_ = (pP, buf("pP")) if c % 2 == 0 else (pX, buf("pX"))
                    rec.op("pe", lambda e, h=h, rows=rows, t=t, pb_=pb_: e.matmul(out=pb_[:, 0:256], lhsT=ktok[rows, t, h, :], rhs=vtok[rows, t, h * 256:(h + 1) * 256], start=True, stop=True),
                           R=[buf("ktok3"), buf("vtok3")], W=[pbb_])
                    if samp:
                        rec.dma("sp", lambda e, h=h, c=c: e.dma_start(out=S[:, h, :], in_=G["st_g"][c, h]), W=[Sb])
                    rec.op("dve", lambda e, h=h, c=c: e.tensor_scalar(out=sal[:, c, :], in0=S[:, h, :], scalar1=sm[:, h, 0, c:c + 1], scalar2=None, op0=ALU.mult),
                           R=[Sb, buf("sm3")], W=[salb])
                    rec.op("dve", lambda e, h=h, c=c, pb_=pb_: e.scalar_tensor_tensor(out=S[:, h, :], in0=S[:, h, :], scalar=sm[:, h, 2, c:c + 1], in1=pb_[:, 0:256], op0=ALU.mult, op1=ALU.add),
                           R=[pbb_, Sb, buf("sm3")], W=[Sb])
                    if samp:
                        rec.dma("pool", lambda e, h=h, c=c: e.dma_start(out=G["gl_s"][c, h], in_=S[:, h, :]), R=[Sb], final=True)
                for e2 in range(2):
                    vs = slice(h * 256 + e2 * 128, h * 256 + (e2 + 1) * 128)
                    for c in range(nch):
                        cols, t, rows = cinfo(c)
                        po_, pob_ = (pO, buf("pO")) if c % 2 == 0 else (pA[0], buf("pA0"))
                        rec.op("pe", lambda e, vs=vs, rows=rows, t=t, at=at, c=c, po_=po_: e.matmul(out=po_[:, (c // 2) * 64:(c // 2) * 64 + C], lhsT=vtok[rows, t, vs], rhs=at[rows, t, :], start=True, stop=True),
                               R=[buf("vtok3"), atb], W=[pob_])
                    for c in range(nch):
                        cols, t, rows = cinfo(c)
                        rec.op("pe", lambda e, h=h, cols=cols, c=c, e2=e2: e.matmul(out=pN[:, c * 64:c * 64 + C], lhsT=sal[:, c, e2 * 128:(e2 + 1) * 128], rhs=qt[:, h, cols], start=True, stop=True),
                               R=[salb, buf("qt3")], W=[buf("pN")])
                    o4 = oT[:, 2 * h + e2, 0:N].rearrange("p (a two k) -> p a two k", two=2, k=64)
                    rec.op("act", lambda e, o4=o4: e.activation(out=o4[:, :, 0, :], in_=pO[:, 0:N // 2].rearrange("p (a k) -> p a k", k=64), func=AF.Copy), R=[buf("pO")], W=[buf("oT3")])
                    rec.op("act", lambda e, o4=o4: e.activation(out=o4[:, :, 1, :], in_=pA[0][:, 0:N // 2].rearrange("p (a k) -> p a k", k=64), func=AF.Copy), R=[buf("pA0")], W=[buf("oT3")])
                    rec.op("dve", lambda e, h=h, e2=e2: e.tensor_tensor(out=oT[:, 2 * h + e2, 0:N], in0=oT[:, 2 * h + e2, 0:N], in1=pN[:, 0:N], op=ALU.add), R=[buf("pN"), buf("oT3")], W=[buf("oT3")])
            if (not samp) and t0 + N == L:
                for h in range(4):
                    rec.dma("pool", lambda e, h=h: e.dma_start(out=G["gl_p"][h], in_=S[:, h, :]), R=[buf("G%d" % h)], final=True)
            for h in range(4):
                for e2 in range(2):
                    rec.op("act", lambda e, h=h, e2=e2: e.activation(out=sq[:, e2, 0:N], in_=oT[:, 2 * h + e2, 0:N], func=AF.Square), R=[buf("oT3")], W=[buf("junk")])
                for e2 in range(2):
                    rec.op("pe", lambda e, e2=e2: e.matmul(out=pN[:, 0:N], lhsT=onesb[:], rhs=sq[:, e2, 0:N], start=(e2 == 0), stop=(e2 == 1)), R=[buf("onesb"), buf("junk")], W=[buf("pN")])
                rec.op("act", lambda e: e.activation(out=lnt[:, 0:N], in_=pN[:, 0:N], func=AF.Ln, scale=1.0 / 256, bias=G["epsb"][:, 0:1]), R=[buf("pN"), buf("epsb")], W=[buf("lnt3")])
                rec.op("act", lambda e: e.activation(out=lnt[:, 0:N], in_=lnt[:, 0:N], func=AF.Exp, scale=-0.5), R=[buf("lnt3")], W=[buf("lnt3")])
                for e2 in range(2):
                    c8 = 2 * h + e2
                    rec.op("dve", lambda e, c8=c8: e.scalar_tensor_tensor(out=tg[:, 0:N], in0=lnt[:, 0:N], scalar=vec[:, 40 + c8:41 + c8], in1=sg[:, c8, 0:N], op0=ALU.mult, op1=ALU.mult),
                           R=[buf("lnt3"), buf("vec"), buf("sg3")], W=[buf("tg3")])
                    rec.op("dve", lambda e, c8=c8: e.tensor_tensor(out=oc2[:, c8, 0:N], in0=oT[:, c8, 0:N], in1=tg[:, 0:N], op=ALU.mult), R=[buf("oT3"), buf("tg3")], W=[buf("oc")])
            for t in range(ntile):
                xb1 = buf("x1_%d" % t)
                for hf in range(2):
                    p, pb = tm(W3, "W3", hf * 512, 512, t, oc2, buf("oc"))
                    rec.op("dve", lambda e, p=p, t=t, hf=hf: e.tensor_tensor(out=x1[:, t, hf * 512:(hf + 1) * 512], in0=p[:, 0:512], in1=x1[:, t, hf * 512:(hf + 1) * 512], op=ALU.add), R=[pb, xb1], W=[xb1])
                rec.op("act", lambda e, t=t: e.activation(out=G["junk"][:], in_=x1[:, t, :], func=AF.Square, accum_out=ss2[:, t:t + 1]), R=[xb1], W=[buf("junk"), buf("ss2")])
                rec.op("act", lambda e, t=t: e.activation(out=ss2[:, t:t + 1], in_=ss2[:, t:t + 1], func=AF.Ln, scale=1.0 / D, bias=G["epsb"][:, 0:1]), R=[buf("ss2"), buf("epsb")], W=[buf("ss2")])
                rec.op("act", lambda e, t=t: e.activation(out=ss2[:, t:t + 1], in_=ss2[:, t:t + 1], func=AF.Exp, scale=-0.5), R=[buf("ss2")], W=[buf("ss2")])
                rec.op("dve", lambda e, t=t: e.scalar_tensor_tensor(out=x1[:, t, :], in0=x1[:, t, :], scalar=ss2[:, t:t + 1], in1=fnb[:], op0=ALU.mult, op1=ALU.mult), R=[xb1, buf("ss2"), buf("fnb")], W=[xb1])
                dst = G["y_s"][t * 128:(t + 1) * 128, :] if samp else G["y_p"][t0 + t * 128:t0 + (t + 1) * 128, :]
                rec.dma("pool", lambda e, t=t, dst=dst: e.dma_start(out=dst, in_=x1[:, t, :]), R=[xb1], final=True)

        groups = [(g * 512, 512, False) for g in range(NG)] + [(L, 256, True)]
        for (t0_, N_, samp_) in groups:
            do_group(t0_, N_, samp_)
        flush(k, rec)


def make_consts():
    c = np.zeros((128, 1280), np.float32)
    p = np.arange(128)[:, None]
    j = np.arange(128)[None, :]
    c[:, 0:128] = np.eye(128)
    c[:, 128:256] = (p <= j)
    j64 = np.arange(64)[None, :]
    c[:, 256:320] = ((p % 64) <= j64)
    c[:, 320:448] = 1.0
    c[:, 448:576] = ((p // 64) == (j // 64)) & (p <= j)
    j512 = np.arange(512)[None, :]
    c[:, 576:1088] = (j512 % 64 != 0) * np.ones((128, 1))
    c[:, 1088:1216] = (p > j)
    c[:, 1216] = np.arange(128)
    return c

def pack_vecs(inp):
    v = np.zeros((128, 64), np.float32)
    f = lambda a, n: np.asarray(a, np.float32).reshape(n, 128).T
    v[:, 0:8] = f(inp['norm_even'][0], 8)
    v[:, 8:16] = f(inp['norm_odd'][0], 8)
    v[:, 16:24] = f(inp['final_norm'], 8)
    v[:, 24:28] = f(inp['lb_logits'][0], 4)
    v[:, 28:32] = f(inp['lb_logits'][1], 4)
    v[:, 32:36] = f(inp['hgrn_gain'][0], 4)
    v[:, 36:40] = f(inp['b_gla_gate'][0], 4)
    v[:, 40:48] = f(inp['gla_gain'][0], 8)
    return v

def make_ckv(inp):
    pool = inp['cache_fox_k'].shape[1]
    return np.concatenate([np.asarray(inp['cache_fox_k'][0]).reshape(pool * 128, 512),
                           np.asarray(inp['cache_fox_v'][0]).reshape(pool * 128, 512)], axis=1)


def core_inputs(inp, c, L, NPG, ckv=None):
    b = c // 4
    m = {}
    m['xp'] = np.ascontiguousarray(inp['x_prompt'][b, :L])
    xs = np.zeros((256, 1024), np.float32)
    for j in range(4):
        xs[64 * j:64 * j + 4] = inp['x_sample'][4 * c + j]
    m['xs'] = xs
    m['w_in_even'] = np.ascontiguousarray(inp['w_in_even'][0])
    m['w_out_even'] = np.ascontiguousarray(inp['w_out_even'][0])
    m['w_in_odd'] = np.ascontiguousarray(inp['w_in_odd'][0])
    m['w_out_odd'] = np.ascontiguousarray(inp['w_out_odd'][0])
    m['w_gla_gate'] = np.ascontiguousarray(inp['w_gla_gate'][0])
    m['vecs'] = pack_vecs(inp)
    m['fnorm'] = np.ascontiguousarray(np.asarray(inp['final_norm'], np.float32))
    m['bfox'] = np.ascontiguousarray(np.broadcast_to(np.asarray(inp['b_fox_f'][0], np.float32)[None, :], (128, 4)))
    m['consts'] = make_consts()
    m['st_hgrn'] = np.ascontiguousarray(inp['state_hgrn'][0, 4 * c:4 * c + 4])
    m['st_gla'] = np.ascontiguousarray(inp['state_gla'][0, 4 * c:4 * c + 4])
    m['ptab'] = np.ascontiguousarray(inp['page_table'][4 * c:4 * c + 4, :NPG]).astype(np.int32)
    pool = inp['cache_fox_k'].shape[1]
    m['cache_kv'] = ckv if ckv is not None else make_ckv(inp)
    m['cache_lf'] = np.asarray(inp['cache_fox_logf'][0]).reshape(pool, 512)
    return m


def assemble(results, L):
    f = lambda a: np.asarray(a, np.float32)
    idx = np.concatenate([np.arange(64 * j, 64 * j + 4) for j in range(4)])
    pc = [0, 4]
    y_p = np.stack([f(results[c]['y_p']) for c in pc])
    y_s = np.concatenate([f(results[c]['y_s'])[idx] for c in range(8)]).reshape(32, 4, 1024)
    npg = L // 128
    fk_p = np.stack([f(results[c]['fk_p']) for c in pc]).reshape(1, 2, npg, 128, 4, 128)
    fv_p = np.stack([f(results[c]['fv_p']) for c in pc]).reshape(1, 2, npg, 128, 4, 128)
    flf_p = np.stack([f(results[c]['flf_p']) for c in pc]).reshape(1, 2, npg, 128, 4)
    hg_p = np.stack([f(results[c]['hg_p']) for c in pc])[None]
    gl_p = np.stack([f(results[c]['gl_p']) for c in pc])[None]
    fk_s = np.concatenate([f(results[c]['fk_s'])[idx] for c in range(8)]).reshape(1, 32, 4, 4, 128)
    fv_s = np.concatenate([f(results[c]['fv_s'])[idx] for c in range(8)]).reshape(1, 32, 4, 4, 128)
    flf_s = np.concatenate([f(results[c]['flf_s'])[idx] for c in range(8)]).reshape(1, 32, 4, 4)
    hg_s = np.concatenate([f(results[c]['hg_s']) for c in range(8)])[None]
    gl_s = np.concatenate([f(results[c]['gl_s']) for c in range(8)])[None]
    return (y_p, y_s, fk_p, fv_p, flf_p, hg_p, gl_p, fk_s, fv_s, flf_s, hg_s, gl_s)


def kernel(**inputs):
    inp = {k_: np.asarray(v) for k_, v in inputs.items()}
    L = inp['x_prompt'].shape[1]
    NPG = inp['page_table'].shape[1]
    POOL = inp['cache_fox_k'].shape[1]
    nc = build(L, NPG, POOL, dbg=False, phases=(1, 2, 3))
    ckv = make_ckv(inp)
    maps = [core_inputs(inp, c, L, NPG, ckv) for c in range(8)]
    res = run_bass_kernel_spmd(nc, maps, core_ids=list(range(8)))
    return assemble(res.results, L)
```
